# Optimizing a Trainium2 kernel written in Bass

```python
import math
import jax, jax.numpy as jnp
from jax import lax
import numpy as np


D_MODEL = 1024
BATCH = 8
SEQ = 2048
DEPTH = 2
DEC_BATCH = 32
DEC_SEQ = 1
PAST_LEN = 8192
PAGE_SIZE = 128

A_HEADS = 4
A_HD = 64
A_VD = 2 * A_HD
A_QK_W = A_HEADS * 2 * A_HD
A_V_W = A_HEADS * A_VD
B_W = D_MODEL // 4
B_K = 3
C_W = D_MODEL // 4
C_K = 31
D_W = D_MODEL // 4
D_GROUPS = 4
CHUNK = 128
N_BRANCH = 4
IN_SIZES = (A_QK_W, A_QK_W, A_V_W, B_W, B_W, B_W, 2 * C_W, 2 * D_W, N_BRANCH * D_MODEL)
IN_COLS = sum(IN_SIZES)
N_MEM = 256
X_HEADS = 4
X_HD = 128
X_W = X_HEADS * X_HD
D_FF = 4 * D_MODEL
Q_BLOCK = 128
EPS = 1e-6
NEG_INF = -1e30

kernel_name = 'hybrid_gated_diffattn_conv_gmlp_decoder_step'


def rms_norm(x, g):
    xf = x.astype(jnp.float32)
    y = xf * lax.rsqrt(jnp.mean(xf * xf, axis=-1, keepdims=True) + EPS)
    return (y * g.astype(jnp.float32)).astype(x.dtype)


def layer_norm(x, g, b):
    xf = x.astype(jnp.float32)
    xc = xf - jnp.mean(xf, axis=-1, keepdims=True)
    var = jnp.mean(xc * xc, axis=-1, keepdims=True)
    return (xc * lax.rsqrt(var + EPS) * g.astype(jnp.float32) + b.astype(jnp.float32)).astype(x.dtype)


def alibi_slopes(n_heads):
    return jnp.asarray([2.0 ** (-8.0 * (i + 1) / n_heads) for i in range(n_heads)], dtype=jnp.float32)


def causal_dwconv(z, prev, w, b):
    width = w.shape[0]
    zp = jnp.concatenate([prev.astype(z.dtype), z], axis=1)
    y = lax.conv_general_dilated(zp, w.astype(z.dtype)[:, None, :], window_strides=(1,), padding='VALID',
                                 dimension_numbers=('NWC', 'WIO', 'NWC'), feature_group_count=z.shape[-1])
    return y + b.astype(z.dtype), zp[:, zp.shape[1] - (width - 1):]


def diff_attention(q1, q2, k1, k2, v, q_pos, k_pos, slopes, lam):
    bsz, lq, nh, d = q1.shape
    blk = min(Q_BLOCK, lq)
    nblk = -(-lq // blk)
    pad = nblk * blk - lq
    scale = d ** -0.5

    def prep(q):
        q = jnp.pad(q.astype(jnp.float32), ((0, 0), (0, pad), (0, 0), (0, 0))) * scale
        return q.reshape(bsz, nblk, blk, nh, d).transpose(1, 0, 2, 3, 4)

    qs1, qs2 = prep(q1), prep(q2)
    qp = jnp.pad(q_pos, (0, pad), mode='edge').reshape(nblk, blk)
    k1f, k2f, vf = k1.astype(jnp.float32), k2.astype(jnp.float32), v.astype(jnp.float32)

    def block(args):
        qb1, qb2, pb = args
        dist = (pb[:, None] - k_pos[None, :]).astype(jnp.float32)
        bias = jnp.where(dist[None] >= 0, -slopes[:, None, None] * dist[None], NEG_INF)
        p1 = jax.nn.softmax(jnp.einsum('bqhd,bkhd->bhqk', qb1, k1f) + bias, axis=-1)
        p2 = jax.nn.softmax(jnp.einsum('bqhd,bkhd->bhqk', qb2, k2f) + bias, axis=-1)
        return jnp.einsum('bhqk,bkhe->bqhe', p1 - lam * p2, vf)

    out = lax.map(block, (qs1, qs2, qp))
    return out.transpose(1, 0, 2, 3, 4).reshape(bsz, nblk * blk, nh, -1)[:, :lq]


def chunk_spatial_gate(v, ws, bs):
    bsz, t, c = v.shape
    nch = -(-t // CHUNK)
    pad = nch * CHUNK - t
    vp = jnp.pad(v, ((0, 0), (0, pad), (0, 0))).reshape(bsz, nch, CHUNK, D_GROUPS, c // D_GROUPS)
    w = ws * jnp.tril(jnp.ones((CHUNK, CHUNK), ws.dtype))
    s = jnp.einsum('gts,bnsgc->bntgc', w, vp) + bs.T[:, :, None]
    return s.reshape(bsz, nch * CHUNK, c)[:, :t]


def memory_kv(mem, mem_norm_g, w_xk, w_xv, x_knorm_g):
    bsz, n, _ = mem.shape
    mem_n = rms_norm(mem, mem_norm_g)
    mk = rms_norm((mem_n @ w_xk).reshape(bsz, n, X_HEADS, X_HD), x_knorm_g)
    mv = (mem_n @ w_xv).reshape(bsz, n, X_HEADS, X_HD)
    return mk, mv


def decoder_layer(x, lw, layer_idx, past_k, past_v, prev_b, prev_c, mem_k, mem_v):
    bsz, t, _ = x.shape
    p_len = 0 if past_k is None else past_k.shape[1]
    q_pos = jnp.arange(p_len, p_len + t, dtype=jnp.int32)
    k_pos = jnp.arange(p_len + t, dtype=jnp.int32)
    lam_init = 0.8 - 0.6 * math.exp(-0.3 * layer_idx)

    h = rms_norm(x, lw['norm_mix_g'])
    z = h @ lw['w_in']
    offs = np.cumsum(IN_SIZES)[:-1].tolist()
    qa, ka, va, bx, bb, bc, cz, dz, gz = jnp.split(z, offs, axis=-1)

    qa = rms_norm(qa.reshape(bsz, t, A_HEADS, 2, A_HD), lw['a_qnorm_g'])
    ka = rms_norm(ka.reshape(bsz, t, A_HEADS, 2, A_HD), lw['a_knorm_g'])
    k_rows = ka.reshape(bsz, t, A_HEADS, 2 * A_HD)
    v_rows = va.reshape(bsz, t, A_HEADS, A_VD)
    if past_k is None:
        k_all, v_all = k_rows, v_rows
    else:
        k_all = jnp.concatenate([past_k.astype(x.dtype), k_rows], axis=1)
        v_all = jnp.concatenate([past_v.astype(x.dtype), v_rows], axis=1)
    k_all = k_all.reshape(bsz, p_len + t, A_HEADS, 2, A_HD)
    lv = lw['a_lam'].astype(jnp.float32)
    lam = jnp.exp(jnp.sum(lv[0] * lv[1])) - jnp.exp(jnp.sum(lv[2] * lv[3])) + lam_init
    o = diff_attention(qa[..., 0, :], qa[..., 1, :], k_all[..., 0, :], k_all[..., 1, :], v_all,
                       q_pos, k_pos, alibi_slopes(A_HEADS), lam)
    o = rms_norm(o, lw['a_subln_g']) * (1.0 - lam_init)
    ya = o.reshape(bsz, t, A_V_W).astype(x.dtype) @ lw['w_a_out']

    conv_b, new_prev_b = causal_dwconv(bc * bx, prev_b, lw['b_conv_w'], lw['b_conv_b'])
    yb = (bb * conv_b) @ lw['w_b_out']

    ca, cg = jnp.split(cz, 2, axis=-1)
    conv_c, new_prev_c = causal_dwconv(ca * jax.nn.sigmoid(cg), prev_c, lw['c_conv_w'], lw['c_conv_b'])
    yc = jax.nn.silu(layer_norm(conv_c, lw['c_ln_g'], lw['c_ln_b'])) @ lw['w_c_out']

    du, dv = jnp.split(jax.nn.gelu(dz), 2, axis=-1)
    dvn = layer_norm(dv, lw['d_ln_g'], lw['d_ln_b'])
    yd = (du * chunk_spatial_gate(dvn, lw['d_ws'], lw['d_bs'])) @ lw['w_d_out']

    g = jax.nn.sigmoid(gz.reshape(bsz, t, N_BRANCH, D_MODEL))
    merged = g[:, :, 0] * ya + g[:, :, 1] * yb + g[:, :, 2] * yc + g[:, :, 3] * yd
    x = x + merged @ lw['w_o']

    h2 = rms_norm(x, lw['norm_x_g'])
    q = rms_norm((h2 @ lw['w_xq']).reshape(bsz, t, X_HEADS, X_HD), lw['x_qnorm_g'])
    s = jnp.einsum('bqhd,bkhd->bhqk', q.astype(jnp.float32), mem_k.astype(jnp.float32)) * (X_HD ** -0.5)
    pm = jax.nn.softmax(s, axis=-1)
    om = jnp.einsum('bhqk,bkhd->bqhd', pm, mem_v.astype(jnp.float32)).astype(x.dtype)
    x = x + om.reshape(bsz, t, X_W) @ lw['w_xo']

    h3 = rms_norm(x, lw['norm_ffn_g'])
    x = x + jnp.square(jax.nn.relu(h3 @ lw['w_up'])) @ lw['w_down']
    return x, (k_rows, v_rows, new_prev_b, new_prev_c, dvn)


def setup_inputs(seed: int = 0) -> dict:
    key = jax.random.key(seed)
    keys = iter(jax.random.split(key, 64))

    def nrm(shape, scale):
        return jax.random.normal(next(keys), shape, jnp.float32) * scale

    def gain(shape):
        return 1.0 + nrm(shape, 0.05)

    n_pages = PAST_LEN // PAGE_SIZE
    n_used = DEC_BATCH * n_pages
    n_pool = n_used + max(1, n_used // 4)
    page_table = jax.random.permutation(next(keys), n_pool)[:n_used].reshape(DEC_BATCH, n_pages).astype(jnp.int32)
    L = DEPTH
    return {
        'x_prompt': nrm((BATCH, SEQ, D_MODEL), 1.0),
        'x_sample': nrm((DEC_BATCH, DEC_SEQ, D_MODEL), 1.0),
        'cache_k_a': nrm((L, n_pool, PAGE_SIZE, A_HEADS, 2 * A_HD), 1.0),
        'cache_v_a': nrm((L, n_pool, PAGE_SIZE, A_HEADS, A_VD), 1.0),
        'state_conv_b': nrm((L, DEC_BATCH, B_K - 1, B_W), 0.5),
        'state_conv_c': nrm((L, DEC_BATCH, C_K - 1, C_W), 0.5),
        'cache_mem_k': nrm((L, DEC_BATCH, N_MEM, X_HEADS, X_HD), 1.0),
        'cache_mem_v': nrm((L, DEC_BATCH, N_MEM, X_HEADS, X_HD), 1.0),
        'page_table': page_table,
        'mem_prompt': nrm((BATCH, N_MEM, D_MODEL), 1.0),
        'norm_mix_g': gain((L, D_MODEL)),
        'w_in': nrm((L, D_MODEL, IN_COLS), D_MODEL ** -0.5),
        'a_qnorm_g': gain((L, A_HD)),
        'a_knorm_g': gain((L, A_HD)),
        'a_lam': nrm((L, 4, A_HD), 0.1),
        'a_subln_g': gain((L, A_VD)),
        'w_a_out': nrm((L, A_V_W, D_MODEL), A_V_W ** -0.5),
        'b_conv_w': nrm((L, B_K, B_W), B_K ** -0.5),
        'b_conv_b': nrm((L, B_W), 0.02),
        'w_b_out': nrm((L, B_W, D_MODEL), B_W ** -0.5),
        'c_conv_w': nrm((L, C_K, C_W), C_K ** -0.5),
        'c_conv_b': nrm((L, C_W), 0.02),
        'c_ln_g': gain((L, C_W)),
        'c_ln_b': nrm((L, C_W), 0.02),
        'w_c_out': nrm((L, C_W, D_MODEL), C_W ** -0.5),
        'd_ln_g': gain((L, D_W)),
        'd_ln_b': nrm((L, D_W), 0.02),
        'd_ws': nrm((L, D_GROUPS, CHUNK, CHUNK), CHUNK ** -0.5),
        'd_bs': gain((L, D_GROUPS, CHUNK)),
        'w_d_out': nrm((L, D_W, D_MODEL), D_W ** -0.5),
        'w_o': nrm((L, D_MODEL, D_MODEL), 0.5 * D_MODEL ** -0.5),
        'norm_x_g': gain((L, D_MODEL)),
        'mem_norm_g': gain((L, D_MODEL)),
        'w_xq': nrm((L, D_MODEL, X_W), D_MODEL ** -0.5),
        'w_xk': nrm((L, D_MODEL, X_W), D_MODEL ** -0.5),
        'w_xv': nrm((L, D_MODEL, X_W), D_MODEL ** -0.5),
        'x_qnorm_g': gain((L, X_HD)),
        'x_knorm_g': gain((L, X_HD)),
        'w_xo': nrm((L, X_W, D_MODEL), 0.5 * X_W ** -0.5),
        'norm_ffn_g': gain((L, D_MODEL)),
        'w_up': nrm((L, D_MODEL, D_FF), D_MODEL ** -0.5),
        'w_down': nrm((L, D_FF, D_MODEL), 0.5 * D_FF ** -0.5),
    }


def reference(x_prompt, x_sample, cache_k_a, cache_v_a, state_conv_b, state_conv_c, cache_mem_k, cache_mem_v,
              page_table, mem_prompt, norm_mix_g, w_in, a_qnorm_g, a_knorm_g, a_lam, a_subln_g, w_a_out,
              b_conv_w, b_conv_b, w_b_out, c_conv_w, c_conv_b, c_ln_g, c_ln_b, w_c_out, d_ln_g, d_ln_b,
              d_ws, d_bs, w_d_out, w_o, norm_x_g, mem_norm_g, w_xq, w_xk, w_xv, x_qnorm_g, x_knorm_g,
              w_xo, norm_ffn_g, w_up, w_down):
    bp = x_prompt.shape[0]
    bs_ = x_sample.shape[0]
    n_pages = page_table.shape[1]
    xp, xs = x_prompt, x_sample
    prev_b_p = jnp.zeros((bp, B_K - 1, B_W), x_prompt.dtype)
    prev_c_p = jnp.zeros((bp, C_K - 1, C_W), x_prompt.dtype)
    kp_l, vp_l, cbp_l, ccp_l, mkp_l, mvp_l = [], [], [], [], [], []
    ks_l, vs_l, cbs_l, ccs_l, dvs_l = [], [], [], [], []
    for l in range(DEPTH):
        lw = dict(norm_mix_g=norm_mix_g[l], w_in=w_in[l], a_qnorm_g=a_qnorm_g[l], a_knorm_g=a_knorm_g[l],
                  a_lam=a_lam[l], a_subln_g=a_subln_g[l], w_a_out=w_a_out[l], b_conv_w=b_conv_w[l],
                  b_conv_b=b_conv_b[l], w_b_out=w_b_out[l], c_conv_w=c_conv_w[l], c_conv_b=c_conv_b[l],
                  c_ln_g=c_ln_g[l], c_ln_b=c_ln_b[l], w_c_out=w_c_out[l], d_ln_g=d_ln_g[l], d_ln_b=d_ln_b[l],
                  d_ws=d_ws[l], d_bs=d_bs[l], w_d_out=w_d_out[l], w_o=w_o[l], norm_x_g=norm_x_g[l],
                  w_xq=w_xq[l], x_qnorm_g=x_qnorm_g[l], w_xo=w_xo[l], norm_ffn_g=norm_ffn_g[l],
                  w_up=w_up[l], w_down=w_down[l])
        mk_p, mv_p = memory_kv(mem_prompt, mem_norm_g[l], w_xk[l], w_xv[l], x_knorm_g[l])
        xp, (k_p, v_p, cb_p, cc_p, _) = decoder_layer(xp, lw, l, None, None, prev_b_p, prev_c_p, mk_p, mv_p)
        past_k = cache_k_a[l][page_table].reshape(bs_, n_pages * PAGE_SIZE, A_HEADS, 2 * A_HD)
        past_v = cache_v_a[l][page_table].reshape(bs_, n_pages * PAGE_SIZE, A_HEADS, A_VD)
        xs, (k_s, v_s, cb_s, cc_s, dv_s) = decoder_layer(xs, lw, l, past_k, past_v, state_conv_b[l],
                                                         state_conv_c[l], cache_mem_k[l], cache_mem_v[l])
        kp_l.append(k_p); vp_l.append(v_p); cbp_l.append(cb_p); ccp_l.append(cc_p)
        mkp_l.append(mk_p); mvp_l.append(mv_p)
        ks_l.append(k_s); vs_l.append(v_s); cbs_l.append(cb_s); ccs_l.append(cc_s); dvs_l.append(dv_s)
    return (xp, xs, jnp.stack(kp_l), jnp.stack(vp_l), jnp.stack(cbp_l), jnp.stack(ccp_l),
            jnp.stack(mkp_l), jnp.stack(mvp_l), jnp.stack(ks_l), jnp.stack(vs_l), jnp.stack(cbs_l),
            jnp.stack(ccs_l), jnp.stack(dvs_l))
```

```python
import contextlib
import math
import numpy as np
import concourse.bass as bass
import concourse.mybir as mybir
from concourse.bass_utils import run_bass_kernel_spmd

F32 = mybir.dt.float32
BF16 = mybir.dt.bfloat16
I32 = mybir.dt.int32
AF = mybir.ActivationFunctionType
ALU = mybir.AluOpType
AX = mybir.AxisListType

D = 1024
NCH = 8
INC = 7424
DFF = 4096
EPS = 1e-6
NCORES = 8
TT = 512
NEPOCH = 12
import os
KDBG = float(os.environ.get('KDBG', '99'))
SW_CLEAR = os.environ.get('SW_CLEAR', '0') == '1'


class _Op:
    __slots__ = ("eng", "fn", "deps", "sig", "kind", "sem", "val", "ep", "clr", "pre")


class Tr:
    def __init__(self, nc, stack, ndma=32):
        self.nc = nc
        self.names = ["pe", "act", "dve", "pool", "sp"]
        self.ops = {k: [] for k in self.names}
        self.lastw = {}
        self.readers = {}
        self.esem = {k: [stack.enter_context(nc.semaphore("s_%s%d" % (k, i))) for i in range(NEPOCH)]
                     for k in self.names if k != "sp"}
        self.esem["sp"] = [stack.enter_context(nc.semaphore("s_sp"))] * NEPOCH
        self.dsem = [stack.enter_context(nc.semaphore("d%d" % i)) for i in range(ndma)]
        self.dval = [0] * ndma
        self.dlast = [None] * ndma
        self.dnext = 0
        self.outs = []
        self.epoch = 0
        self.stack = stack
        self.bufsem = {}

    def new_epoch(self):
        self.epoch += 1
        assert self.epoch < NEPOCH, "too many epochs"

    def _mk(self, eng, fn, kind, r, w, extra=()):
        op = _Op()
        op.eng, op.fn, op.kind, op.sig, op.sem, op.val, op.ep = eng, fn, kind, False, None, 0, self.epoch
        op.clr, op.pre = False, None
        deps = list(extra)
        for k in r:
            lw = self.lastw.get(k)
            if lw is not None:
                deps.append(lw)
        for k in w:
            lw = self.lastw.get(k)
            if lw is not None:
                deps.append(lw)
            deps.extend(self.readers.get(k, {}).values())
        keep = []
        for d in deps:
            if d is op or d in keep:
                continue
            if d.kind == "c" and kind == "c" and d.eng == eng and eng == "pe":
                continue
            d.sig = True
            keep.append(d)
        op.deps = keep
        for k in w:
            self.lastw[k] = op
            self.readers[k] = {}
        for k in r:
            rk = (eng, id(op)) if kind == "d" else eng
            self.readers.setdefault(k, {})[rk] = op
        self.ops[eng].append(op)
        return op

    def op(self, eng, fn, r=(), w=()):
        return self._mk(eng, fn, "c", r, w)

    def dma(self, q, fn, r=(), w=(), out=False):
        k = self.dnext
        self.dnext = (self.dnext + 1) % len(self.dsem)
        extra = [self.dlast[k]] if self.dlast[k] is not None else []
        op = self._mk(q, fn, "d", r, w, extra)
        self.dval[k] += 16
        op.sem, op.val = self.dsem[k], self.dval[k]
        self.dlast[k] = op
        if out:
            self.outs.append(op)
        return op

    def dma_sw(self, fn, r, w, bufkey):
        if bufkey not in self.bufsem:
            i = len(self.bufsem)
            self.bufsem[bufkey] = [[self.stack.enter_context(self.nc.semaphore("b%d_%d" % (i, j))) for j in range(2)], 0]
        sems, n = self.bufsem[bufkey]
        self.bufsem[bufkey][1] = n + 1
        op = self._mk("pool", fn, "d", r, w)
        if SW_CLEAR:
            op.sem, op.val, op.clr = sems[n % 2], 16, True
            op.pre = sems[(n - 1) % 2] if n >= 1 else None
        else:
            op.sem, op.val = sems[0], 16 * (n + 1)
        return op

    def finish(self):
        return self._mk("sp", None, "c", (), (), list(self.outs))

    def emit(self, block):
        for k in self.names:
            c = {}
            for op in self.ops[k]:
                if op.kind == "c" and op.sig:
                    c[op.ep] = c.get(op.ep, 0) + 1
                    op.val = c[op.ep]
        tr = self

        def body(name):
            def run(e):
                known = {}
                for op in tr.ops[name]:
                    waits = {}
                    for d in op.deps:
                        sem = d.sem if d.kind == "d" else tr.esem[d.eng][d.ep]
                        key = id(sem)
                        if d.clr:
                            if known.get(key) is d:
                                continue
                            waits[key] = (sem, 16)
                            known[key] = d
                            continue
                        if known.get(key, 0) >= d.val:
                            continue
                        if key not in waits or waits[key][1] < d.val:
                            waits[key] = (sem, d.val)
                    for key, (sem, val) in waits.items():
                        e.wait_ge(sem, val)
                        if not isinstance(known.get(key), _Op):
                            known[key] = val
                    if op.pre is not None:
                        e.sem_clear(op.pre)
                    if op.fn is None:
                        continue
                    ins = op.fn(e)
                    if op.kind == "d":
                        ins.then_inc(op.sem, 16)
                    elif op.sig:
                        ins.then_inc(tr.esem[name][op.ep], 1)
            return run

        block.tensor(body("pe"))
        block.scalar(body("act"))
        block.vector(body("dve"))
        block.gpsimd(body("pool"))
        block.sync(body("sp"))


class Cx:
    pass


def build(SEQ, NPG, NPOOL, L=2, NS=4, PAGE_TOK=128, with_sample=True):
    NT = SEQ // TT
    NKT = SEQ // 128
    NCHK = PAGE_TOK // 8
    BPG = max(1, min(NS, 128 // NPG))
    NG = (NS + BPG - 1) // BPG
    NCOL = BPG * 8
    if with_sample:
        assert NS % BPG == 0 and (BPG == 1 or NPG % 32 == 0) and BPG * NPG <= 128
    nc = bass.Bass("TRN2", target_bir_lowering=False)
    stack = contextlib.ExitStack()

    def din(name, shape, dt=F32):
        return nc.dram_tensor(name, list(shape), dt, kind="ExternalInput").ap()

    def dout(name, shape, dt=F32):
        return nc.dram_tensor(name, list(shape), dt, kind="ExternalOutput").ap()

    x_prompt = din("x_prompt", [SEQ, D])
    mem_prompt = din("mem_prompt", [256, D])
    c_ident = din("c_ident", [128, 128])
    c_mask = din("c_mask", [128, 128])
    c_tril = din("c_tril", [128, 128])
    c_alibi = din("c_alibi", [128, 64])
    W = {}
    wshapes = dict(
        norm_mix_g=[L, D], w_in=[L, D, INC], a_qnorm_g=[L, 64], a_knorm_g=[L, 64], a_lam=[L, 4, 64],
        a_subln_g=[L, 128], w_a_out=[L, 512, D], b_conv_w=[L, 3, 256], b_conv_b=[L, 256],
        w_b_out=[L, 256, D], c_conv_w=[L, 31, 256], c_conv_b=[L, 256], c_ln_g=[L, 256], c_ln_b=[L, 256],
        w_c_out=[L, 256, D], d_ln_g=[L, 256], d_ln_b=[L, 256], d_ws=[L, 4, 128, 128], d_bs=[L, 4, 128],
        w_d_out=[L, 256, D], w_o=[L, D, D], norm_x_g=[L, D], mem_norm_g=[L, D], w_xq=[L, D, 512],
        w_xk=[L, D, 512], w_xv=[L, D, 512], x_qnorm_g=[L, 128], x_knorm_g=[L, 128], w_xo=[L, 512, D],
        norm_ffn_g=[L, D], w_up=[L, D, DFF], w_down=[L, DFF, D])
    for k, s in wshapes.items():
        W[k] = din(k, s)

    y_prompt = dout("y_prompt", [SEQ, D])
    k_a_prompt = dout("k_a_prompt", [L, SEQ, 512])
    v_a_prompt = dout("v_a_prompt", [L, SEQ, 512])
    conv_b_prompt = dout("conv_b_prompt", [L, 2, 256])
    conv_c_prompt = dout("conv_c_prompt", [L, 30, 256])
    mem_k_prompt = dout("mem_k_prompt", [L, 256, 512])
    mem_v_prompt = dout("mem_v_prompt", [L, 256, 512])
    xs_dram = nc.dram_tensor("xs_scratch", [128, NCH, SEQ], F32).ap()
    if with_sample:
        x_sample = din("x_sample", [NS, D])
        ck = din("ck", [L * NCHK * NPOOL, 4096])
        cv = din("cv", [L * NCHK * NPOOL, 4096])
        state_b = din("state_conv_b", [L, NS, 512])
        state_c = din("state_conv_c", [L, NS, 30 * 256])
        cmk = din("cache_mem_k", [L, NS, 256, 512])
        cmv = din("cache_mem_v", [L, NS, 256, 512])
        page_table = din("page_table", [NS, NPG], I32)
        c_sbias = din("c_sbias", [128, NCHK * 32])
        y_sample = dout("y_sample", [NS, D])
        k_a_sample = dout("k_a_sample", [L, NS, 512])
        v_a_sample = dout("v_a_sample", [L, NS, 512])
        conv_b_sample = dout("conv_b_sample", [L, NS, 512])
        conv_c_sample = dout("conv_c_sample", [L, NS, 30 * 256])
        d_v_sample = dout("d_v_sample", [L, NS, 256])
        qs_dram = nc.dram_tensor("qs_scratch", [NS, 512], F32).ap()
        qx_dram = nc.dram_tensor("qx_scratch", [NS, 512], F32).ap()
        om_dram = nc.dram_tensor("om_scratch", [NS, 4, 512], F32).ap()
        ol_dram = nc.dram_tensor("ol_scratch", [1, NS * 8], F32).ap()
        pa_dram = nc.dram_tensor("pa_scratch", [NG * NCOL, 516], F32).ap()

    T = Tr(nc, stack)

    def sb(name, shape, dt):
        return stack.enter_context(nc.sbuf_tensor(name, list(shape), dt))

    identf = sb("identf", [128, 128], F32)
    identb = sb("identb", [128, 128], BF16)
    maskf = sb("maskf", [128, 128], F32)
    maskb = sb("maskb", [128, 128], BF16)
    trilf = sb("trilf", [128, 128], F32)
    alibi = sb("alibi", [128, 64], F32)
    ones_dm = sb("ones_dm", [128, 128], BF16)
    ones_c = sb("ones_c", [128, 128], BF16)
    ones_h = sb("ones_h", [128, 128], BF16)
    ones_1 = sb("ones_1", [128, 128], BF16)
    ones_f = sb("ones_f", [128, 2], F32)
    g_mix = sb("g_mix", [128, NCH], F32)
    g_x = sb("g_x", [128, NCH], F32)
    g_ffn = sb("g_ffn", [128, NCH], F32)
    g_mem = sb("g_mem", [128, NCH], F32)
    gq_b = sb("gq_b", [128, 64], F32)
    gk_b = sb("gk_b", [128, 64], F32)
    gsub_b = sb("gsub_b", [128, 128], F32)
    lam_b = sb("lam_b", [128, 256], F32)
    lam_t = sb("lam_t", [128, 128], F32)
    lam_s = sb("lam_s", [128, 8], F32)
    bcw = sb("bcw", [128, 2, 3], F32)
    bcb = sb("bcb", [128, 2], F32)
    ccw = sb("ccw", [128, 2, 31], F32)
    ccb = sb("ccb", [128, 2], F32)
    clg = sb("clg", [128, 2], F32)
    clb = sb("clb", [128, 2], F32)
    dlg_b = sb("dlg_b", [128, 256], F32)
    dlb_b = sb("dlb_b", [128, 256], F32)
    bsT = sb("bsT", [128, 2, 128], F32)
    wsf = sb("wsf", [128, 4, 128], F32)
    wsb = sb("wsb", [128, 4, 128], BF16)
    WT = sb("WT", [128, 4, 128], BF16)
    gxq = sb("gxq", [128, 1], F32)
    gxk_b = sb("gxk_b", [128, 128], F32)
    mkT = sb("mkT", [128, 4, 256], BF16)
    mvb = sb("mvb", [128, 2, 512], BF16)
    memTg = sb("memTg", [128, NCH, 256], BF16)
    mrstd = sb("mrstd", [128, 2], F32)
    kT = sb("kT", [128, 4, SEQ], BF16)
    Vx = sb("Vx", [128, NKT, 4, 132], BF16)
    xT = sb("xT", [128, NCH, TT], F32)
    H = sb("H", [128, NCH, TT], BF16)
    S1 = sb("S1", [128, NCH, TT], BF16)
    RS = sb("RS", [128, TT], F32)
    SCR = sb("SCR", [128, 8448], F32)
    MG = SCR[:, 0:4096].rearrange("p (a b) -> p a b", a=8)
    FA = SCR[:, 4096:5184].rearrange("p (a b) -> p a b", a=2)
    FB = SCR[:, 5184:6272].rearrange("p (a b) -> p a b", a=2)
    FC = SCR[:, 6272:7296].rearrange("p (a b) -> p a b", a=2)
    FD = SCR[:, 7296:8320].rearrange("p (a b) -> p a b", a=2)
    MGK = tuple(("MG", f) for f in range(8))
    FK = ("FA", "FB", "FC", "FD")
    T1 = sb("T1", [128, 512], F32)
    T2 = sb("T2", [128, 512], F32)
    T3 = sb("T3", [128, 512], F32)
    TB = sb("TB", [128, 512], BF16)
    qT = sb("qT", [128, 4, TT], BF16)
    PT = [sb("PT%d" % i, [128, 16, 128], BF16) for i in range(2)]
    SG = [sb("SG%d" % i, [128, TT], F32) for i in range(2)]
    ST = sb("ST", [128, 64], F32)
    NSLOT = 4
    WS = [sb("WS%d" % i, [128, 4096], BF16) for i in range(NSLOT)]
    PS = [stack.enter_context(nc.psum_tensor("ps%d" % i, [128, 512], F32)) for i in range(8)]
    if with_sample:
        xsT = sb("xsT", [128, NCH, NS], F32)
        Hs = sb("Hs", [128, NCH, NS], BF16)
        S1s = sb("S1s", [128, NCH, NS], BF16)
        MGs = sb("MGs", [128, NCH, NS], F32)
        RSs = sb("RSs", [128, NS], F32)
        SGs = [sb("SGs%d" % i, [128, NS], F32) for i in range(2)]
        Ssc = sb("Ssc", [128, 8, 8], F32)
        Pz = sb("Pz", [128, 8, NCOL], F32)
        Pzs = sb("Pzs", [128, NCOL], F32)
        PTI = sb("PTI", [128, NG], I32)
        sbias = sb("sbias", [128, NCHK * 32], F32)
        Pc = sb("Pc", [128, NS, 8], F32)
        w00 = sb("w00", [128, 8], F32)

    st = dict(bank=0, slot=0, pt=0, sg=0, sgs=0)

    def bank():
        b = st["bank"]
        st["bank"] = (b + 1) % 8
        return PS[b], ("ps", b)

    CP = Cx()
    CP.xT, CP.H, CP.S1, CP.MG, CP.RS, CP.N = xT, H, S1, MG, RS, TT
    CP.kx, CP.kh, CP.ks1, CP.krs, CP.kmg = "xT", "H", "S1", "RS", "MG"
    CP.SG, CP.sgk, CP.sgi = SG, "SG", "sg"
    if with_sample:
        CS = Cx()
        CS.xT, CS.H, CS.S1, CS.MG, CS.RS, CS.N = xsT, Hs, S1s, MGs, RSs, NS
        CS.kx, CS.kh, CS.ks1, CS.krs, CS.kmg = "xsT", "Hs", "S1s", "RSs", "MGs"
        CS.SG, CS.sgk, CS.sgi = SGs, "SGs", "sgs"

    def sgbuf(cx):
        i = st[cx.sgi]
        st[cx.sgi] = 1 - i
        return cx.SG[i], (cx.sgk, i)

    def mm(out, lhsT, rhs, start, stop, r, w):
        T.op("pe", lambda e, o=out, a=lhsT, b=rhs, s0=start, s1=stop: e.matmul(o, a, b, start=s0, stop=s1), r=r, w=w)

    def tp(out, in_, ident, r, w):
        T.op("pe", lambda e, o=out, a=in_, i=ident: e.transpose(o, a, i), r=r, w=w)

    def act(out, in_, func, r, w, bias=None, scale=None):
        def f(e, o=out, a=in_, fn=func, b=bias, s=scale):
            kw = {}
            if b is not None:
                kw["bias"] = b
            if s is not None:
                kw["scale"] = s
            return e.activation(o, a, fn, **kw)
        T.op("act", f, r=r, w=w)

    def tsc(eng, out, in0, s1, s2, op0, op1, r, w):
        if op1 is None:
            T.op(eng, lambda e, o=out, a=in0, x=s1, p=op0: e.tensor_scalar(o, a, x, 0.0, p, ALU.add), r=r, w=w)
        else:
            T.op(eng, lambda e, o=out, a=in0, x=s1, y=s2, p=op0, q=op1: e.tensor_scalar(o, a, x, y, p, q), r=r, w=w)

    def rstd(out, in_, scale, r, w):
        act(out, in_, AF.Sqrt, r, w, bias=EPS, scale=scale)
        T.op("dve", lambda e, o=out: e.reciprocal(o, o), r=w, w=w)

    def recip(out, in_, r, w):
        T.op("dve", lambda e, o=out, a=in_: e.reciprocal(o, a), r=r, w=w)

    def red(out, in_, r, w):
        T.op("dve", lambda e, o=out, a=in_: e.tensor_reduce(o, a, AX.X, ALU.add), r=r, w=w)

    def stt(eng, out, in0, sc, in1, op0, op1, r, w):
        T.op(eng, lambda e, o=out, a=in0, s=sc, b=in1, p=op0, q=op1: e.scalar_tensor_tensor(o, a, s, b, p, q), r=r, w=w)

    def tt(eng, out, in0, in1, op, r, w):
        T.op(eng, lambda e, o=out, a=in0, b=in1, p=op: e.tensor_tensor(o, a, b, p), r=r, w=w)

    def cp(eng, out, in_, r, w):
        if eng == "act":
            T.op("act", lambda e, o=out, a=in_: e.copy(o, a), r=r, w=w)
        else:
            T.op(eng, lambda e, o=out, a=in_: e.tensor_copy(o, a), r=r, w=w)

    def dma(q, out, in_, r, w, isout=False, slow=False):
        if slow:
            T.dma(q, lambda e, o=out, a=in_: e.dma_start(out=o, in_=a, allow_slow_non_contiguous=True), r=r, w=w, out=isout)
        else:
            T.dma(q, lambda e, o=out, a=in_: e.dma_start(out=o, in_=a), r=r, w=w, out=isout)

    def wload(src_ap, shape3):
        i = st["slot"]
        st["slot"] = (i + 1) % NSLOT
        a, b = shape3
        dst = WS[i][:, 0:a * b].rearrange("p (a b) -> p a b", a=a)
        T.dma_sw(lambda e, o=dst, s=src_ap: e.dma_start(out=o, in_=s), (), (("WS", i),), ("WS", i))
        return dst, ("WS", i)

    def wview(w2d, r0, nr, c0, ncol):
        return w2d[r0 * 128:(r0 + nr) * 128, c0:c0 + ncol].rearrange("(k p) n -> p k n", p=128)

    dma("sp", identf[:], c_ident, (), ("identf",))
    dma("sp", maskf[:], c_mask, (), ("maskf",))
    dma("sp", trilf[:], c_tril, (), ("trilf",))
    dma("sp", alibi[:], c_alibi, (), ("alibi",))
    cp("dve", identb[:], identf[:], ("identf",), ("identb",))
    cp("dve", maskb[:], maskf[:], ("maskf",), ("maskb",))
    T.op("pool", lambda e: e.memset(ones_dm[:], 1.0 / D), w=("ones_dm",))
    T.op("pool", lambda e: e.memset(ones_c[:], 1.0 / 256), w=("ones_c",))
    T.op("pool", lambda e: e.memset(ones_h[:], 1.0 / 128), w=("ones_h",))
    T.op("pool", lambda e: e.memset(ones_1[:], 1.0), w=("ones_1",))
    T.op("pool", lambda e: e.memset(ones_f[:], 1.0), w=("ones_f",))
    T.op("pool", lambda e: e.memset(Vx[:], 1.0), w=("Vx",))
    if with_sample:
        T.op("pool", lambda e: e.memset(Pz[:], 0.0), w=("Pz",))
        dma("sp", sbias[:], c_sbias, (), ("sbias",))
        T.op("pool", lambda e: e.memset(PTI[:], 0), w=("PTI",))
        for g in range(NG):
            dma("sp", PTI[0:BPG * NPG, g:g + 1], page_table[g * BPG:(g + 1) * BPG, :].rearrange("b (j o) -> (b j) o", o=1),
                (), ("PTI",), slow=True)

    def rmsnorm_fm(cx, gt, gkey):
        N = cx.N
        act(cx.S1[:, :, 0:N], cx.xT[:, :, 0:N], AF.Square, (cx.kx,), (cx.ks1,))
        ps, pk = bank()
        for c in range(NCH):
            mm(ps[:, 0:N], ones_dm[:], cx.S1[:, c, 0:N], c == 0, c == NCH - 1, ("ones_dm", cx.ks1), (pk,))
        rstd(cx.RS[:, 0:N], ps[:, 0:N], 1.0, (pk,), (cx.krs,))
        for c in range(NCH):
            stt("dve", cx.H[:, c, 0:N], cx.xT[:, c, 0:N], gt[:, c:c + 1], cx.RS[:, 0:N], ALU.mult, ALU.mult,
                (cx.kx, cx.krs, gkey), (cx.kh,))

    def load_layer_params(l):
        def fm(dst, src, key):
            dma("sp", dst[:], src.rearrange("(c p) -> p c", p=128), (), (key,), slow=True)
        fm(g_mix, W["norm_mix_g"][l], "g_mix")
        fm(g_x, W["norm_x_g"][l], "g_x")
        fm(g_ffn, W["norm_ffn_g"][l], "g_ffn")
        fm(g_mem, W["mem_norm_g"][l], "g_mem")
        fm(bcb, W["b_conv_b"][l], "bcb")
        fm(ccb, W["c_conv_b"][l], "ccb")
        fm(clg, W["c_ln_g"][l], "clg")
        fm(clb, W["c_ln_b"][l], "clb")
        for cc in range(2):
            dma("sp", bcw[:, cc, :], W["b_conv_w"][l][:, cc * 128:(cc + 1) * 128].rearrange("k p -> p k"), (), ("bcw",), slow=True)
            dma("sp", ccw[:, cc, :], W["c_conv_w"][l][:, cc * 128:(cc + 1) * 128].rearrange("k p -> p k"), (), ("ccw",), slow=True)
        dma("sp", gxq[:], W["x_qnorm_g"][l].rearrange("(p o) -> p o", o=1), (), ("gxq",), slow=True)

        def bc(dst, src, key):
            dma("sp", dst[:], src.partition_broadcast(128), (), (key,))
        bc(gq_b, W["a_qnorm_g"][l], "gq_b")
        bc(gk_b, W["a_knorm_g"][l], "gk_b")
        bc(gsub_b, W["a_subln_g"][l], "gsub_b")
        bc(lam_b, W["a_lam"][l].rearrange("a b -> (a b)"), "lam_b")
        bc(dlg_b, W["d_ln_g"][l], "dlg_b")
        bc(dlb_b, W["d_ln_b"][l], "dlb_b")
        bc(gxk_b, W["x_knorm_g"][l], "gxk_b")
        for g in range(4):
            cc, gg = g // 2, g % 2
            dma("sp", bsT[gg * 64:(gg + 1) * 64, cc, :], W["d_bs"][l, g].partition_broadcast(64), (), ("bsT",))
        dma("sp", wsf[:], W["d_ws"][l].rearrange("g t s -> t g s"), (), ("wsf",))
        if with_sample:
            dma("sp", w00[:, 0:4], W["d_ws"][l][:, 0, 0].partition_broadcast(128), (), ("w00",), slow=True)
            dma("sp", w00[:, 4:8], W["d_bs"][l][:, 0].partition_broadcast(128), (), ("w00",), slow=True)
        lam_init = 0.8 - 0.6 * math.exp(-0.3 * l)
        tsc("dve", gq_b[:], gq_b[:], 0.125, None, ALU.mult, None, ("gq_b",), ("gq_b",))
        tsc("dve", gsub_b[:], gsub_b[:], 1.0 - lam_init, None, ALU.mult, None, ("gsub_b",), ("gsub_b",))
        tt("dve", lam_t[:, 0:64], lam_b[:, 0:64], lam_b[:, 64:128], ALU.mult, ("lam_b",), ("lam_t",))
        tt("dve", lam_t[:, 64:128], lam_b[:, 128:192], lam_b[:, 192:256], ALU.mult, ("lam_b",), ("lam_t",))
        red(lam_s[:, 0:2], lam_t[:, 0:128].rearrange("p (a b) -> p a b", a=2), ("lam_t",), ("lam_s",))
        act(lam_s[:, 2:4], lam_s[:, 0:2], AF.Exp, ("lam_s",), ("lam_s",))
        tt("dve", lam_s[:, 4:5], lam_s[:, 3:4], lam_s[:, 2:3], ALU.subtract, ("lam_s",), ("lam_s",))
        tsc("dve", lam_s[:, 4:5], lam_s[:, 4:5], -lam_init, None, ALU.add, None, ("lam_s",), ("lam_s",))
        tt("dve", wsb[:], wsf[:], trilf[:, None, :].to_broadcast([128, 4, 128]), ALU.mult, ("wsf", "trilf"), ("wsb",))
        ps, pk = bank()
        psb = ps[:].bitcast(BF16)
        for g in range(4):
            tp(psb[:, g * 128:(g + 1) * 128], wsb[:, g, :], identb[:], ("wsb", "identb"), (pk,))
        cp("dve", WT[:].rearrange("p g t -> p (g t)"), psb[:, 0:512], (pk,), ("WT",))

    def mem_layer(l):
        for tcn in range(2):
            dma("sp", T1[:], mem_prompt[tcn * 128:(tcn + 1) * 128, 0:512], (), ("T1",))
            dma("sp", T2[:], mem_prompt[tcn * 128:(tcn + 1) * 128, 512:1024], (), ("T2",))
            act(T3[:], T1[:], AF.Square, ("T1",), ("T3",))
            red(ST[:, 0:1], T3[:], ("T3",), ("ST",))
            act(T3[:], T2[:], AF.Square, ("T2",), ("T3",))
            red(ST[:, 1:2], T3[:], ("T3",), ("ST",))
            tt("dve", ST[:, 2:3], ST[:, 0:1], ST[:, 1:2], ALU.add, ("ST",), ("ST",))
            rstd(mrstd[:, tcn:tcn + 1], ST[:, 2:3], 1.0 / D, ("ST",), ("mrstd",))
            for half, src, sk in ((0, T1, "T1"), (1, T2, "T2")):
                cp("dve", TB[:], src[:], (sk,), ("TB",))
                ps, pk = bank()
                psb = ps[:].bitcast(BF16)
                for j in range(4):
                    tp(psb[:, j * 128:(j + 1) * 128], TB[:, j * 128:(j + 1) * 128], identb[:], ("TB", "identb"), (pk,))
                for j in range(4):
                    c = half * 4 + j
                    tsc("dve", memTg[:, c, tcn * 128:(tcn + 1) * 128], psb[:, j * 128:(j + 1) * 128],
                        g_mem[:, c:c + 1], None, ALU.mult, None, (pk, "g_mem"), ("memTg",))
        wk, wkk = wload(wview(W["w_xk"][l], 0, 8, 0, 512), (8, 512))
        wv, wvk = wload(wview(W["w_xv"][l], 0, 8, 0, 512), (8, 512))
        for tcn in range(2):
            ps, pk = bank()
            for c in range(NCH):
                mm(ps[:], memTg[:, c, tcn * 128:(tcn + 1) * 128], wk[:, c, :], c == 0, c == NCH - 1, ("memTg", wkk), (pk,))
            act(T1[:], ps[:], AF.Copy, (pk, "mrstd"), ("T1",), scale=mrstd[:, tcn:tcn + 1])
            tt("dve", T3[:], T1[:], T1[:], ALU.mult, ("T1",), ("T3",))
            red(ST[:, 0:4], T3[:].rearrange("p (h d) -> p h d", h=4), ("T3",), ("ST",))
            rstd(ST[:, 0:4], ST[:, 0:4], 1.0 / 128, ("ST",), ("ST",))
            v3 = T1[:].rearrange("p (h d) -> p h d", h=4)
            tt("dve", v3, v3, ST[:, 0:4, None].to_broadcast([128, 4, 128]), ALU.mult, ("T1", "ST"), ("T1",))
            tt("dve", v3, v3, gxk_b[:, None, :].to_broadcast([128, 4, 128]), ALU.mult, ("T1", "gxk_b"), ("T1",))
            dma("sp", mem_k_prompt[l, tcn * 128:(tcn + 1) * 128, :], T1[:], ("T1",), (), isout=True)
            cp("dve", TB[:], T1[:], ("T1",), ("TB",))
            ps2, pk2 = bank()
            psb = ps2[:].bitcast(BF16)
            for h in range(4):
                tp(psb[:, h * 128:(h + 1) * 128], TB[:, h * 128:(h + 1) * 128], identb[:], ("TB", "identb"), (pk2,))
            for h in range(4):
                cp("dve", mkT[:, h, tcn * 128:(tcn + 1) * 128], psb[:, h * 128:(h + 1) * 128], (pk2,), ("mkT",))
            ps, pk = bank()
            for c in range(NCH):
                mm(ps[:], memTg[:, c, tcn * 128:(tcn + 1) * 128], wv[:, c, :], c == 0, c == NCH - 1, ("memTg", wvk), (pk,))
            act(T2[:], ps[:], AF.Copy, (pk, "mrstd"), ("T2",), scale=mrstd[:, tcn:tcn + 1])
            dma("sp", mem_v_prompt[l, tcn * 128:(tcn + 1) * 128, :], T2[:], ("T2",), (), isout=True)
            cp("dve", mvb[:, tcn, :], T2[:], ("T2",), ("mvb",))

    def load_x_tile(l, i):
        if l == 0:
            for s in range(4):
                r0 = i * TT + s * 128
                dma("sp", T1[:], x_prompt[r0:r0 + 128, 0:512], (), ("T1",))
                dma("sp", T2[:], x_prompt[r0:r0 + 128, 512:1024], (), ("T2",))
                for half, src, sk in ((0, T1, "T1"), (1, T2, "T2")):
                    ps, pk = bank()
                    for j in range(4):
                        tp(ps[:, j * 128:(j + 1) * 128], src[:, j * 128:(j + 1) * 128], identf[:], (sk, "identf"), (pk,))
                    for j in range(4):
                        cp("act", xT[:, half * 4 + j, s * 128:(s + 1) * 128], ps[:, j * 128:(j + 1) * 128], (pk,), ("xT",))
        else:
            dma("sp", xT[:], xs_dram[:, :, i * TT:(i + 1) * TT], (("xs", i),), ("xT",))

    def store_x_tile(l, i):
        if l < L - 1:
            dma("sp", xs_dram[:, :, i * TT:(i + 1) * TT], xT[:], ("xT",), (("xs", i),))
        else:
            for s in range(4):
                r0 = i * TT + s * 128
                for half, dst, dk in ((0, T1, "T1"), (1, T2, "T2")):
                    ps, pk = bank()
                    for j in range(4):
                        tp(ps[:, j * 128:(j + 1) * 128], xT[:, half * 4 + j, s * 128:(s + 1) * 128], identf[:],
                           ("xT", "identf"), (pk,))
                    cp("act", dst[:], ps[:], (pk,), (dk,))
                    dma("sp", y_prompt[r0:r0 + 128, half * 512:(half + 1) * 512], dst[:], (dk,), (), isout=True)

    def branch_out(cx, l, bi, wout_name, nk, s1_base, first):
        N = cx.N
        for half in range(2):
            gcol = 3328 + bi * D + half * 512
            wg, wgk = wload(wview(W["w_in"][l], 0, 8, gcol, 512), (8, 512))
            wo, wok = wload(wview(W[wout_name][l], 0, nk, half * 512, 512), (nk, 512))
            for fl in range(4):
                f = half * 4 + fl
                psy, pky = bank()
                for kc in range(nk):
                    mm(psy[:, 0:N], wo[:, kc, fl * 128:(fl + 1) * 128], cx.S1[:, s1_base + kc, 0:N], kc == 0, kc == nk - 1,
                       (wok, cx.ks1), (pky,))
                psg, pkg = bank()
                for c in range(NCH):
                    mm(psg[:, 0:N], wg[:, c, fl * 128:(fl + 1) * 128], cx.H[:, c, 0:N], c == 0, c == NCH - 1, (wgk, cx.kh), (pkg,))
                sg, sgk = sgbuf(cx)
                act(sg[:, 0:N], psg[:, 0:N], AF.Sigmoid, (pkg,), (sgk,))
                mk = (cx.kmg, f)
                if first:
                    tt("dve", cx.MG[:, f, 0:N], psy[:, 0:N], sg[:, 0:N], ALU.mult, (pky, sgk), (mk,))
                else:
                    tt("dve", sg[:, 0:N], psy[:, 0:N], sg[:, 0:N], ALU.mult, (pky, sgk), (sgk,))
                    tt("dve", cx.MG[:, f, 0:N], cx.MG[:, f, 0:N], sg[:, 0:N], ALU.add, (mk, sgk), (mk,))

    def merge_proj(cx, l):
        N = cx.N
        for f in range(NCH):
            cp("act", cx.H[:, f, 0:N], cx.MG[:, f, 0:N], ((cx.kmg, f),), (cx.kh,))
        if KDBG <= 5.8 and cx.N != TT:
            return
        for half in range(2):
            if KDBG <= 5.9 and cx.N != TT and half == 1:
                return
            if KDBG == 5.95 and cx.N != TT and half == 0:
                continue
            wo, wok = wload(wview(W["w_o"][l], 0, 8, half * 512, 512), (8, 512))
            for fl in range(4):
                f = half * 4 + fl
                ps, pk = bank()
                for c in range(NCH):
                    mm(ps[:, 0:N], wo[:, c, fl * 128:(fl + 1) * 128], cx.H[:, c, 0:N], c == 0, c == NCH - 1, (wok, cx.kh), (pk,))
                tt("dve", cx.xT[:, f, 0:N], cx.xT[:, f, 0:N], ps[:, 0:N], ALU.add, (cx.kx, pk), (cx.kx,))

    def xo_proj(cx, l):
        N = cx.N
        for half in range(2):
            wxo, wxok = wload(wview(W["w_xo"][l], 0, 4, half * 512, 512), (4, 512))
            for fl in range(4):
                f = half * 4 + fl
                ps, pk = bank()
                for kc in range(4):
                    mm(ps[:, 0:N], wxo[:, kc, fl * 128:(fl + 1) * 128], cx.S1[:, kc, 0:N], kc == 0, kc == 3, (wxok, cx.ks1), (pk,))
                tt("dve", cx.xT[:, f, 0:N], cx.xT[:, f, 0:N], ps[:, 0:N], ALU.add, (cx.kx, pk), (cx.kx,))

    def ffn(cx, l, Aflat, akeys):
        N = cx.N
        rmsnorm_fm(cx, g_ffn, "g_ffn")
        for half in range(2):
            for j4 in range(4):
                wu, wuk = wload(wview(W["w_up"][l], 0, 8, (half * 16 + j4 * 4) * 128, 512), (8, 512))
                for jj in range(4):
                    j = j4 * 4 + jj
                    ps, pk = bank()
                    for c in range(NCH):
                        mm(ps[:, 0:N], wu[:, c, jj * 128:(jj + 1) * 128], cx.H[:, c, 0:N], c == 0, c == NCH - 1, (wuk, cx.kh), (pk,))
                    sg, sgk = sgbuf(cx)
                    act(sg[:, 0:N], ps[:, 0:N], AF.Relu, (pk,), (sgk,))
                    act(Aflat[:, j * N:(j + 1) * N], sg[:, 0:N], AF.Square, (sgk,), (akeys(j),))
            for fp in range(4):
                wdn, wdnk = wload(wview(W["w_down"][l], half * 16, 16, fp * 256, 256), (16, 256))
                for fl in range(2):
                    f = fp * 2 + fl
                    ps, pk = bank()
                    for j in range(16):
                        mm(ps[:, 0:N], wdn[:, j, fl * 128:(fl + 1) * 128], Aflat[:, j * N:(j + 1) * N],
                           j == 0, j == 15, (wdnk, akeys(j)), (pk,))
                    tt("dve", cx.xT[:, f, 0:N], cx.xT[:, f, 0:N], ps[:, 0:N], ALU.add, (cx.kx, pk), (cx.kx,))

    def gelu_from_psum(ps_ap, out_ap, tmp, tmpk, pk, outk):
        act(tmp, ps_ap, AF.Square, (pk,), (tmpk,))
        tsc("dve", tmp, tmp, 0.044715, 1.0, ALU.mult, ALU.add, (tmpk,), (tmpk,))
        tt("dve", tmp, tmp, ps_ap, ALU.mult, (tmpk, pk), (tmpk,))
        act(tmp, tmp, AF.Sigmoid, (tmpk,), (tmpk,), scale=1.5957691216)
        tt("dve", out_ap, tmp, ps_ap, ALU.mult, (tmpk, pk), (outk,))

    def qknorm_tm(stage, sk, P, ngrp, gdim, sqbuf, sqk, stcol):
        act(sqbuf, stage, AF.Square, (sk,), (sqk,))
        red(ST[0:P, stcol:stcol + ngrp], sqbuf.rearrange("p (g d) -> p g d", g=ngrp), (sqk,), ("ST",))
        rstd(ST[0:P, stcol:stcol + ngrp], ST[0:P, stcol:stcol + ngrp], 1.0 / gdim, ("ST",), ("ST",))
        v3 = stage.rearrange("p (g d) -> p g d", g=ngrp)
        tt("dve", v3, v3, ST[0:P, stcol:stcol + ngrp, None].to_broadcast([P, ngrp, gdim]), ALU.mult, (sk, "ST"), (sk,))
        return v3

    def tile_layer(l, i):
        cx = CP
        lastt = (i == NT - 1)
        w_in = W["w_in"][l]
        load_x_tile(l, i)
        rmsnorm_fm(cx, g_mix, "g_mix")

        wq, wqk = wload(wview(w_in, 0, 8, 0, 512), (8, 512))
        wkk_, wkkk = wload(wview(w_in, 0, 8, 512, 512), (8, 512))
        wv, wvk = wload(wview(w_in, 0, 8, 1024, 512), (8, 512))
        for s in range(4):
            tok = slice(s * 128, (s + 1) * 128)
            jg = i * 4 + s
            r0 = i * TT + s * 128
            for which, wt, wtk, stage, sk in ((0, wq, wqk, T1, "T1"), (1, wkk_, wkkk, T2, "T2")):
                ps, pk = bank()
                for c in range(NCH):
                    mm(ps[:], H[:, c, tok], wt[:, c, :], c == 0, c == NCH - 1, ("H", wtk), (pk,))
                cp("act", stage[:], ps[:], (pk,), (sk,))
                v3 = qknorm_tm(stage[:], sk, 128, 8, 64, T3[:], "T3", 0)
                if which == 0:
                    tt("dve", TB[:].rearrange("p (g d) -> p g d", g=8), v3, gq_b[:, None, :].to_broadcast([128, 8, 64]),
                       ALU.mult, (sk, "gq_b"), ("TB",))
                else:
                    tt("dve", v3, v3, gk_b[:, None, :].to_broadcast([128, 8, 64]), ALU.mult, (sk, "gk_b"), (sk,))
                    dma("sp", k_a_prompt[l, r0:r0 + 128, :], stage[:], (sk,), (), isout=True)
                    cp("dve", TB[:], stage[:], (sk,), ("TB",))
                ps2, pk2 = bank()
                psb = ps2[:].bitcast(BF16)
                for h in range(4):
                    tp(psb[:, h * 128:(h + 1) * 128], TB[:, h * 128:(h + 1) * 128], identb[:], ("TB", "identb"), (pk2,))
                if which == 0:
                    cp("act", qT[:, :, tok], psb[:, 0:512].rearrange("p (h t) -> p h t", h=4), (pk2,), ("qT",))
                else:
                    cp("act", kT[:, :, r0:r0 + 128], psb[:, 0:512].rearrange("p (h t) -> p h t", h=4), (pk2,), ("kT",))
            ps, pk = bank()
            for c in range(NCH):
                mm(ps[:], H[:, c, tok], wv[:, c, :], c == 0, c == NCH - 1, ("H", wvk), (pk,))
            cp("act", T3[:], ps[:], (pk,), ("T3",))
            dma("sp", v_a_prompt[l, r0:r0 + 128, :], T3[:], ("T3",), (), isout=True)
            cp("dve", Vx[:, jg, :, 0:128], T3[:].rearrange("p (h d) -> p h d", h=4), ("T3",), ("Vx",))

        for s in range(4):
            tok = slice(s * 128, (s + 1) * 128)
            jg = i * 4 + s
            nk_t = jg + 1
            for h in range(4):
                pso = []
                pipelined = nk_t <= 8
                stage = []
                for c in range(2):
                    prow = slice(c * 64, (c + 1) * 64)
                    pi = st["pt"]
                    st["pt"] = 1 - pi
                    ptb, ptk = PT[pi], ("PT", pi)
                    sbanks = []
                    for kt in range(nk_t):
                        if kt % 4 == 0:
                            sbanks.append(bank())
                        pss, pks = sbanks[-1]
                        blk = slice((kt % 4) * 128, (kt % 4 + 1) * 128)
                        diag = (kt == jg)
                        mm(pss[:, blk], kT[prow, h, kt * 128:(kt + 1) * 128], qT[prow, h, tok], True, not diag,
                           ("kT", "qT"), (pks,))
                        if diag:
                            mm(pss[:, blk], identb[:], maskb[:], False, True, ("identb", "maskb"), (pks,))

                    def finish_c(ptb=ptb, ptk=ptk, sbanks=sbanks):
                        for kt in range(nk_t):
                            pss, pks = sbanks[kt // 4]
                            blk = slice((kt % 4) * 128, (kt % 4 + 1) * 128)
                            dl = jg - kt
                            act(ptb[:, kt, :], pss[:, blk], AF.Exp, (pks, "alibi"), (ptk,),
                                bias=alibi[:, h * 16 + dl:h * 16 + dl + 1], scale=1.0)
                        po, pok = bank()
                        for kt in range(nk_t):
                            mm(po[:, 0:129], ptb[:, kt, :], Vx[:, kt, h, 0:129], kt == 0, kt == nk_t - 1, (ptk, "Vx"), (pok,))
                        pso.append((po, pok))
                    if pipelined:
                        stage.append(finish_c)
                    else:
                        finish_c()
                for fn_ in stage:
                    fn_()
                (p0, k0), (p1, k1) = pso
                recip(ST[:, 8:9], p0[:, 128:129], (k0,), ("ST",))
                recip(ST[:, 9:10], p1[:, 128:129], (k1,), ("ST",))
                tt("dve", ST[:, 9:10], ST[:, 9:10], lam_s[:, 4:5], ALU.mult, ("ST", "lam_s"), ("ST",))
                tsc("dve", T1[:, 0:128], p0[:, 0:128], ST[:, 8:9], None, ALU.mult, None, (k0, "ST"), ("T1",))
                stt("dve", T1[:, 0:128], p1[:, 0:128], ST[:, 9:10], T1[:, 0:128], ALU.mult, ALU.add, (k1, "ST", "T1"), ("T1",))
                act(T1[:, 128:256], T1[:, 0:128], AF.Square, ("T1",), ("T1",))
                red(ST[:, 10:11], T1[:, 128:256], ("T1",), ("ST",))
                rstd(ST[:, 10:11], ST[:, 10:11], 1.0 / 128, ("ST",), ("ST",))
                stt("dve", TB[:, 0:128], T1[:, 0:128], ST[:, 10:11], gsub_b[:], ALU.mult, ALU.mult,
                    ("T1", "ST", "gsub_b"), ("TB",))
                pt_, ptk_ = bank()
                ptb_ = pt_[:].bitcast(BF16)
                tp(ptb_[:, 0:128], TB[:, 0:128], identb[:], ("TB", "identb"), (ptk_,))
                cp("act", S1[:, h, tok], ptb_[:, 0:128], (ptk_,), ("S1",))
        branch_out(cx, l, 0, "w_a_out", 4, 0, True)

        wb1, wb1k = wload(wview(w_in, 0, 8, 1536, 512), (8, 512))
        wb2, wb2k = wload(wview(w_in, 0, 8, 2048, 256), (8, 256))
        for cc in range(2):
            if i == 0:
                T.op("dve", lambda e, cc=cc: e.memset(FA[:, cc, 0:2], 0.0), w=("FA",))
            else:
                cp("dve", FA[:, cc, 0:2], FA[:, cc, 512:514], ("FA",), ("FA",))
            psx, pkx = bank()
            for c in range(NCH):
                mm(psx[:], wb1[:, c, cc * 128:(cc + 1) * 128], H[:, c, :], c == 0, c == NCH - 1, (wb1k, "H"), (pkx,))
            cp("act", FC[:, cc, :], psx[:], (pkx,), ("FC",))
            psb_, pkb = bank()
            for c in range(NCH):
                mm(psb_[:], wb1[:, c, 256 + cc * 128:256 + (cc + 1) * 128], H[:, c, :], c == 0, c == NCH - 1, (wb1k, "H"), (pkb,))
            cp("act", FD[:, cc, :], psb_[:], (pkb,), ("FD",))
            psc, pkc = bank()
            for c in range(NCH):
                mm(psc[:], wb2[:, c, cc * 128:(cc + 1) * 128], H[:, c, :], c == 0, c == NCH - 1, (wb2k, "H"), (pkc,))
            tt("dve", FA[:, cc, 2:514], psc[:], FC[:, cc, :], ALU.mult, (pkc, "FC"), ("FA",))
            tsc("dve", FC[:, cc, :], FA[:, cc, 2:514], bcw[:, cc, 2:3], bcb[:, cc:cc + 1], ALU.mult, ALU.add,
                ("FA", "bcw", "bcb"), ("FC",))
            stt("dve", FC[:, cc, :], FA[:, cc, 1:513], bcw[:, cc, 1:2], FC[:, cc, :], ALU.mult, ALU.add, ("FA", "FC", "bcw"), ("FC",))
            stt("dve", FC[:, cc, :], FA[:, cc, 0:512], bcw[:, cc, 0:1], FC[:, cc, :], ALU.mult, ALU.add, ("FA", "FC", "bcw"), ("FC",))
            tt("dve", S1[:, 4 + cc, :], FC[:, cc, :], FD[:, cc, :], ALU.mult, ("FC", "FD"), ("S1",))
            if lastt:
                dma("sp", conv_b_prompt[l, :, cc * 128:(cc + 1) * 128].rearrange("t c -> c t"), FA[:, cc, 512:514],
                    ("FA",), (), isout=True, slow=True)
        branch_out(cx, l, 1, "w_b_out", 2, 4, False)

        wc, wck = wload(wview(w_in, 0, 8, 2304, 512), (8, 512))
        for cc in range(2):
            if i == 0:
                T.op("dve", lambda e, cc=cc: e.memset(FB[:, cc, 0:30], 0.0), w=("FB",))
            else:
                cp("dve", FB[:, cc, 0:30], FB[:, cc, 512:542], ("FB",), ("FB",))
            psg, pkg = bank()
            for c in range(NCH):
                mm(psg[:], wc[:, c, 256 + cc * 128:256 + (cc + 1) * 128], H[:, c, :], c == 0, c == NCH - 1, (wck, "H"), (pkg,))
            act(FC[:, cc, :], psg[:], AF.Sigmoid, (pkg,), ("FC",))
            psa, pka = bank()
            for c in range(NCH):
                mm(psa[:], wc[:, c, cc * 128:(cc + 1) * 128], H[:, c, :], c == 0, c == NCH - 1, (wck, "H"), (pka,))
            tt("dve", FB[:, cc, 30:542], psa[:], FC[:, cc, :], ALU.mult, (pka, "FC"), ("FB",))
            tsc("dve", FC[:, cc, :], FB[:, cc, 30:542], ccw[:, cc, 30:31], ccb[:, cc:cc + 1], ALU.mult, ALU.add,
                ("FB", "ccw", "ccb"), ("FC",))
            for k in range(30):
                stt("dve", FC[:, cc, :], FB[:, cc, k:k + 512], ccw[:, cc, k:k + 1], FC[:, cc, :], ALU.mult, ALU.add,
                    ("FB", "FC", "ccw"), ("FC",))
            if lastt:
                pst, pkt = bank()
                tp(pst[0:30, 0:128], FB[:, cc, 512:542], identf[:], ("FB", "identf"), (pkt,))
                cp("act", T1[0:30, cc * 128:(cc + 1) * 128], pst[0:30, 0:128], (pkt,), ("T1",))
        if lastt:
            dma("sp", conv_c_prompt[l], T1[0:30, 0:256], ("T1",), (), isout=True)
        for cc in range(2):
            cp("act", S1[:, cc, :], FC[:, cc, :], ("FC",), ("S1",))
            act(S1[:, 2 + cc, :], FC[:, cc, :], AF.Square, ("FC",), ("S1",))
        psm, pkm = bank()
        for cc in range(2):
            mm(psm[:], ones_c[:], S1[:, cc, :], cc == 0, cc == 1, ("ones_c", "S1"), (pkm,))
        psq, pkq = bank()
        for cc in range(2):
            mm(psq[:], ones_c[:], S1[:, 2 + cc, :], cc == 0, cc == 1, ("ones_c", "S1"), (pkq,))
        cp("act", FD[:, 0, :], psm[:], (pkm,), ("FD",))
        tt("dve", FD[:, 1, :], FD[:, 0, :], FD[:, 0, :], ALU.mult, ("FD",), ("FD",))
        tt("dve", FD[:, 1, :], psq[:], FD[:, 1, :], ALU.subtract, (pkq, "FD"), ("FD",))
        rstd(FD[:, 1, :], FD[:, 1, :], 1.0, ("FD",), ("FD",))
        for cc in range(2):
            tt("dve", FC[:, cc, :], FC[:, cc, :], FD[:, 0, :], ALU.subtract, ("FC", "FD"), ("FC",))
            tt("dve", FC[:, cc, :], FC[:, cc, :], FD[:, 1, :], ALU.mult, ("FC", "FD"), ("FC",))
            act(S1[:, 6 + cc, :], FC[:, cc, :], AF.Silu, ("FC", "clg", "clb"), ("S1",), bias=clb[:, cc:cc + 1],
                scale=clg[:, cc:cc + 1])
        branch_out(cx, l, 2, "w_c_out", 2, 6, False)

        wd, wdk = wload(wview(w_in, 0, 8, 2816, 512), (8, 512))
        for cc in range(2):
            ps, pk = bank()
            for c in range(NCH):
                mm(ps[:], wd[:, c, cc * 128:(cc + 1) * 128], H[:, c, :], c == 0, c == NCH - 1, (wdk, "H"), (pk,))
            gelu_from_psum(ps[:], FD[:, cc, :], FC[:, cc, :], "FC", pk, "FD")
        psd = [bank(), bank()]
        for s in range(4):
            tok = slice(s * 128, (s + 1) * 128)
            ps, pk = bank()
            for c in range(NCH):
                mm(ps[:, 0:256], H[:, c, tok], wd[:, c, 256:512], c == 0, c == NCH - 1, ("H", wdk), (pk,))
            gelu_from_psum(ps[:, 0:256], T1[:, 0:256], T1[:, 256:512], "T1", pk, "T1")
            T.op("dve", lambda e: e.bn_stats(ST[:, 16:22], T1[:, 0:256]), r=("T1",), w=("ST",))
            T.op("dve", lambda e: e.bn_aggr(ST[:, 22:24], ST[:, 16:22]), r=("ST",), w=("ST",))
            rstd(ST[:, 23:24], ST[:, 23:24], 1.0, ("ST",), ("ST",))
            tsc("dve", T1[:, 0:256], T1[:, 0:256], ST[:, 22:23], ST[:, 23:24], ALU.subtract, ALU.mult, ("T1", "ST"), ("T1",))
            tt("dve", T1[:, 0:256], T1[:, 0:256], dlg_b[:], ALU.mult, ("T1", "dlg_b"), ("T1",))
            tt("dve", TB[:, 0:256], T1[:, 0:256], dlb_b[:], ALU.add, ("T1", "dlb_b"), ("TB",))
            for g in range(4):
                cc, gg = g // 2, g % 2
                pd, pdk = psd[cc]
                mm(pd[gg * 64:(gg + 1) * 64, tok], TB[:, g * 64:(g + 1) * 64], WT[:, g, :], True, True, ("TB", "WT"), (pdk,))
        for cc in range(2):
            pd, pdk = psd[cc]
            tt("dve", FC[:, cc, :].rearrange("p (s t) -> p s t", s=4), pd[:].rearrange("p (s t) -> p s t", s=4),
               bsT[:, cc, None, :].to_broadcast([128, 4, 128]), ALU.add, (pdk, "bsT"), ("FC",))
            tt("dve", S1[:, cc, :], FC[:, cc, :], FD[:, cc, :], ALU.mult, ("FC", "FD"), ("S1",))
        branch_out(cx, l, 3, "w_d_out", 2, 0, False)
        merge_proj(cx, l)

        rmsnorm_fm(cx, g_x, "g_x")
        wxq, wxqk = wload(wview(W["w_xq"][l], 0, 8, 0, 512), (8, 512))
        for h in range(4):
            psq_, pkq_ = bank()
            for c in range(NCH):
                mm(psq_[:], wxq[:, c, h * 128:(h + 1) * 128], H[:, c, :], c == 0, c == NCH - 1, (wxqk, "H"), (pkq_,))
            act(S1[:, 4, :], psq_[:], AF.Square, (pkq_,), ("S1",))
            psm_, pkm_ = bank()
            mm(psm_[:], ones_h[:], S1[:, 4, :], True, True, ("ones_h", "S1"), (pkm_,))
            rstd(RS[:], psm_[:], 1.0, (pkm_,), ("RS",))
            stt("dve", qT[:, h, :], psq_[:], gxq[:, 0:1], RS[:], ALU.mult, ALU.mult, (pkq_, "gxq", "RS"), ("qT",))
            ptb, ptk = PT[0], ("PT", 0)
            p2 = ptb[:].rearrange("p a b -> p (a b)")
            for kc in range(2):
                pss, pks = bank()
                mm(pss[:], mkT[:, h, kc * 128:(kc + 1) * 128], qT[:, h, :], True, True, ("mkT", "qT"), (pks,))
                act(p2[:, kc * 512:(kc + 1) * 512], pss[:], AF.Exp, (pks,), (ptk,), scale=128.0 ** -0.5)
            po, pok = bank()
            pl, plk = bank()
            for kc in range(2):
                mm(po[:], mvb[:, kc, h * 128:(h + 1) * 128], p2[:, kc * 512:(kc + 1) * 512], kc == 0, kc == 1, ("mvb", ptk), (pok,))
            for kc in range(2):
                mm(pl[:], ones_1[:], p2[:, kc * 512:(kc + 1) * 512], kc == 0, kc == 1, ("ones_1", ptk), (plk,))
            recip(RS[:], pl[:], (plk,), ("RS",))
            tt("dve", S1[:, h, :], po[:], RS[:], ALU.mult, (pok, "RS"), ("S1",))
        xo_proj(cx, l)
        ffn(cx, l, MG.rearrange("p a b -> p (a b)").bitcast(BF16), lambda j: ("MG", j // 2))
        store_x_tile(l, i)

    def tps_to_fm(src_bf, srck, nchunk, dst_base):
        ps, pk = bank()
        psb = ps[:].bitcast(BF16)
        for j in range(nchunk):
            tp(psb[:, j * NS:(j + 1) * NS], src_bf[0:NS, j * 128:(j + 1) * 128], identb[0:NS, 0:NS], (srck, "identb"), (pk,))
        for j in range(nchunk):
            cp("act", S1s[:, dst_base + j, 0:NS], psb[:, j * NS:(j + 1) * NS], (pk,), ("S1s",))

    def tm_linear(wt, wtk, ncol):
        ps, pk = bank()
        for c in range(NCH):
            mm(ps[0:NS, 0:ncol], Hs[:, c, 0:NS], wt[:, c, 0:ncol], c == 0, c == NCH - 1, ("Hs", wtk), (pk,))
        return ps, pk

    def sample_layer(l):
        cx = CS
        w_in = W["w_in"][l]
        QK = PT[0][:].rearrange("p a b -> p (a b)").bitcast(F32)
        VX = PT[1][:].rearrange("p a b -> p (a b)").bitcast(F32)
        q_s, k_s, v_s = QK[0:NS, 0:512], QK[0:NS, 512:1024], VX[0:NS, 0:512]
        kQ, kV = ("PT", 0), ("PT", 1)
        if l == 0:
            dma("sp", T1[0:NS, :], x_sample[:, 0:512], (), ("T1",))
            dma("sp", T2[0:NS, :], x_sample[:, 512:1024], (), ("T2",))
            ps, pk = bank()
            for half, src, sk in ((0, T1, "T1"), (1, T2, "T2")):
                for j in range(4):
                    c = half * 4 + j
                    tp(ps[:, c * NS:(c + 1) * NS], src[0:NS, j * 128:(j + 1) * 128], identf[0:NS, 0:NS], (sk, "identf"), (pk,))
            for c in range(NCH):
                cp("act", xsT[:, c, :], ps[:, c * NS:(c + 1) * NS], (pk,), ("xsT",))
        rmsnorm_fm(cx, g_mix, "g_mix")

        wq, wqk = wload(wview(w_in, 0, 8, 0, 512), (8, 512))
        wk_, wkk = wload(wview(w_in, 0, 8, 512, 512), (8, 512))
        wv, wvk = wload(wview(w_in, 0, 8, 1024, 512), (8, 512))
        ps, pk = tm_linear(wq, wqk, 512)
        cp("act", q_s, ps[0:NS, :], (pk,), (kQ,))
        ps, pk = tm_linear(wk_, wkk, 512)
        cp("act", k_s, ps[0:NS, :], (pk,), (kQ,))
        ps, pk = tm_linear(wv, wvk, 512)
        cp("act", v_s, ps[0:NS, :], (pk,), (kV,))
        v3 = qknorm_tm(q_s, kQ, NS, 8, 64, T3[0:NS, :], "T3", 0)
        tt("dve", v3, v3, gq_b[0:NS, None, :].to_broadcast([NS, 8, 64]), ALU.mult, (kQ, "gq_b"), (kQ,))
        v3 = qknorm_tm(k_s, kQ, NS, 8, 64, T3[0:NS, :], "T3", 0)
        tt("dve", v3, v3, gk_b[0:NS, None, :].to_broadcast([NS, 8, 64]), ALU.mult, (kQ, "gk_b"), (kQ,))
        dma("sp", k_a_sample[l], k_s, (kQ,), (), isout=True)
        dma("sp", v_a_sample[l], v_s, (kV,), (), isout=True)
        dma("sp", qs_dram, q_s, (kQ,), ("qs_dram",))

        if KDBG <= 1:
            return
        PU = BPG * NPG
        KC = SCR[0:PU, 0:4096]
        VC = SCR[0:PU, 4096:8192]
        for g in range(NG):
            for bl in range(BPG):
                dma("sp", T2[bl * NPG:(bl + 1) * NPG, :], qs_dram[g * BPG + bl].partition_broadcast(NPG), ("qs_dram",), ("T2",))
            po, pok = bank()
            pl, plk = bank()
            for ch in range(NCHK):
                eoff = (l * NCHK + ch) * NPOOL * 4096
                T.dma_sw(lambda e, eoff=eoff, g=g: e.indirect_dma_start(
                    out=KC, out_offset=None, in_=ck,
                    in_offset=bass.IndirectOffsetOnAxis(ap=PTI[0:PU, g:g + 1], axis=0), element_offset=eoff), ("PTI",), MGK, "KC")
                T.dma_sw(lambda e, eoff=eoff, g=g: e.indirect_dma_start(
                    out=VC, out_offset=None, in_=cv,
                    in_offset=bass.IndirectOffsetOnAxis(ap=PTI[0:PU, g:g + 1], axis=0), element_offset=eoff), ("PTI",), FK, "VC")
                K3 = KC.rearrange("p (r d) -> p r d", r=8)
                tt("dve", K3, K3, T2[0:PU, None, :].to_broadcast([PU, 8, 512]), ALU.mult, MGK + ("T2",), MGK)
                red(Ssc[0:PU].rearrange("p r c -> p (r c)"), KC.rearrange("p (a d) -> p a d", d=64), MGK, ("Ssc",))
                sv4 = Ssc[0:PU].rearrange("p r (h c) -> p r h c", c=2)
                tt("dve", sv4, sv4,
                   sbias[0:PU, ch * 32:(ch + 1) * 32].rearrange("p (r h) -> p r h", h=4)[:, :, :, None].to_broadcast([PU, 8, 4, 2]),
                   ALU.add, ("Ssc", "sbias"), ("Ssc",))
                for bl in range(BPG):
                    rows = slice(bl * NPG, (bl + 1) * NPG)
                    act(Pz[rows, :, bl * 8:(bl + 1) * 8], Ssc[rows, :, :], AF.Exp, ("Ssc",), ("Pz",))
                V3 = VC.rearrange("p (r d) -> p r d", r=8)
                for r8 in range(8):
                    mm(po[0:NCOL, :], Pz[0:PU, r8, :], V3[:, r8, :], ch == 0 and r8 == 0, ch == NCHK - 1 and r8 == 7,
                       ("Pz",) + FK, (pok,))
                red(Pzs[0:PU, :], Pz[0:PU].rearrange("p r c -> p c r"), ("Pz",), ("Pzs",))
                mm(pl[0:NCOL, 0:2], Pzs[0:PU, :], ones_f[0:PU, 0:2], ch == 0, ch == NCHK - 1, ("Pzs", "ones_f"), (plk,))
            cp("act", T3[0:NCOL, :], po[0:NCOL, :], (pok,), ("T3",))
            cp("act", ST[0:NCOL, 30:31], pl[0:NCOL, 0:1], (plk,), ("ST",))
            dma("sp", pa_dram[g * NCOL:(g + 1) * NCOL, 0:512], T3[0:NCOL, :], ("T3",), ("pa_dram",))
            dma("sp", pa_dram[g * NCOL:(g + 1) * NCOL, 512:513], ST[0:NCOL, 30:31], ("ST",), ("pa_dram",), slow=True)

        if KDBG <= 2:
            return
        wb1, wb1k = wload(wview(w_in, 0, 8, 1536, 512), (8, 512))
        wb2, wb2k = wload(wview(w_in, 0, 8, 2048, 256), (8, 256))
        ps, pk = tm_linear(wb1, wb1k, 512)
        cp("act", T1[0:NS, :], ps[0:NS, :], (pk,), ("T1",))
        ps, pk = tm_linear(wb2, wb2k, 256)
        tt("dve", T2[0:NS, 0:256], ps[0:NS, 0:256], T1[0:NS, 0:256], ALU.mult, (pk, "T1"), ("T2",))
        dma("sp", T3[0:NS, :], state_b[l], (), ("T3",))
        WB = SCR[0:NS, 0:768]
        BB = SCR[0:NS, 768:1024]
        Y = SCR[0:NS, 1024:1280]
        TM_ = SCR[0:NS, 1280:1536]
        dma("sp", WB, W["b_conv_w"][l].rearrange("k c -> (k c)").partition_broadcast(NS), (), MGK)
        dma("sp", BB, W["b_conv_b"][l].partition_broadcast(NS), (), MGK)
        tt("dve", Y, T2[0:NS, 0:256], WB[:, 512:768], ALU.mult, ("T2",) + MGK, MGK)
        tt("dve", Y, Y, BB, ALU.add, MGK, MGK)
        tt("dve", TM_, T3[0:NS, 256:512], WB[:, 256:512], ALU.mult, ("T3",) + MGK, MGK)
        tt("dve", Y, Y, TM_, ALU.add, MGK, MGK)
        tt("dve", TM_, T3[0:NS, 0:256], WB[:, 0:256], ALU.mult, ("T3",) + MGK, MGK)
        tt("dve", Y, Y, TM_, ALU.add, MGK, MGK)
        tt("dve", TB[0:NS, 0:256], Y, T1[0:NS, 256:512], ALU.mult, ("T1",) + MGK, ("TB",))
        dma("sp", conv_b_sample[l][:, 0:256], T3[0:NS, 256:512], ("T3",), (), isout=True)
        dma("sp", conv_b_sample[l][:, 256:512], T2[0:NS, 0:256], ("T2",), (), isout=True)
        tps_to_fm(TB, "TB", 2, 4)
        branch_out(cx, l, 1, "w_b_out", 2, 4, True)

        if KDBG <= 3:
            return
        wc, wck = wload(wview(w_in, 0, 8, 2304, 512), (8, 512))
        ps, pk = tm_linear(wc, wck, 512)
        act(T1[0:NS, 256:512], ps[0:NS, 256:512], AF.Sigmoid, (pk,), ("T1",))
        UC = T2[0:NS, 256:512]
        tt("dve", UC, ps[0:NS, 0:256], T1[0:NS, 256:512], ALU.mult, (pk, "T1"), ("T2",))
        W30 = SCR[0:NS, 1536:1792]
        CB = SCR[0:NS, 1792:2048]
        YC = SCR[0:NS, 2048:2304]
        TC = SCR[0:NS, 2304:2560]
        dma("sp", W30, W["c_conv_w"][l][30].partition_broadcast(NS), (), MGK)
        dma("sp", CB, W["c_conv_b"][l].partition_broadcast(NS), (), MGK)
        tt("dve", YC, UC, W30, ALU.mult, ("T2",) + MGK, MGK)
        tt("dve", YC, YC, CB, ALU.add, MGK, MGK)
        WCK = SCR[0:NS, 4096:4096 + 1536]
        PCK = SCR[0:NS, 6144:6144 + 1536]
        for k5 in range(5):
            dma("sp", WCK, W["c_conv_w"][l][k5 * 6:(k5 + 1) * 6].rearrange("k c -> (k c)").partition_broadcast(NS), (), FK)
            dma("sp", PCK, state_c[l][:, k5 * 1536:(k5 + 1) * 1536], (), FK)
            tt("dve", PCK, PCK, WCK, ALU.mult, FK, FK)
            red(TC, PCK.rearrange("p (k c) -> p c k", k=6), FK, MGK)
            tt("dve", YC, YC, TC, ALU.add, MGK, MGK)
        dma("sp", conv_c_sample[l][:, 0:29 * 256], state_c[l][:, 256:30 * 256], (), (), isout=True)
        dma("sp", conv_c_sample[l][:, 29 * 256:30 * 256], UC, ("T2",), (), isout=True)
        LG = SCR[0:NS, 2560:2816]
        LB = SCR[0:NS, 2816:3072]
        dma("sp", LG, W["c_ln_g"][l].partition_broadcast(NS), (), MGK)
        dma("sp", LB, W["c_ln_b"][l].partition_broadcast(NS), (), MGK)
        T.op("dve", lambda e: e.bn_stats(ST[0:NS, 16:22], YC), r=MGK, w=("ST",))
        T.op("dve", lambda e: e.bn_aggr(ST[0:NS, 22:24], ST[0:NS, 16:22]), r=("ST",), w=("ST",))
        rstd(ST[0:NS, 23:24], ST[0:NS, 23:24], 1.0, ("ST",), ("ST",))
        tsc("dve", YC, YC, ST[0:NS, 22:23], ST[0:NS, 23:24], ALU.subtract, ALU.mult, MGK + ("ST",), MGK)
        tt("dve", YC, YC, LG, ALU.mult, MGK, MGK)
        tt("dve", YC, YC, LB, ALU.add, MGK, MGK)
        act(TB[0:NS, 0:256], YC, AF.Silu, MGK, ("TB",))
        tps_to_fm(TB, "TB", 2, 6)
        branch_out(cx, l, 2, "w_c_out", 2, 6, False)

        if KDBG <= 4:
            return
        wd, wdk = wload(wview(w_in, 0, 8, 2816, 512), (8, 512))
        ps, pk = tm_linear(wd, wdk, 512)
        gelu_from_psum(ps[0:NS, :], T1[0:NS, :], T3[0:NS, :], "T3", pk, "T1")
        T.op("dve", lambda e: e.bn_stats(ST[0:NS, 16:22], T1[0:NS, 256:512]), r=("T1",), w=("ST",))
        T.op("dve", lambda e: e.bn_aggr(ST[0:NS, 22:24], ST[0:NS, 16:22]), r=("ST",), w=("ST",))
        rstd(ST[0:NS, 23:24], ST[0:NS, 23:24], 1.0, ("ST",), ("ST",))
        DV = T3[0:NS, 0:256]
        DG = T3[0:NS, 256:512]
        tsc("dve", DV, T1[0:NS, 256:512], ST[0:NS, 22:23], ST[0:NS, 23:24], ALU.subtract, ALU.mult, ("T1", "ST"), ("T3",))
        tt("dve", DV, DV, dlg_b[0:NS, :], ALU.mult, ("T3", "dlg_b"), ("T3",))
        tt("dve", DV, DV, dlb_b[0:NS, :], ALU.add, ("T3", "dlb_b"), ("T3",))
        dma("sp", d_v_sample[l], DV, ("T3",), (), isout=True)
        DV3 = DV.rearrange("p (g c) -> p g c", g=4)
        DG3 = DG.rearrange("p (g c) -> p g c", g=4)
        tt("dve", DG3, DV3, w00[0:NS, 0:4, None].to_broadcast([NS, 4, 64]), ALU.mult, ("T3", "w00"), ("T3",))
        tt("dve", DG3, DG3, w00[0:NS, 4:8, None].to_broadcast([NS, 4, 64]), ALU.add, ("T3", "w00"), ("T3",))
        tt("dve", TB[0:NS, 0:256], DG, T1[0:NS, 0:256], ALU.mult, ("T3", "T1"), ("TB",))
        tps_to_fm(TB, "TB", 2, 0)
        branch_out(cx, l, 3, "w_d_out", 2, 0, False)

        if KDBG <= 5:
            return
        ARd = qT[:].rearrange("p a b -> p (a b)").bitcast(F32)[0:NS, :].rearrange("p (a d) -> p a d", a=8)
        aro = pa_dram.rearrange("(b a) w -> b a w", a=8)
        for hc in range(8):
            h = hc // 2
            dma("sp", ARd[:, hc, :], aro[:, hc, h * 128:(h + 1) * 128], ("pa_dram",), ("qT",))
        dma("sp", ST[0:NS, 32:40], aro[:, :, 512], ("pa_dram",), ("ST",), slow=True)
        tt("dve", T2[0:NS, :], q_s, k_s, ALU.mult, (kQ,), ("T2",))
        red(ST[0:NS, 40:48], T2[0:NS, :].rearrange("p (a d) -> p a d", a=8), ("T2",), ("ST",))
        act(ST[0:NS, 40:48], ST[0:NS, 40:48], AF.Exp, ("ST",), ("ST",))
        tt("dve", ST[0:NS, 32:40], ST[0:NS, 32:40], ST[0:NS, 40:48], ALU.add, ("ST",), ("ST",))
        for hc in range(8):
            h = hc // 2
            stt("dve", ARd[:, hc, :], v_s[:, h * 128:(h + 1) * 128], ST[0:NS, 40 + hc:41 + hc], ARd[:, hc, :], ALU.mult, ALU.add,
                (kV, "ST", "qT"), ("qT",))
        recip(ST[0:NS, 32:40], ST[0:NS, 32:40], ("ST",), ("ST",))
        for h in range(4):
            tt("dve", ST[0:NS, 33 + 2 * h:34 + 2 * h], ST[0:NS, 33 + 2 * h:34 + 2 * h], lam_s[0:NS, 4:5], ALU.mult, ("ST", "lam_s"), ("ST",))
            oh = T2[0:NS, h * 128:(h + 1) * 128]
            tsc("dve", oh, ARd[:, 2 * h, :], ST[0:NS, 32 + 2 * h:33 + 2 * h], None, ALU.mult, None, ("qT", "ST"), ("T2",))
            stt("dve", oh, ARd[:, 2 * h + 1, :], ST[0:NS, 33 + 2 * h:34 + 2 * h], oh, ALU.mult, ALU.add, ("qT", "ST", "T2"), ("T2",))
        if KDBG <= 5.2:
            return
        v3 = qknorm_tm(T2[0:NS, :], "T2", NS, 4, 128, T3[0:NS, :], "T3", 48)
        tt("dve", TB[0:NS, :].rearrange("p (h d) -> p h d", h=4), v3, gsub_b[0:NS, None, :].to_broadcast([NS, 4, 128]),
           ALU.mult, ("T2", "gsub_b"), ("TB",))
        if KDBG <= 5.4:
            return
        tps_to_fm(TB, "TB", 4, 0)
        if KDBG <= 5.5:
            return
        branch_out(cx, l, 0, "w_a_out", 4, 0, False)
        if KDBG <= 5.7:
            return
        merge_proj(cx, l)

        if KDBG <= 6:
            return
        rmsnorm_fm(cx, g_x, "g_x")
        wxq, wxqk = wload(wview(W["w_xq"][l], 0, 8, 0, 512), (8, 512))
        ps, pk = tm_linear(wxq, wxqk, 512)
        cp("act", T1[0:NS, :], ps[0:NS, :], (pk,), ("T1",))
        v3 = qknorm_tm(T1[0:NS, :], "T1", NS, 4, 128, T3[0:NS, :], "T3", 0)
        GX = SCR[0:NS, 3072:3200]
        dma("sp", GX, W["x_qnorm_g"][l].partition_broadcast(NS), (), MGK)
        tt("dve", v3, v3, GX[:, None, :].to_broadcast([NS, 4, 128]), ALU.mult, ("T1",) + MGK, ("T1",))
        dma("sp", qx_dram, T1[0:NS, :], ("T1",), ("qx_dram",))
        MK = SCR[:, 4096:5120]
        MV = SCR[:, 5120:6144]
        for b in range(NS):
            dma("sp", T2[:], qx_dram[b].partition_broadcast(128), ("qx_dram",), ("T2",))
            dma("sp", MK.rearrange("p (k d) -> p k d", k=2), cmk[l, b].rearrange("(k p) d -> p k d", p=128), (), ("FA",))
            dma("sp", MV.rearrange("p (k d) -> p k d", k=2), cmv[l, b].rearrange("(k p) d -> p k d", p=128), (), ("FB",))
            M3 = MK.rearrange("p (k d) -> p k d", k=2)
            tt("dve", M3, M3, T2[:, None, :].to_broadcast([128, 2, 512]), ALU.mult, ("FA", "T2"), ("FA",))
            red(Pc[:, b, :], MK.rearrange("p (a d) -> p a d", d=128), ("FA",), ("Pc",))
            act(Pc[:, b, :], Pc[:, b, :], AF.Exp, ("Pc",), ("Pc",), scale=128.0 ** -0.5)
            po, pok = bank()
            for kc in range(2):
                mm(po[0:4, :], Pc[:, b, kc * 4:(kc + 1) * 4], MV[:, kc * 512:(kc + 1) * 512], kc == 0, kc == 1, ("Pc", "FB"), (pok,))
            cp("act", T3[0:4, :], po[0:4, :], (pok,), ("T3",))
            dma("sp", om_dram[b], T3[0:4, :], ("T3",), ("om_dram",))
        pl, plk = bank()
        mm(pl[0:1, 0:NS * 8], ones_f[:, 0:1], Pc[:].rearrange("p a b -> p (a b)"), True, True, ("ones_f", "Pc"), (plk,))
        cp("act", T3[0:1, 0:NS * 8], pl[0:1, 0:NS * 8], (plk,), ("T3",))
        dma("sp", ol_dram, T3[0:1, 0:NS * 8], ("T3",), ("ol_dram",))
        for h in range(4):
            dma("sp", T1[0:NS, h * 128:(h + 1) * 128], om_dram[:, h, h * 128:(h + 1) * 128], ("om_dram",), ("T1",))
        dma("sp", ST[0:NS, 48:56], ol_dram.rearrange("o (b a) -> (o b) a", a=8), ("ol_dram",), ("ST",))
        tt("dve", ST[0:NS, 48:52], ST[0:NS, 48:52], ST[0:NS, 52:56], ALU.add, ("ST",), ("ST",))
        recip(ST[0:NS, 48:52], ST[0:NS, 48:52], ("ST",), ("ST",))
        tt("dve", TB[0:NS, :].rearrange("p (h d) -> p h d", h=4), T1[0:NS, :].rearrange("p (h d) -> p h d", h=4),
           ST[0:NS, 48:52, None].to_broadcast([NS, 4, 128]), ALU.mult, ("T1", "ST"), ("TB",))
        tps_to_fm(TB, "TB", 4, 0)
        xo_proj(cx, l)
        ffn(cx, l, S1[:, 0:2, :].rearrange("p a b -> p (a b)"), lambda j: "S1")
        if l == L - 1:
            for half, dst, dk in ((0, T1, "T1"), (1, T2, "T2")):
                ps, pk = bank()
                for j in range(4):
                    tp(ps[0:NS, j * 128:(j + 1) * 128], xsT[:, half * 4 + j, 0:NS], identf[:], ("xsT", "identf"), (pk,))
                cp("act", dst[0:NS, :], ps[0:NS, :], (pk,), (dk,))
                dma("sp", y_sample[:, half * 512:(half + 1) * 512], dst[0:NS, :], (dk,), (), isout=True)

    for l in range(L):
        T.new_epoch()
        load_layer_params(l)
        mem_layer(l)
        if with_sample:
            sample_layer(l)
        for i in range(NT):
            T.new_epoch()
            tile_layer(l, i)
    T.finish()
    with nc.Block() as block:
        T.emit(block)
    stack.close()
    return nc


_CACHE = {}


def _consts():
    ident = np.eye(128, dtype=np.float32)
    kk = np.arange(128)[:, None]
    qq = np.arange(128)[None, :]
    mask = np.where(kk > qq, -1e30, 0.0).astype(np.float32)
    tril = (qq <= kk).astype(np.float32)
    slopes = np.array([2.0 ** (-8.0 * (i + 1) / 4) for i in range(4)], dtype=np.float64)
    p = np.arange(128)[:, None, None]
    dl = np.arange(16)[None, None, :]
    alibi = (slopes[None, :, None] * (p - 128.0 * dl)).astype(np.float32).reshape(128, 64)
    return dict(c_ident=ident, c_mask=mask, c_tril=tril, c_alibi=alibi), slopes


WEIGHT_KEYS = ["norm_mix_g", "w_in", "a_qnorm_g", "a_knorm_g", "a_lam", "a_subln_g", "w_a_out", "b_conv_w",
               "b_conv_b", "w_b_out", "c_conv_w", "c_conv_b", "c_ln_g", "c_ln_b", "w_c_out", "d_ln_g", "d_ln_b",
               "d_ws", "d_bs", "w_d_out", "w_o", "norm_x_g", "mem_norm_g", "w_xq", "w_xk", "w_xv", "x_qnorm_g",
               "x_knorm_g", "w_xo", "norm_ffn_g", "w_up", "w_down"]


def kernel(**inp):
    x_prompt = np.asarray(inp["x_prompt"])
    B, SEQ, _ = x_prompt.shape
    L = np.asarray(inp["w_in"]).shape[0]
    page_table = np.ascontiguousarray(np.asarray(inp["page_table"]).astype(np.int32))
    NSG, NPG = page_table.shape
    NSL = NSG // NCORES
    cache_k = np.asarray(inp["cache_k_a"])
    cache_v = np.asarray(inp["cache_v_a"])
    NPOOL, PAGE_TOK = cache_k.shape[1], cache_k.shape[2]
    NCHK = PAGE_TOK // 8
    past_len = NPG * PAGE_TOK
    key = (SEQ, NPG, NPOOL, L, NSL, PAGE_TOK)
    if key not in _CACHE:
        _CACHE[key] = build(SEQ, NPG, NPOOL, L, NSL, PAGE_TOK)
    nc = _CACHE[key]
    consts, slopes = _consts()
    shared = {k: np.asarray(inp[k]) for k in WEIGHT_KEYS}
    shared["ck"] = np.ascontiguousarray(
        cache_k.reshape(L, NPOOL, NCHK, 4096).transpose(0, 2, 1, 3)).reshape(L * NCHK * NPOOL, 4096)
    shared["cv"] = np.ascontiguousarray(
        cache_v.reshape(L, NPOOL, NCHK, 4096).transpose(0, 2, 1, 3)).reshape(L * NCHK * NPOOL, 4096)
    BPG = max(1, min(NSL, 128 // NPG))
    j = (np.arange(128) % NPG)[:, None, None]
    t = np.arange(PAGE_TOK)[None, :, None]
    dist = past_len - (PAGE_TOK * j + t)
    shared["c_sbias"] = (-slopes[None, None, :] * dist).astype(np.float32).reshape(128, PAGE_TOK * 4)
    x_sample = np.asarray(inp["x_sample"]).reshape(NSG, D)
    state_b = np.asarray(inp["state_conv_b"]).reshape(L, NSG, 512)
    state_c = np.asarray(inp["state_conv_c"]).reshape(L, NSG, 30 * 256)
    cmk = np.asarray(inp["cache_mem_k"]).reshape(L, NSG, 256, 512)
    cmv = np.asarray(inp["cache_mem_v"]).reshape(L, NSG, 256, 512)
    mem_prompt = np.asarray(inp["mem_prompt"])
    in_maps = []
    for c in range(NCORES):
        m = dict(consts)
        m.update(shared)
        sl = slice(c * NSL, (c + 1) * NSL)
        m["x_prompt"] = np.ascontiguousarray(x_prompt[c])
        m["mem_prompt"] = np.ascontiguousarray(mem_prompt[c])
        m["x_sample"] = np.ascontiguousarray(x_sample[sl])
        m["state_conv_b"] = np.ascontiguousarray(state_b[:, sl])
        m["state_conv_c"] = np.ascontiguousarray(state_c[:, sl])
        m["cache_mem_k"] = np.ascontiguousarray(cmk[:, sl])
        m["cache_mem_v"] = np.ascontiguousarray(cmv[:, sl])
        m["page_table"] = np.ascontiguousarray(page_table[sl])
        in_maps.append(m)
    if inp.get("_prepare_only"):
        return nc, in_maps
    res = run_bass_kernel_spmd(nc, in_maps, core_ids=list(range(NCORES))).results
    y_prompt = np.stack([r["y_prompt"] for r in res], 0)
    kp = np.stack([r["k_a_prompt"] for r in res], 1).reshape(L, B, SEQ, 4, 128)
    vp = np.stack([r["v_a_prompt"] for r in res], 1).reshape(L, B, SEQ, 4, 128)
    cbp = np.stack([r["conv_b_prompt"] for r in res], 1)
    ccp = np.stack([r["conv_c_prompt"] for r in res], 1)
    mkp = np.stack([r["mem_k_prompt"] for r in res], 1).reshape(L, B, 256, 4, 128)
    mvp = np.stack([r["mem_v_prompt"] for r in res], 1).reshape(L, B, 256, 4, 128)
    ys = np.concatenate([r["y_sample"] for r in res], 0).reshape(NSG, 1, D)
    ks = np.concatenate([r["k_a_sample"] for r in res], 1).reshape(L, NSG, 1, 4, 128)
    vs = np.concatenate([r["v_a_sample"] for r in res], 1).reshape(L, NSG, 1, 4, 128)
    cbs = np.concatenate([r["conv_b_sample"] for r in res], 1).reshape(L, NSG, 2, 256)
    ccs = np.concatenate([r["conv_c_sample"] for r in res], 1).reshape(L, NSG, 30, 256)
    dvs = np.concatenate([r["d_v_sample"] for r in res], 1).reshape(L, NSG, 1, 256)
    return (y_prompt, ys, kp, vp, cbp, ccp, mkp, mvp, ks, vs, cbs, ccs, dvs)
```

```python
import contextlib
import math
import numpy as np
import concourse.bass as bass
import concourse.mybir as mybir
from concourse.bass_utils import run_bass_kernel_spmd

F32 = mybir.dt.float32
BF16 = mybir.dt.bfloat16
I32 = mybir.dt.int32
AF = mybir.ActivationFunctionType
ALU = mybir.AluOpType
AX = mybir.AxisListType

D = 1024
NCH = 8
INC = 7424
DFF = 4096
EPS = 1e-6
NCORES = 8
TT = 512
NEPOCH = 12
import os
KDBG = float(os.environ.get('KDBG', '99'))
SW_CLEAR = os.environ.get('SW_CLEAR', '0') == '1'


class _Op:
    __slots__ = ("eng", "fn", "deps", "sig", "kind", "sem", "val", "ep", "clr", "pre")


class Tr:
    def __init__(self, nc, stack, ndma=32):
        self.nc = nc
        self.names = ["pe", "act", "dve", "pool", "sp"]
        self.ops = {k: [] for k in self.names}
        self.lastw = {}
        self.readers = {}
        self.esem = {k: [stack.enter_context(nc.semaphore("s_%s%d" % (k, i))) for i in range(NEPOCH)]
                     for k in self.names if k != "sp"}
        self.esem["sp"] = [stack.enter_context(nc.semaphore("s_sp"))] * NEPOCH
        self.dsem = [stack.enter_context(nc.semaphore("d%d" % i)) for i in range(ndma)]
        self.dval = [0] * ndma
        self.dlast = [None] * ndma
        self.dnext = 0
        self.outs = []
        self.epoch = 0
        self.stack = stack
        self.bufsem = {}

    def new_epoch(self):
        self.epoch += 1
        assert self.epoch < NEPOCH, "too many epochs"

    def _mk(self, eng, fn, kind, r, w, extra=()):
        op = _Op()
        op.eng, op.fn, op.kind, op.sig, op.sem, op.val, op.ep = eng, fn, kind, False, None, 0, self.epoch
        op.clr, op.pre = False, None
        deps = list(extra)
        for k in r:
            lw = self.lastw.get(k)
            if lw is not None:
                deps.append(lw)
        for k in w:
            lw = self.lastw.get(k)
            if lw is not None:
                deps.append(lw)
            deps.extend(self.readers.get(k, {}).values())
        keep = []
        for d in deps:
            if d is op or d in keep:
                continue
            if d.kind == "c" and kind == "c" and d.eng == eng and eng == "pe":
                continue
            d.sig = True
            keep.append(d)
        op.deps = keep
        for k in w:
            self.lastw[k] = op
            self.readers[k] = {}
        for k in r:
            rk = (eng, id(op)) if kind == "d" else eng
            self.readers.setdefault(k, {})[rk] = op
        self.ops[eng].append(op)
        return op

    def op(self, eng, fn, r=(), w=()):
        return self._mk(eng, fn, "c", r, w)

    def dma(self, q, fn, r=(), w=(), out=False):
        k = self.dnext
        self.dnext = (self.dnext + 1) % len(self.dsem)
        extra = [self.dlast[k]] if self.dlast[k] is not None else []
        op = self._mk(q, fn, "d", r, w, extra)
        self.dval[k] += 16
        op.sem, op.val = self.dsem[k], self.dval[k]
        self.dlast[k] = op
        if out:
            self.outs.append(op)
        return op

    def dma_sw(self, fn, r, w, bufkey):
        if bufkey not in self.bufsem:
            i = len(self.bufsem)
            self.bufsem[bufkey] = [[self.stack.enter_context(self.nc.semaphore("b%d_%d" % (i, j))) for j in range(2)], 0]
        sems, n = self.bufsem[bufkey]
        self.bufsem[bufkey][1] = n + 1
        op = self._mk("pool", fn, "d", r, w)
        if SW_CLEAR:
            op.sem, op.val, op.clr = sems[n % 2], 16, True
            op.pre = sems[(n - 1) % 2] if n >= 1 else None
        else:
            op.sem, op.val = sems[0], 16 * (n + 1)
        return op

    def finish(self):
        return self._mk("sp", None, "c", (), (), list(self.outs))

    def emit(self, block):
        for k in self.names:
            c = {}
            for op in self.ops[k]:
                if op.kind == "c" and op.sig:
                    c[op.ep] = c.get(op.ep, 0) + 1
                    op.val = c[op.ep]
        tr = self

        def body(name):
            def run(e):
                known = {}
                for op in tr.ops[name]:
                    waits = {}
                    for d in op.deps:
                        sem = d.sem if d.kind == "d" else tr.esem[d.eng][d.ep]
                        key = id(sem)
                        if d.clr:
                            if known.get(key) is d:
                                continue
                            waits[key] = (sem, 16)
                            known[key] = d
                            continue
                        if known.get(key, 0) >= d.val:
                            continue
                        if key not in waits or waits[key][1] < d.val:
                            waits[key] = (sem, d.val)
                    for key, (sem, val) in waits.items():
                        e.wait_ge(sem, val)
                        if not isinstance(known.get(key), _Op):
                            known[key] = val
                    if op.pre is not None:
                        e.sem_clear(op.pre)
                    if op.fn is None:
                        continue
                    ins = op.fn(e)
                    if op.kind == "d":
                        ins.then_inc(op.sem, 16)
                    elif op.sig:
                        ins.then_inc(tr.esem[name][op.ep], 1)
            return run

        block.tensor(body("pe"))
        block.scalar(body("act"))
        block.vector(body("dve"))
        block.gpsimd(body("pool"))
        block.sync(body("sp"))


class Cx:
    pass


def build(SEQ, NPG, NPOOL, L=2, NS=4, PAGE_TOK=128, with_sample=True):
    NT = SEQ // TT
    NKT = SEQ // 128
    NCHK = PAGE_TOK // 4
    BPG = max(1, min(NS, 128 // NPG))
    NG = (NS + BPG - 1) // BPG
    NCOL = BPG * 8
    if with_sample:
        assert NS % BPG == 0 and (BPG == 1 or NPG % 32 == 0) and BPG * NPG <= 128
    nc = bass.Bass("TRN2", target_bir_lowering=False)
    stack = contextlib.ExitStack()

    def din(name, shape, dt=F32):
        return nc.dram_tensor(name, list(shape), dt, kind="ExternalInput").ap()

    def dout(name, shape, dt=F32):
        return nc.dram_tensor(name, list(shape), dt, kind="ExternalOutput").ap()

    x_prompt = din("x_prompt", [SEQ, D])
    mem_prompt = din("mem_prompt", [256, D])
    c_ident = din("c_ident", [128, 128])
    c_mask = din("c_mask", [128, 128])
    c_tril = din("c_tril", [128, 128])
    c_alibi = din("c_alibi", [128, 64])
    W = {}
    wshapes = dict(
        norm_mix_g=[L, D], w_in=[L, D, INC], a_qnorm_g=[L, 64], a_knorm_g=[L, 64], a_lam=[L, 4, 64],
        a_subln_g=[L, 128], w_a_out=[L, 512, D], b_conv_w=[L, 3, 256], b_conv_b=[L, 256],
        w_b_out=[L, 256, D], c_conv_w=[L, 31, 256], c_conv_b=[L, 256], c_ln_g=[L, 256], c_ln_b=[L, 256],
        w_c_out=[L, 256, D], d_ln_g=[L, 256], d_ln_b=[L, 256], d_ws=[L, 4, 128, 128], d_bs=[L, 4, 128],
        w_d_out=[L, 256, D], w_o=[L, D, D], norm_x_g=[L, D], mem_norm_g=[L, D], w_xq=[L, D, 512],
        w_xk=[L, D, 512], w_xv=[L, D, 512], x_qnorm_g=[L, 128], x_knorm_g=[L, 128], w_xo=[L, 512, D],
        norm_ffn_g=[L, D], w_up=[L, D, DFF], w_down=[L, DFF, D])
    for k, s in wshapes.items():
        W[k] = din(k, s)

    y_prompt = dout("y_prompt", [SEQ, D])
    k_a_prompt = dout("k_a_prompt", [L, SEQ, 512])
    v_a_prompt = dout("v_a_prompt", [L, SEQ, 512])
    conv_b_prompt = dout("conv_b_prompt", [L, 2, 256])
    conv_c_prompt = dout("conv_c_prompt", [L, 30, 256])
    mem_k_prompt = dout("mem_k_prompt", [L, 256, 512])
    mem_v_prompt = dout("mem_v_prompt", [L, 256, 512])
    xs_dram = nc.dram_tensor("xs_scratch", [128, NCH, SEQ], F32).ap()
    if with_sample:
        x_sample = din("x_sample", [NS, D])
        ck = din("ck", [L * NCHK * NPOOL, 2048])
        cv = din("cv", [L * NCHK * NPOOL, 2048])
        state_b = din("state_conv_b", [L, NS, 512])
        state_c = din("state_conv_c", [L, NS, 30 * 256])
        cmk = din("cache_mem_k", [L, NS, 256, 512])
        cmv = din("cache_mem_v", [L, NS, 256, 512])
        page_table = din("page_table", [NS, NPG], I32)
        c_sbias = din("c_sbias", [128, NCHK * 16])
        y_sample = dout("y_sample", [NS, D])
        k_a_sample = dout("k_a_sample", [L, NS, 512])
        v_a_sample = dout("v_a_sample", [L, NS, 512])
        conv_b_sample = dout("conv_b_sample", [L, NS, 512])
        conv_c_sample = dout("conv_c_sample", [L, NS, 30 * 256])
        d_v_sample = dout("d_v_sample", [L, NS, 256])
        qs_dram = nc.dram_tensor("qs_scratch", [NS, 512], F32).ap()
        qx_dram = nc.dram_tensor("qx_scratch", [NS, 512], F32).ap()
        om_dram = nc.dram_tensor("om_scratch", [NS, 4, 512], F32).ap()
        ol_dram = nc.dram_tensor("ol_scratch", [1, NS * 8], F32).ap()
        pa_dram = nc.dram_tensor("pa_scratch", [NG * NCOL, 516], F32).ap()

    T = Tr(nc, stack)

    def sb(name, shape, dt):
        return stack.enter_context(nc.sbuf_tensor(name, list(shape), dt))

    identf = sb("identf", [128, 128], F32)
    identb = sb("identb", [128, 128], BF16)
    maskf = sb("maskf", [128, 128], F32)
    maskb = sb("maskb", [128, 128], BF16)
    trilf = sb("trilf", [128, 128], F32)
    alibi = sb("alibi", [128, 64], F32)
    ones_dm = sb("ones_dm", [128, 128], BF16)
    ones_c = sb("ones_c", [128, 128], BF16)
    ones_h = sb("ones_h", [128, 128], BF16)
    ones_1 = sb("ones_1", [128, 128], BF16)
    ones_f = sb("ones_f", [128, 2], F32)
    g_mix = sb("g_mix", [128, NCH], F32)
    g_x = sb("g_x", [128, NCH], F32)
    g_ffn = sb("g_ffn", [128, NCH], F32)
    g_mem = sb("g_mem", [128, NCH], F32)
    gq_b = sb("gq_b", [128, 64], F32)
    gk_b = sb("gk_b", [128, 64], F32)
    gsub_b = sb("gsub_b", [128, 128], F32)
    lam_b = sb("lam_b", [128, 256], F32)
    lam_t = sb("lam_t", [128, 128], F32)
    lam_s = sb("lam_s", [128, 8], F32)
    bcw = sb("bcw", [128, 2, 3], F32)
    bcb = sb("bcb", [128, 2], F32)
    ccw = sb("ccw", [128, 2, 31], F32)
    ccb = sb("ccb", [128, 2], F32)
    clg = sb("clg", [128, 2], F32)
    clb = sb("clb", [128, 2], F32)
    dlg_b = sb("dlg_b", [128, 256], F32)
    dlb_b = sb("dlb_b", [128, 256], F32)
    bsT = sb("bsT", [128, 2, 128], F32)
    wsf = sb("wsf", [128, 4, 128], F32)
    wsb = sb("wsb", [128, 4, 128], BF16)
    WT = sb("WT", [128, 4, 128], BF16)
    gxq = sb("gxq", [128, 1], F32)
    gxk_b = sb("gxk_b", [128, 128], F32)
    mkT = sb("mkT", [128, 4, 256], BF16)
    mvb = sb("mvb", [128, 2, 512], BF16)
    memTg = sb("memTg", [128, NCH, 256], BF16)
    mrstd = sb("mrstd", [128, 2], F32)
    kT = sb("kT", [128, 4, SEQ], BF16)
    Vx = sb("Vx", [128, NKT, 4, 132], BF16)
    xT = sb("xT", [128, NCH, TT], F32)
    H = sb("H", [128, NCH, TT], BF16)
    S1 = sb("S1", [128, NCH, TT], BF16)
    RS = sb("RS", [128, TT], F32)
    SCR = sb("SCR", [128, 8448], F32)
    MG = SCR[:, 0:4096].rearrange("p (a b) -> p a b", a=8)
    FA = SCR[:, 4096:5184].rearrange("p (a b) -> p a b", a=2)
    FB = SCR[:, 5184:6272].rearrange("p (a b) -> p a b", a=2)
    FC = SCR[:, 6272:7296].rearrange("p (a b) -> p a b", a=2)
    FD = SCR[:, 7296:8320].rearrange("p (a b) -> p a b", a=2)
    MGK = tuple(("MG", f) for f in range(8))
    FK = ("FA", "FB", "FC", "FD")
    T1 = sb("T1", [128, 512], F32)
    T2 = sb("T2", [128, 512], F32)
    T3 = sb("T3", [128, 512], F32)
    TB = sb("TB", [128, 512], BF16)
    qT = sb("qT", [128, 4, TT], BF16)
    PT = [sb("PT%d" % i, [128, 16, 128], BF16) for i in range(2)]
    SG = [sb("SG%d" % i, [128, TT], F32) for i in range(2)]
    ST = sb("ST", [128, 64], F32)
    NSLOT = 4
    WS = [sb("WS%d" % i, [128, 4096], BF16) for i in range(NSLOT)]
    PS = [stack.enter_context(nc.psum_tensor("ps%d" % i, [128, 512], F32)) for i in range(8)]
    if with_sample:
        xsT = sb("xsT", [128, NCH, NS], F32)
        Hs = sb("Hs", [128, NCH, NS], BF16)
        S1s = sb("S1s", [128, NCH, NS], BF16)
        MGs = sb("MGs", [128, NCH, NS], F32)
        RSs = sb("RSs", [128, NS], F32)
        SGs = [sb("SGs%d" % i, [128, NS], F32) for i in range(2)]
        Ssc = sb("Ssc", [128, 4, 8], F32)
        Pz = sb("Pz", [128, 4, NCOL], F32)
        Pzs = sb("Pzs", [128, NCOL], F32)
        PTI = sb("PTI", [128, NG], I32)
        sbias = sb("sbias", [128, NCHK * 16], F32)
        Pc = sb("Pc", [128, NS, 8], F32)
        w00 = sb("w00", [128, 8], F32)

    st = dict(bank=0, slot=0, pt=0, sg=0, sgs=0)

    def bank():
        b = st["bank"]
        st["bank"] = (b + 1) % 8
        return PS[b], ("ps", b)

    CP = Cx()
    CP.xT, CP.H, CP.S1, CP.MG, CP.RS, CP.N = xT, H, S1, MG, RS, TT
    CP.kx, CP.kh, CP.ks1, CP.krs, CP.kmg = "xT", "H", "S1", "RS", "MG"
    CP.SG, CP.sgk, CP.sgi = SG, "SG", "sg"
    if with_sample:
        CS = Cx()
        CS.xT, CS.H, CS.S1, CS.MG, CS.RS, CS.N = xsT, Hs, S1s, MGs, RSs, NS
        CS.kx, CS.kh, CS.ks1, CS.krs, CS.kmg = "xsT", "Hs", "S1s", "RSs", "MGs"
        CS.SG, CS.sgk, CS.sgi = SGs, "SGs", "sgs"

    def sgbuf(cx):
        i = st[cx.sgi]
        st[cx.sgi] = 1 - i
        return cx.SG[i], (cx.sgk, i)

    def mm(out, lhsT, rhs, start, stop, r, w):
        T.op("pe", lambda e, o=out, a=lhsT, b=rhs, s0=start, s1=stop: e.matmul(o, a, b, start=s0, stop=s1), r=r, w=w)

    def tp(out, in_, ident, r, w):
        T.op("pe", lambda e, o=out, a=in_, i=ident: e.transpose(o, a, i), r=r, w=w)

    def act(out, in_, func, r, w, bias=None, scale=None):
        def f(e, o=out, a=in_, fn=func, b=bias, s=scale):
            kw = {}
            if b is not None:
                kw["bias"] = b
            if s is not None:
                kw["scale"] = s
            return e.activation(o, a, fn, **kw)
        T.op("act", f, r=r, w=w)

    def tsc(eng, out, in0, s1, s2, op0, op1, r, w):
        if op1 is None:
            T.op(eng, lambda e, o=out, a=in0, x=s1, p=op0: e.tensor_scalar(o, a, x, 0.0, p, ALU.add), r=r, w=w)
        else:
            T.op(eng, lambda e, o=out, a=in0, x=s1, y=s2, p=op0, q=op1: e.tensor_scalar(o, a, x, y, p, q), r=r, w=w)

    def rstd(out, in_, scale, r, w):
        act(out, in_, AF.Sqrt, r, w, bias=EPS, scale=scale)
        T.op("dve", lambda e, o=out: e.reciprocal(o, o), r=w, w=w)

    def recip(out, in_, r, w):
        T.op("dve", lambda e, o=out, a=in_: e.reciprocal(o, a), r=r, w=w)

    def red(out, in_, r, w):
        T.op("dve", lambda e, o=out, a=in_: e.tensor_reduce(o, a, AX.X, ALU.add), r=r, w=w)

    def stt(eng, out, in0, sc, in1, op0, op1, r, w):
        T.op(eng, lambda e, o=out, a=in0, s=sc, b=in1, p=op0, q=op1: e.scalar_tensor_tensor(o, a, s, b, p, q), r=r, w=w)

    def tt(eng, out, in0, in1, op, r, w):
        T.op(eng, lambda e, o=out, a=in0, b=in1, p=op: e.tensor_tensor(o, a, b, p), r=r, w=w)

    def cp(eng, out, in_, r, w):
        if eng == "act":
            T.op("act", lambda e, o=out, a=in_: e.copy(o, a), r=r, w=w)
        else:
            T.op(eng, lambda e, o=out, a=in_: e.tensor_copy(o, a), r=r, w=w)

    def dma(q, out, in_, r, w, isout=False, slow=False):
        if slow:
            T.dma(q, lambda e, o=out, a=in_: e.dma_start(out=o, in_=a, allow_slow_non_contiguous=True), r=r, w=w, out=isout)
        else:
            T.dma(q, lambda e, o=out, a=in_: e.dma_start(out=o, in_=a), r=r, w=w, out=isout)

    def wload(src_ap, shape3):
        i = st["slot"]
        st["slot"] = (i + 1) % NSLOT
        a, b = shape3
        dst = WS[i][:, 0:a * b].rearrange("p (a b) -> p a b", a=a)
        T.dma_sw(lambda e, o=dst, s=src_ap: e.dma_start(out=o, in_=s), (), (("WS", i),), ("WS", i))
        return dst, ("WS", i)

    def wview(w2d, r0, nr, c0, ncol):
        return w2d[r0 * 128:(r0 + nr) * 128, c0:c0 + ncol].rearrange("(k p) n -> p k n", p=128)

    dma("sp", identf[:], c_ident, (), ("identf",))
    dma("sp", maskf[:], c_mask, (), ("maskf",))
    dma("sp", trilf[:], c_tril, (), ("trilf",))
    dma("sp", alibi[:], c_alibi, (), ("alibi",))
    cp("dve", identb[:], identf[:], ("identf",), ("identb",))
    cp("dve", maskb[:], maskf[:], ("maskf",), ("maskb",))
    T.op("pool", lambda e: e.memset(ones_dm[:], 1.0 / D), w=("ones_dm",))
    T.op("pool", lambda e: e.memset(ones_c[:], 1.0 / 256), w=("ones_c",))
    T.op("pool", lambda e: e.memset(ones_h[:], 1.0 / 128), w=("ones_h",))
    T.op("pool", lambda e: e.memset(ones_1[:], 1.0), w=("ones_1",))
    T.op("pool", lambda e: e.memset(ones_f[:], 1.0), w=("ones_f",))
    T.op("pool", lambda e: e.memset(Vx[:], 1.0), w=("Vx",))
    if with_sample:
        T.op("pool", lambda e: e.memset(Pz[:], 0.0), w=("Pz",))
        dma("sp", sbias[:], c_sbias, (), ("sbias",))
        T.op("pool", lambda e: e.memset(PTI[:], 0), w=("PTI",))
        for g in range(NG):
            dma("sp", PTI[0:BPG * NPG, g:g + 1], page_table[g * BPG:(g + 1) * BPG, :].rearrange("b (j o) -> (b j) o", o=1),
                (), ("PTI",), slow=True)

    def rmsnorm_fm(cx, gt, gkey):
        N = cx.N
        act(cx.S1[:, :, 0:N], cx.xT[:, :, 0:N], AF.Square, (cx.kx,), (cx.ks1,))
        ps, pk = bank()
        for c in range(NCH):
            mm(ps[:, 0:N], ones_dm[:], cx.S1[:, c, 0:N], c == 0, c == NCH - 1, ("ones_dm", cx.ks1), (pk,))
        rstd(cx.RS[:, 0:N], ps[:, 0:N], 1.0, (pk,), (cx.krs,))
        for c in range(NCH):
            stt("dve", cx.H[:, c, 0:N], cx.xT[:, c, 0:N], gt[:, c:c + 1], cx.RS[:, 0:N], ALU.mult, ALU.mult,
                (cx.kx, cx.krs, gkey), (cx.kh,))

    def load_layer_params(l):
        def fm(dst, src, key):
            dma("sp", dst[:], src.rearrange("(c p) -> p c", p=128), (), (key,), slow=True)
        fm(g_mix, W["norm_mix_g"][l], "g_mix")
        fm(g_x, W["norm_x_g"][l], "g_x")
        fm(g_ffn, W["norm_ffn_g"][l], "g_ffn")
        fm(g_mem, W["mem_norm_g"][l], "g_mem")
        fm(bcb, W["b_conv_b"][l], "bcb")
        fm(ccb, W["c_conv_b"][l], "ccb")
        fm(clg, W["c_ln_g"][l], "clg")
        fm(clb, W["c_ln_b"][l], "clb")
        for cc in range(2):
            dma("sp", bcw[:, cc, :], W["b_conv_w"][l][:, cc * 128:(cc + 1) * 128].rearrange("k p -> p k"), (), ("bcw",), slow=True)
            dma("sp", ccw[:, cc, :], W["c_conv_w"][l][:, cc * 128:(cc + 1) * 128].rearrange("k p -> p k"), (), ("ccw",), slow=True)
        dma("sp", gxq[:], W["x_qnorm_g"][l].rearrange("(p o) -> p o", o=1), (), ("gxq",), slow=True)

        def bc(dst, src, key):
            dma("sp", dst[:], src.partition_broadcast(128), (), (key,))
        bc(gq_b, W["a_qnorm_g"][l], "gq_b")
        bc(gk_b, W["a_knorm_g"][l], "gk_b")
        bc(gsub_b, W["a_subln_g"][l], "gsub_b")
        bc(lam_b, W["a_lam"][l].rearrange("a b -> (a b)"), "lam_b")
        bc(dlg_b, W["d_ln_g"][l], "dlg_b")
        bc(dlb_b, W["d_ln_b"][l], "dlb_b")
        bc(gxk_b, W["x_knorm_g"][l], "gxk_b")
        for g in range(4):
            cc, gg = g // 2, g % 2
            dma("sp", bsT[gg * 64:(gg + 1) * 64, cc, :], W["d_bs"][l, g].partition_broadcast(64), (), ("bsT",))
        dma("sp", wsf[:], W["d_ws"][l].rearrange("g t s -> t g s"), (), ("wsf",))
        if with_sample:
            dma("sp", w00[:, 0:4], W["d_ws"][l][:, 0, 0].partition_broadcast(128), (), ("w00",), slow=True)
            dma("sp", w00[:, 4:8], W["d_bs"][l][:, 0].partition_broadcast(128), (), ("w00",), slow=True)
        lam_init = 0.8 - 0.6 * math.exp(-0.3 * l)
        tsc("dve", gq_b[:], gq_b[:], 0.125, None, ALU.mult, None, ("gq_b",), ("gq_b",))
        tsc("dve", gsub_b[:], gsub_b[:], 1.0 - lam_init, None, ALU.mult, None, ("gsub_b",), ("gsub_b",))
        tt("dve", lam_t[:, 0:64], lam_b[:, 0:64], lam_b[:, 64:128], ALU.mult, ("lam_b",), ("lam_t",))
        tt("dve", lam_t[:, 64:128], lam_b[:, 128:192], lam_b[:, 192:256], ALU.mult, ("lam_b",), ("lam_t",))
        red(lam_s[:, 0:2], lam_t[:, 0:128].rearrange("p (a b) -> p a b", a=2), ("lam_t",), ("lam_s",))
        act(lam_s[:, 2:4], lam_s[:, 0:2], AF.Exp, ("lam_s",), ("lam_s",))
        tt("dve", lam_s[:, 4:5], lam_s[:, 3:4], lam_s[:, 2:3], ALU.subtract, ("lam_s",), ("lam_s",))
        tsc("dve", lam_s[:, 4:5], lam_s[:, 4:5], -lam_init, None, ALU.add, None, ("lam_s",), ("lam_s",))
        tt("dve", wsb[:], wsf[:], trilf[:, None, :].to_broadcast([128, 4, 128]), ALU.mult, ("wsf", "trilf"), ("wsb",))
        ps, pk = bank()
        psb = ps[:].bitcast(BF16)
        for g in range(4):
            tp(psb[:, g * 128:(g + 1) * 128], wsb[:, g, :], identb[:], ("wsb", "identb"), (pk,))
        cp("dve", WT[:].rearrange("p g t -> p (g t)"), psb[:, 0:512], (pk,), ("WT",))

    def mem_layer(l):
        for tcn in range(2):
            dma("sp", T1[:], mem_prompt[tcn * 128:(tcn + 1) * 128, 0:512], (), ("T1",))
            dma("sp", T2[:], mem_prompt[tcn * 128:(tcn + 1) * 128, 512:1024], (), ("T2",))
            act(T3[:], T1[:], AF.Square, ("T1",), ("T3",))
            red(ST[:, 0:1], T3[:], ("T3",), ("ST",))
            act(T3[:], T2[:], AF.Square, ("T2",), ("T3",))
            red(ST[:, 1:2], T3[:], ("T3",), ("ST",))
            tt("dve", ST[:, 2:3], ST[:, 0:1], ST[:, 1:2], ALU.add, ("ST",), ("ST",))
            rstd(mrstd[:, tcn:tcn + 1], ST[:, 2:3], 1.0 / D, ("ST",), ("mrstd",))
            for half, src, sk in ((0, T1, "T1"), (1, T2, "T2")):
                cp("dve", TB[:], src[:], (sk,), ("TB",))
                ps, pk = bank()
                psb = ps[:].bitcast(BF16)
                for j in range(4):
                    tp(psb[:, j * 128:(j + 1) * 128], TB[:, j * 128:(j + 1) * 128], identb[:], ("TB", "identb"), (pk,))
                for j in range(4):
                    c = half * 4 + j
                    tsc("dve", memTg[:, c, tcn * 128:(tcn + 1) * 128], psb[:, j * 128:(j + 1) * 128],
                        g_mem[:, c:c + 1], None, ALU.mult, None, (pk, "g_mem"), ("memTg",))
        wk, wkk = wload(wview(W["w_xk"][l], 0, 8, 0, 512), (8, 512))
        wv, wvk = wload(wview(W["w_xv"][l], 0, 8, 0, 512), (8, 512))
        for tcn in range(2):
            ps, pk = bank()
            for c in range(NCH):
                mm(ps[:], memTg[:, c, tcn * 128:(tcn + 1) * 128], wk[:, c, :], c == 0, c == NCH - 1, ("memTg", wkk), (pk,))
            act(T1[:], ps[:], AF.Copy, (pk, "mrstd"), ("T1",), scale=mrstd[:, tcn:tcn + 1])
            tt("dve", T3[:], T1[:], T1[:], ALU.mult, ("T1",), ("T3",))
            red(ST[:, 0:4], T3[:].rearrange("p (h d) -> p h d", h=4), ("T3",), ("ST",))
            rstd(ST[:, 0:4], ST[:, 0:4], 1.0 / 128, ("ST",), ("ST",))
            v3 = T1[:].rearrange("p (h d) -> p h d", h=4)
            tt("dve", v3, v3, ST[:, 0:4, None].to_broadcast([128, 4, 128]), ALU.mult, ("T1", "ST"), ("T1",))
            tt("dve", v3, v3, gxk_b[:, None, :].to_broadcast([128, 4, 128]), ALU.mult, ("T1", "gxk_b"), ("T1",))
            dma("sp", mem_k_prompt[l, tcn * 128:(tcn + 1) * 128, :], T1[:], ("T1",), (), isout=True)
            cp("dve", TB[:], T1[:], ("T1",), ("TB",))
            ps2, pk2 = bank()
            psb = ps2[:].bitcast(BF16)
            for h in range(4):
                tp(psb[:, h * 128:(h + 1) * 128], TB[:, h * 128:(h + 1) * 128], identb[:], ("TB", "identb"), (pk2,))
            for h in range(4):
                cp("dve", mkT[:, h, tcn * 128:(tcn + 1) * 128], psb[:, h * 128:(h + 1) * 128], (pk2,), ("mkT",))
            ps, pk = bank()
            for c in range(NCH):
                mm(ps[:], memTg[:, c, tcn * 128:(tcn + 1) * 128], wv[:, c, :], c == 0, c == NCH - 1, ("memTg", wvk), (pk,))
            act(T2[:], ps[:], AF.Copy, (pk, "mrstd"), ("T2",), scale=mrstd[:, tcn:tcn + 1])
            dma("sp", mem_v_prompt[l, tcn * 128:(tcn + 1) * 128, :], T2[:], ("T2",), (), isout=True)
            cp("dve", mvb[:, tcn, :], T2[:], ("T2",), ("mvb",))

    def load_x_tile(l, i):
        if l == 0:
            for s in range(4):
                r0 = i * TT + s * 128
                dma("sp", T1[:], x_prompt[r0:r0 + 128, 0:512], (), ("T1",))
                dma("sp", T2[:], x_prompt[r0:r0 + 128, 512:1024], (), ("T2",))
                for half, src, sk in ((0, T1, "T1"), (1, T2, "T2")):
                    ps, pk = bank()
                    for j in range(4):
                        tp(ps[:, j * 128:(j + 1) * 128], src[:, j * 128:(j + 1) * 128], identf[:], (sk, "identf"), (pk,))
                    for j in range(4):
                        cp("act", xT[:, half * 4 + j, s * 128:(s + 1) * 128], ps[:, j * 128:(j + 1) * 128], (pk,), ("xT",))
        else:
            dma("sp", xT[:], xs_dram[:, :, i * TT:(i + 1) * TT], (("xs", i),), ("xT",))

    def store_x_tile(l, i):
        if l < L - 1:
            dma("sp", xs_dram[:, :, i * TT:(i + 1) * TT], xT[:], ("xT",), (("xs", i),))
        else:
            for s in range(4):
                r0 = i * TT + s * 128
                for half, dst, dk in ((0, T1, "T1"), (1, T2, "T2")):
                    ps, pk = bank()
                    for j in range(4):
                        tp(ps[:, j * 128:(j + 1) * 128], xT[:, half * 4 + j, s * 128:(s + 1) * 128], identf[:],
                           ("xT", "identf"), (pk,))
                    cp("act", dst[:], ps[:], (pk,), (dk,))
                    dma("sp", y_prompt[r0:r0 + 128, half * 512:(half + 1) * 512], dst[:], (dk,), (), isout=True)

    def branch_out(cx, l, bi, wout_name, nk, s1_base, first):
        N = cx.N
        for half in range(2):
            gcol = 3328 + bi * D + half * 512
            wg, wgk = wload(wview(W["w_in"][l], 0, 8, gcol, 512), (8, 512))
            wo, wok = wload(wview(W[wout_name][l], 0, nk, half * 512, 512), (nk, 512))
            for fl in range(4):
                f = half * 4 + fl
                psy, pky = bank()
                for kc in range(nk):
                    mm(psy[:, 0:N], wo[:, kc, fl * 128:(fl + 1) * 128], cx.S1[:, s1_base + kc, 0:N], kc == 0, kc == nk - 1,
                       (wok, cx.ks1), (pky,))
                psg, pkg = bank()
                for c in range(NCH):
                    mm(psg[:, 0:N], wg[:, c, fl * 128:(fl + 1) * 128], cx.H[:, c, 0:N], c == 0, c == NCH - 1, (wgk, cx.kh), (pkg,))
                sg, sgk = sgbuf(cx)
                act(sg[:, 0:N], psg[:, 0:N], AF.Sigmoid, (pkg,), (sgk,))
                mk = (cx.kmg, f)
                if first:
                    tt("dve", cx.MG[:, f, 0:N], psy[:, 0:N], sg[:, 0:N], ALU.mult, (pky, sgk), (mk,))
                else:
                    tt("dve", sg[:, 0:N], psy[:, 0:N], sg[:, 0:N], ALU.mult, (pky, sgk), (sgk,))
                    tt("dve", cx.MG[:, f, 0:N], cx.MG[:, f, 0:N], sg[:, 0:N], ALU.add, (mk, sgk), (mk,))

    def merge_proj(cx, l):
        N = cx.N
        for f in range(NCH):
            cp("act", cx.H[:, f, 0:N], cx.MG[:, f, 0:N], ((cx.kmg, f),), (cx.kh,))
        if KDBG <= 5.8 and cx.N != TT:
            return
        for half in range(2):
            if KDBG <= 5.9 and cx.N != TT and half == 1:
                return
            if KDBG == 5.95 and cx.N != TT and half == 0:
                continue
            wo, wok = wload(wview(W["w_o"][l], 0, 8, half * 512, 512), (8, 512))
            for fl in range(4):
                f = half * 4 + fl
                ps, pk = bank()
                for c in range(NCH):
                    mm(ps[:, 0:N], wo[:, c, fl * 128:(fl + 1) * 128], cx.H[:, c, 0:N], c == 0, c == NCH - 1, (wok, cx.kh), (pk,))
                tt("dve", cx.xT[:, f, 0:N], cx.xT[:, f, 0:N], ps[:, 0:N], ALU.add, (cx.kx, pk), (cx.kx,))

    def xo_proj(cx, l):
        N = cx.N
        for half in range(2):
            wxo, wxok = wload(wview(W["w_xo"][l], 0, 4, half * 512, 512), (4, 512))
            for fl in range(4):
                f = half * 4 + fl
                ps, pk = bank()
                for kc in range(4):
                    mm(ps[:, 0:N], wxo[:, kc, fl * 128:(fl + 1) * 128], cx.S1[:, kc, 0:N], kc == 0, kc == 3, (wxok, cx.ks1), (pk,))
                tt("dve", cx.xT[:, f, 0:N], cx.xT[:, f, 0:N], ps[:, 0:N], ALU.add, (cx.kx, pk), (cx.kx,))

    def ffn(cx, l, Aflat, akeys):
        N = cx.N
        rmsnorm_fm(cx, g_ffn, "g_ffn")
        for half in range(2):
            for j4 in range(4):
                wu, wuk = wload(wview(W["w_up"][l], 0, 8, (half * 16 + j4 * 4) * 128, 512), (8, 512))
                for jj in range(4):
                    j = j4 * 4 + jj
                    ps, pk = bank()
                    for c in range(NCH):
                        mm(ps[:, 0:N], wu[:, c, jj * 128:(jj + 1) * 128], cx.H[:, c, 0:N], c == 0, c == NCH - 1, (wuk, cx.kh), (pk,))
                    sg, sgk = sgbuf(cx)
                    act(sg[:, 0:N], ps[:, 0:N], AF.Relu, (pk,), (sgk,))
                    act(Aflat[:, j * N:(j + 1) * N], sg[:, 0:N], AF.Square, (sgk,), (akeys(j),))
            for fp in range(4):
                wdn, wdnk = wload(wview(W["w_down"][l], half * 16, 16, fp * 256, 256), (16, 256))
                for fl in range(2):
                    f = fp * 2 + fl
                    ps, pk = bank()
                    for j in range(16):
                        mm(ps[:, 0:N], wdn[:, j, fl * 128:(fl + 1) * 128], Aflat[:, j * N:(j + 1) * N],
                           j == 0, j == 15, (wdnk, akeys(j)), (pk,))
                    tt("dve", cx.xT[:, f, 0:N], cx.xT[:, f, 0:N], ps[:, 0:N], ALU.add, (cx.kx, pk), (cx.kx,))

    def gelu_from_psum(ps_ap, out_ap, tmp, tmpk, pk, outk):
        act(tmp, ps_ap, AF.Square, (pk,), (tmpk,))
        tsc("dve", tmp, tmp, 0.044715, 1.0, ALU.mult, ALU.add, (tmpk,), (tmpk,))
        tt("dve", tmp, tmp, ps_ap, ALU.mult, (tmpk, pk), (tmpk,))
        act(tmp, tmp, AF.Sigmoid, (tmpk,), (tmpk,), scale=1.5957691216)
        tt("dve", out_ap, tmp, ps_ap, ALU.mult, (tmpk, pk), (outk,))

    def qknorm_tm(stage, sk, P, ngrp, gdim, sqbuf, sqk, stcol):
        act(sqbuf, stage, AF.Square, (sk,), (sqk,))
        red(ST[0:P, stcol:stcol + ngrp], sqbuf.rearrange("p (g d) -> p g d", g=ngrp), (sqk,), ("ST",))
        rstd(ST[0:P, stcol:stcol + ngrp], ST[0:P, stcol:stcol + ngrp], 1.0 / gdim, ("ST",), ("ST",))
        v3 = stage.rearrange("p (g d) -> p g d", g=ngrp)
        tt("dve", v3, v3, ST[0:P, stcol:stcol + ngrp, None].to_broadcast([P, ngrp, gdim]), ALU.mult, (sk, "ST"), (sk,))
        return v3

    def tile_layer(l, i):
        cx = CP
        lastt = (i == NT - 1)
        w_in = W["w_in"][l]
        load_x_tile(l, i)
        rmsnorm_fm(cx, g_mix, "g_mix")

        wq, wqk = wload(wview(w_in, 0, 8, 0, 512), (8, 512))
        wkk_, wkkk = wload(wview(w_in, 0, 8, 512, 512), (8, 512))
        wv, wvk = wload(wview(w_in, 0, 8, 1024, 512), (8, 512))
        for s in range(4):
            tok = slice(s * 128, (s + 1) * 128)
            jg = i * 4 + s
            r0 = i * TT + s * 128
            for which, wt, wtk, stage, sk in ((0, wq, wqk, T1, "T1"), (1, wkk_, wkkk, T2, "T2")):
                ps, pk = bank()
                for c in range(NCH):
                    mm(ps[:], H[:, c, tok], wt[:, c, :], c == 0, c == NCH - 1, ("H", wtk), (pk,))
                cp("act", stage[:], ps[:], (pk,), (sk,))
                v3 = qknorm_tm(stage[:], sk, 128, 8, 64, T3[:], "T3", 0)
                if which == 0:
                    tt("dve", TB[:].rearrange("p (g d) -> p g d", g=8), v3, gq_b[:, None, :].to_broadcast([128, 8, 64]),
                       ALU.mult, (sk, "gq_b"), ("TB",))
                else:
                    tt("dve", v3, v3, gk_b[:, None, :].to_broadcast([128, 8, 64]), ALU.mult, (sk, "gk_b"), (sk,))
                    dma("sp", k_a_prompt[l, r0:r0 + 128, :], stage[:], (sk,), (), isout=True)
                    cp("dve", TB[:], stage[:], (sk,), ("TB",))
                ps2, pk2 = bank()
                psb = ps2[:].bitcast(BF16)
                for h in range(4):
                    tp(psb[:, h * 128:(h + 1) * 128], TB[:, h * 128:(h + 1) * 128], identb[:], ("TB", "identb"), (pk2,))
                if which == 0:
                    cp("act", qT[:, :, tok], psb[:, 0:512].rearrange("p (h t) -> p h t", h=4), (pk2,), ("qT",))
                else:
                    cp("act", kT[:, :, r0:r0 + 128], psb[:, 0:512].rearrange("p (h t) -> p h t", h=4), (pk2,), ("kT",))
            ps, pk = bank()
            for c in range(NCH):
                mm(ps[:], H[:, c, tok], wv[:, c, :], c == 0, c == NCH - 1, ("H", wvk), (pk,))
            cp("act", T3[:], ps[:], (pk,), ("T3",))
            dma("sp", v_a_prompt[l, r0:r0 + 128, :], T3[:], ("T3",), (), isout=True)
            cp("dve", Vx[:, jg, :, 0:128], T3[:].rearrange("p (h d) -> p h d", h=4), ("T3",), ("Vx",))

        for s in range(4):
            tok = slice(s * 128, (s + 1) * 128)
            jg = i * 4 + s
            nk_t = jg + 1
            for h in range(4):
                pso = []
                pipelined = nk_t <= 8
                stage = []
                for c in range(2):
                    prow = slice(c * 64, (c + 1) * 64)
                    pi = st["pt"]
                    st["pt"] = 1 - pi
                    ptb, ptk = PT[pi], ("PT", pi)
                    sbanks = []
                    for kt in range(nk_t):
                        if kt % 4 == 0:
                            sbanks.append(bank())
                        pss, pks = sbanks[-1]
                        blk = slice((kt % 4) * 128, (kt % 4 + 1) * 128)
                        diag = (kt == jg)
                        mm(pss[:, blk], kT[prow, h, kt * 128:(kt + 1) * 128], qT[prow, h, tok], True, not diag,
                           ("kT", "qT"), (pks,))
                        if diag:
                            mm(pss[:, blk], identb[:], maskb[:], False, True, ("identb", "maskb"), (pks,))

                    def finish_c(ptb=ptb, ptk=ptk, sbanks=sbanks):
                        for kt in range(nk_t):
                            pss, pks = sbanks[kt // 4]
                            blk = slice((kt % 4) * 128, (kt % 4 + 1) * 128)
                            dl = jg - kt
                            act(ptb[:, kt, :], pss[:, blk], AF.Exp, (pks, "alibi"), (ptk,),
                                bias=alibi[:, h * 16 + dl:h * 16 + dl + 1], scale=1.0)
                        po, pok = bank()
                        for kt in range(nk_t):
                            mm(po[:, 0:129], ptb[:, kt, :], Vx[:, kt, h, 0:129], kt == 0, kt == nk_t - 1, (ptk, "Vx"), (pok,))
                        pso.append((po, pok))
                    if pipelined:
                        stage.append(finish_c)
                    else:
                        finish_c()
                for fn_ in stage:
                    fn_()
                (p0, k0), (p1, k1) = pso
                recip(ST[:, 8:9], p0[:, 128:129], (k0,), ("ST",))
                recip(ST[:, 9:10], p1[:, 128:129], (k1,), ("ST",))
                tt("dve", ST[:, 9:10], ST[:, 9:10], lam_s[:, 4:5], ALU.mult, ("ST", "lam_s"), ("ST",))
                tsc("dve", T1[:, 0:128], p0[:, 0:128], ST[:, 8:9], None, ALU.mult, None, (k0, "ST"), ("T1",))
                stt("dve", T1[:, 0:128], p1[:, 0:128], ST[:, 9:10], T1[:, 0:128], ALU.mult, ALU.add, (k1, "ST", "T1"), ("T1",))
                act(T1[:, 128:256], T1[:, 0:128], AF.Square, ("T1",), ("T1",))
                red(ST[:, 10:11], T1[:, 128:256], ("T1",), ("ST",))
                rstd(ST[:, 10:11], ST[:, 10:11], 1.0 / 128, ("ST",), ("ST",))
                stt("dve", TB[:, 0:128], T1[:, 0:128], ST[:, 10:11], gsub_b[:], ALU.mult, ALU.mult,
                    ("T1", "ST", "gsub_b"), ("TB",))
                pt_, ptk_ = bank()
                ptb_ = pt_[:].bitcast(BF16)
                tp(ptb_[:, 0:128], TB[:, 0:128], identb[:], ("TB", "identb"), (ptk_,))
                cp("act", S1[:, h, tok], ptb_[:, 0:128], (ptk_,), ("S1",))
        branch_out(cx, l, 0, "w_a_out", 4, 0, True)

        wb1, wb1k = wload(wview(w_in, 0, 8, 1536, 512), (8, 512))
        wb2, wb2k = wload(wview(w_in, 0, 8, 2048, 256), (8, 256))
        for cc in range(2):
            if i == 0:
                T.op("dve", lambda e, cc=cc: e.memset(FA[:, cc, 0:2], 0.0), w=("FA",))
            else:
                cp("dve", FA[:, cc, 0:2], FA[:, cc, 512:514], ("FA",), ("FA",))
            psx, pkx = bank()
            for c in range(NCH):
                mm(psx[:], wb1[:, c, cc * 128:(cc + 1) * 128], H[:, c, :], c == 0, c == NCH - 1, (wb1k, "H"), (pkx,))
            cp("act", FC[:, cc, :], psx[:], (pkx,), ("FC",))
            psb_, pkb = bank()
            for c in range(NCH):
                mm(psb_[:], wb1[:, c, 256 + cc * 128:256 + (cc + 1) * 128], H[:, c, :], c == 0, c == NCH - 1, (wb1k, "H"), (pkb,))
            cp("act", FD[:, cc, :], psb_[:], (pkb,), ("FD",))
            psc, pkc = bank()
            for c in range(NCH):
                mm(psc[:], wb2[:, c, cc * 128:(cc + 1) * 128], H[:, c, :], c == 0, c == NCH - 1, (wb2k, "H"), (pkc,))
            tt("dve", FA[:, cc, 2:514], psc[:], FC[:, cc, :], ALU.mult, (pkc, "FC"), ("FA",))
            tsc("dve", FC[:, cc, :], FA[:, cc, 2:514], bcw[:, cc, 2:3], bcb[:, cc:cc + 1], ALU.mult, ALU.add,
                ("FA", "bcw", "bcb"), ("FC",))
            stt("dve", FC[:, cc, :], FA[:, cc, 1:513], bcw[:, cc, 1:2], FC[:, cc, :], ALU.mult, ALU.add, ("FA", "FC", "bcw"), ("FC",))
            stt("dve", FC[:, cc, :], FA[:, cc, 0:512], bcw[:, cc, 0:1], FC[:, cc, :], ALU.mult, ALU.add, ("FA", "FC", "bcw"), ("FC",))
            tt("dve", S1[:, 4 + cc, :], FC[:, cc, :], FD[:, cc, :], ALU.mult, ("FC", "FD"), ("S1",))
            if lastt:
                dma("sp", conv_b_prompt[l, :, cc * 128:(cc + 1) * 128].rearrange("t c -> c t"), FA[:, cc, 512:514],
                    ("FA",), (), isout=True, slow=True)
        branch_out(cx, l, 1, "w_b_out", 2, 4, False)

        wc, wck = wload(wview(w_in, 0, 8, 2304, 512), (8, 512))
        for cc in range(2):
            if i == 0:
                T.op("dve", lambda e, cc=cc: e.memset(FB[:, cc, 0:30], 0.0), w=("FB",))
            else:
                cp("dve", FB[:, cc, 0:30], FB[:, cc, 512:542], ("FB",), ("FB",))
            psg, pkg = bank()
            for c in range(NCH):
                mm(psg[:], wc[:, c, 256 + cc * 128:256 + (cc + 1) * 128], H[:, c, :], c == 0, c == NCH - 1, (wck, "H"), (pkg,))
            act(FC[:, cc, :], psg[:], AF.Sigmoid, (pkg,), ("FC",))
            psa, pka = bank()
            for c in range(NCH):
                mm(psa[:], wc[:, c, cc * 128:(cc + 1) * 128], H[:, c, :], c == 0, c == NCH - 1, (wck, "H"), (pka,))
            tt("dve", FB[:, cc, 30:542], psa[:], FC[:, cc, :], ALU.mult, (pka, "FC"), ("FB",))
            tsc("dve", FC[:, cc, :], FB[:, cc, 30:542], ccw[:, cc, 30:31], ccb[:, cc:cc + 1], ALU.mult, ALU.add,
                ("FB", "ccw", "ccb"), ("FC",))
            for k in range(30):
                stt("dve", FC[:, cc, :], FB[:, cc, k:k + 512], ccw[:, cc, k:k + 1], FC[:, cc, :], ALU.mult, ALU.add,
                    ("FB", "FC", "ccw"), ("FC",))
            if lastt:
                pst, pkt = bank()
                tp(pst[0:30, 0:128], FB[:, cc, 512:542], identf[:], ("FB", "identf"), (pkt,))
                cp("act", T1[0:30, cc * 128:(cc + 1) * 128], pst[0:30, 0:128], (pkt,), ("T1",))
        if lastt:
            dma("sp", conv_c_prompt[l], T1[0:30, 0:256], ("T1",), (), isout=True)
        for cc in range(2):
            cp("act", S1[:, cc, :], FC[:, cc, :], ("FC",), ("S1",))
            act(S1[:, 2 + cc, :], FC[:, cc, :], AF.Square, ("FC",), ("S1",))
        psm, pkm = bank()
        for cc in range(2):
            mm(psm[:], ones_c[:], S1[:, cc, :], cc == 0, cc == 1, ("ones_c", "S1"), (pkm,))
        psq, pkq = bank()
        for cc in range(2):
            mm(psq[:], ones_c[:], S1[:, 2 + cc, :], cc == 0, cc == 1, ("ones_c", "S1"), (pkq,))
        cp("act", FD[:, 0, :], psm[:], (pkm,), ("FD",))
        tt("dve", FD[:, 1, :], FD[:, 0, :], FD[:, 0, :], ALU.mult, ("FD",), ("FD",))
        tt("dve", FD[:, 1, :], psq[:], FD[:, 1, :], ALU.subtract, (pkq, "FD"), ("FD",))
        rstd(FD[:, 1, :], FD[:, 1, :], 1.0, ("FD",), ("FD",))
        for cc in range(2):
            tt("dve", FC[:, cc, :], FC[:, cc, :], FD[:, 0, :], ALU.subtract, ("FC", "FD"), ("FC",))
            tt("dve", FC[:, cc, :], FC[:, cc, :], FD[:, 1, :], ALU.mult, ("FC", "FD"), ("FC",))
            act(S1[:, 6 + cc, :], FC[:, cc, :], AF.Silu, ("FC", "clg", "clb"), ("S1",), bias=clb[:, cc:cc + 1],
                scale=clg[:, cc:cc + 1])
        branch_out(cx, l, 2, "w_c_out", 2, 6, False)

        wd, wdk = wload(wview(w_in, 0, 8, 2816, 512), (8, 512))
        for cc in range(2):
            ps, pk = bank()
            for c in range(NCH):
                mm(ps[:], wd[:, c, cc * 128:(cc + 1) * 128], H[:, c, :], c == 0, c == NCH - 1, (wdk, "H"), (pk,))
            gelu_from_psum(ps[:], FD[:, cc, :], FC[:, cc, :], "FC", pk, "FD")
        psd = [bank(), bank()]
        for s in range(4):
            tok = slice(s * 128, (s + 1) * 128)
            ps, pk = bank()
            for c in range(NCH):
                mm(ps[:, 0:256], H[:, c, tok], wd[:, c, 256:512], c == 0, c == NCH - 1, ("H", wdk), (pk,))
            gelu_from_psum(ps[:, 0:256], T1[:, 0:256], T1[:, 256:512], "T1", pk, "T1")
            T.op("dve", lambda e: e.bn_stats(ST[:, 16:22], T1[:, 0:256]), r=("T1",), w=("ST",))
            T.op("dve", lambda e: e.bn_aggr(ST[:, 22:24], ST[:, 16:22]), r=("ST",), w=("ST",))
            rstd(ST[:, 23:24], ST[:, 23:24], 1.0, ("ST",), ("ST",))
            tsc("dve", T1[:, 0:256], T1[:, 0:256], ST[:, 22:23], ST[:, 23:24], ALU.subtract, ALU.mult, ("T1", "ST"), ("T1",))
            tt("dve", T1[:, 0:256], T1[:, 0:256], dlg_b[:], ALU.mult, ("T1", "dlg_b"), ("T1",))
            tt("dve", TB[:, 0:256], T1[:, 0:256], dlb_b[:], ALU.add, ("T1", "dlb_b"), ("TB",))
            for g in range(4):
                cc, gg = g // 2, g % 2
                pd, pdk = psd[cc]
                mm(pd[gg * 64:(gg + 1) * 64, tok], TB[:, g * 64:(g + 1) * 64], WT[:, g, :], True, True, ("TB", "WT"), (pdk,))
        for cc in range(2):
            pd, pdk = psd[cc]
            tt("dve", FC[:, cc, :].rearrange("p (s t) -> p s t", s=4), pd[:].rearrange("p (s t) -> p s t", s=4),
               bsT[:, cc, None, :].to_broadcast([128, 4, 128]), ALU.add, (pdk, "bsT"), ("FC",))
            tt("dve", S1[:, cc, :], FC[:, cc, :], FD[:, cc, :], ALU.mult, ("FC", "FD"), ("S1",))
        branch_out(cx, l, 3, "w_d_out", 2, 0, False)
        merge_proj(cx, l)

        rmsnorm_fm(cx, g_x, "g_x")
        wxq, wxqk = wload(wview(W["w_xq"][l], 0, 8, 0, 512), (8, 512))
        for h in range(4):
            psq_, pkq_ = bank()
            for c in range(NCH):
                mm(psq_[:], wxq[:, c, h * 128:(h + 1) * 128], H[:, c, :], c == 0, c == NCH - 1, (wxqk, "H"), (pkq_,))
            act(S1[:, 4, :], psq_[:], AF.Square, (pkq_,), ("S1",))
            psm_, pkm_ = bank()
            mm(psm_[:], ones_h[:], S1[:, 4, :], True, True, ("ones_h", "S1"), (pkm_,))
            rstd(RS[:], psm_[:], 1.0, (pkm_,), ("RS",))
            stt("dve", qT[:, h, :], psq_[:], gxq[:, 0:1], RS[:], ALU.mult, ALU.mult, (pkq_, "gxq", "RS"), ("qT",))
            ptb, ptk = PT[0], ("PT", 0)
            p2 = ptb[:].rearrange("p a b -> p (a b)")
            for kc in range(2):
                pss, pks = bank()
                mm(pss[:], mkT[:, h, kc * 128:(kc + 1) * 128], qT[:, h, :], True, True, ("mkT", "qT"), (pks,))
                act(p2[:, kc * 512:(kc + 1) * 512], pss[:], AF.Exp, (pks,), (ptk,), scale=128.0 ** -0.5)
            po, pok = bank()
            pl, plk = bank()
            for kc in range(2):
                mm(po[:], mvb[:, kc, h * 128:(h + 1) * 128], p2[:, kc * 512:(kc + 1) * 512], kc == 0, kc == 1, ("mvb", ptk), (pok,))
            for kc in range(2):
                mm(pl[:], ones_1[:], p2[:, kc * 512:(kc + 1) * 512], kc == 0, kc == 1, ("ones_1", ptk), (plk,))
            recip(RS[:], pl[:], (plk,), ("RS",))
            tt("dve", S1[:, h, :], po[:], RS[:], ALU.mult, (pok, "RS"), ("S1",))
        xo_proj(cx, l)
        ffn(cx, l, MG.rearrange("p a b -> p (a b)").bitcast(BF16), lambda j: ("MG", j // 2))
        store_x_tile(l, i)

    def tps_to_fm(src_bf, srck, nchunk, dst_base):
        ps, pk = bank()
        psb = ps[:].bitcast(BF16)
        for j in range(nchunk):
            tp(psb[:, j * NS:(j + 1) * NS], src_bf[0:NS, j * 128:(j + 1) * 128], identb[0:NS, 0:NS], (srck, "identb"), (pk,))
        for j in range(nchunk):
            cp("act", S1s[:, dst_base + j, 0:NS], psb[:, j * NS:(j + 1) * NS], (pk,), ("S1s",))

    def tm_linear(wt, wtk, ncol):
        ps, pk = bank()
        for c in range(NCH):
            mm(ps[0:NS, 0:ncol], Hs[:, c, 0:NS], wt[:, c, 0:ncol], c == 0, c == NCH - 1, ("Hs", wtk), (pk,))
        return ps, pk

    def sample_layer(l):
        cx = CS
        w_in = W["w_in"][l]
        QK = PT[0][:].rearrange("p a b -> p (a b)").bitcast(F32)
        VX = PT[1][:].rearrange("p a b -> p (a b)").bitcast(F32)
        q_s, k_s, v_s = QK[0:NS, 0:512], QK[0:NS, 512:1024], VX[0:NS, 0:512]
        kQ, kV = ("PT", 0), ("PT", 1)
        if l == 0:
            dma("sp", T1[0:NS, :], x_sample[:, 0:512], (), ("T1",))
            dma("sp", T2[0:NS, :], x_sample[:, 512:1024], (), ("T2",))
            ps, pk = bank()
            for half, src, sk in ((0, T1, "T1"), (1, T2, "T2")):
                for j in range(4):
                    c = half * 4 + j
                    tp(ps[:, c * NS:(c + 1) * NS], src[0:NS, j * 128:(j + 1) * 128], identf[0:NS, 0:NS], (sk, "identf"), (pk,))
            for c in range(NCH):
                cp("act", xsT[:, c, :], ps[:, c * NS:(c + 1) * NS], (pk,), ("xsT",))
        rmsnorm_fm(cx, g_mix, "g_mix")

        wq, wqk = wload(wview(w_in, 0, 8, 0, 512), (8, 512))
        wk_, wkk = wload(wview(w_in, 0, 8, 512, 512), (8, 512))
        wv, wvk = wload(wview(w_in, 0, 8, 1024, 512), (8, 512))
        ps, pk = tm_linear(wq, wqk, 512)
        cp("act", q_s, ps[0:NS, :], (pk,), (kQ,))
        ps, pk = tm_linear(wk_, wkk, 512)
        cp("act", k_s, ps[0:NS, :], (pk,), (kQ,))
        ps, pk = tm_linear(wv, wvk, 512)
        cp("act", v_s, ps[0:NS, :], (pk,), (kV,))
        v3 = qknorm_tm(q_s, kQ, NS, 8, 64, T3[0:NS, :], "T3", 0)
        tt("dve", v3, v3, gq_b[0:NS, None, :].to_broadcast([NS, 8, 64]), ALU.mult, (kQ, "gq_b"), (kQ,))
        v3 = qknorm_tm(k_s, kQ, NS, 8, 64, T3[0:NS, :], "T3", 0)
        tt("dve", v3, v3, gk_b[0:NS, None, :].to_broadcast([NS, 8, 64]), ALU.mult, (kQ, "gk_b"), (kQ,))
        dma("sp", k_a_sample[l], k_s, (kQ,), (), isout=True)
        dma("sp", v_a_sample[l], v_s, (kV,), (), isout=True)
        dma("sp", qs_dram, q_s, (kQ,), ("qs_dram",))

        if KDBG <= 1:
            return
        PU = BPG * NPG
        KCs = [SCR[0:PU, 0:2048], SCR[0:PU, 2048:4096]]
        VCs = [SCR[0:PU, 4096:6144], SCR[0:PU, 6272:8320]]
        KKs = [MGK[0:4], MGK[4:8]]
        VKs = [("FA", "FB"), ("FC", "FD")]
        for g in range(NG):
            for bl in range(BPG):
                dma("sp", T2[bl * NPG:(bl + 1) * NPG, :], qs_dram[g * BPG + bl].partition_broadcast(NPG), ("qs_dram",), ("T2",))
            po, pok = bank()
            pl, plk = bank()
            for ch in range(NCHK):
                bi_ = ch % 2
                KC, VC, KK, VK = KCs[bi_], VCs[bi_], KKs[bi_], VKs[bi_]
                eoff = (l * NCHK + ch) * NPOOL * 2048
                T.dma_sw(lambda e, eoff=eoff, g=g, KC=KC: e.indirect_dma_start(
                    out=KC, out_offset=None, in_=ck,
                    in_offset=bass.IndirectOffsetOnAxis(ap=PTI[0:PU, g:g + 1], axis=0), element_offset=eoff), ("PTI",), KK, ("KC", bi_))
                T.dma_sw(lambda e, eoff=eoff, g=g, VC=VC: e.indirect_dma_start(
                    out=VC, out_offset=None, in_=cv,
                    in_offset=bass.IndirectOffsetOnAxis(ap=PTI[0:PU, g:g + 1], axis=0), element_offset=eoff), ("PTI",), VK, ("VC", bi_))
                K3 = KC.rearrange("p (r d) -> p r d", r=4)
                tt("dve", K3, K3, T2[0:PU, None, :].to_broadcast([PU, 4, 512]), ALU.mult, KK + ("T2",), KK)
                red(Ssc[0:PU].rearrange("p r c -> p (r c)"), KC.rearrange("p (a d) -> p a d", d=64), KK, ("Ssc",))
                sv4 = Ssc[0:PU].rearrange("p r (h c) -> p r h c", c=2)
                tt("dve", sv4, sv4,
                   sbias[0:PU, ch * 16:(ch + 1) * 16].rearrange("p (r h) -> p r h", h=4)[:, :, :, None].to_broadcast([PU, 4, 4, 2]),
                   ALU.add, ("Ssc", "sbias"), ("Ssc",))
                for bl in range(BPG):
                    rows = slice(bl * NPG, (bl + 1) * NPG)
                    act(Pz[rows, :, bl * 8:(bl + 1) * 8], Ssc[rows, :, :], AF.Exp, ("Ssc",), ("Pz",))
                V3 = VC.rearrange("p (r d) -> p r d", r=4)
                for r4 in range(4):
                    mm(po[0:NCOL, :], Pz[0:PU, r4, :], V3[:, r4, :], ch == 0 and r4 == 0, ch == NCHK - 1 and r4 == 3,
                       ("Pz",) + VK, (pok,))
                red(Pzs[0:PU, :], Pz[0:PU].rearrange("p r c -> p c r"), ("Pz",), ("Pzs",))
                mm(pl[0:NCOL, 0:2], Pzs[0:PU, :], ones_f[0:PU, 0:2], ch == 0, ch == NCHK - 1, ("Pzs", "ones_f"), (plk,))
            cp("act", T3[0:NCOL, :], po[0:NCOL, :], (pok,), ("T3",))
            cp("act", ST[0:NCOL, 30:31], pl[0:NCOL, 0:1], (plk,), ("ST",))
            dma("sp", pa_dram[g * NCOL:(g + 1) * NCOL, 0:512], T3[0:NCOL, :], ("T3",), ("pa_dram",))
            dma("sp", pa_dram[g * NCOL:(g + 1) * NCOL, 512:513], ST[0:NCOL, 30:31], ("ST",), ("pa_dram",), slow=True)

        if KDBG <= 2:
            return
        wb1, wb1k = wload(wview(w_in, 0, 8, 1536, 512), (8, 512))
        wb2, wb2k = wload(wview(w_in, 0, 8, 2048, 256), (8, 256))
        ps, pk = tm_linear(wb1, wb1k, 512)
        cp("act", T1[0:NS, :], ps[0:NS, :], (pk,), ("T1",))
        ps, pk = tm_linear(wb2, wb2k, 256)
        tt("dve", T2[0:NS, 0:256], ps[0:NS, 0:256], T1[0:NS, 0:256], ALU.mult, (pk, "T1"), ("T2",))
        dma("sp", T3[0:NS, :], state_b[l], (), ("T3",))
        WB = SCR[0:NS, 0:768]
        BB = SCR[0:NS, 768:1024]
        Y = SCR[0:NS, 1024:1280]
        TM_ = SCR[0:NS, 1280:1536]
        dma("sp", WB, W["b_conv_w"][l].rearrange("k c -> (k c)").partition_broadcast(NS), (), MGK)
        dma("sp", BB, W["b_conv_b"][l].partition_broadcast(NS), (), MGK)
        tt("dve", Y, T2[0:NS, 0:256], WB[:, 512:768], ALU.mult, ("T2",) + MGK, MGK)
        tt("dve", Y, Y, BB, ALU.add, MGK, MGK)
        tt("dve", TM_, T3[0:NS, 256:512], WB[:, 256:512], ALU.mult, ("T3",) + MGK, MGK)
        tt("dve", Y, Y, TM_, ALU.add, MGK, MGK)
        tt("dve", TM_, T3[0:NS, 0:256], WB[:, 0:256], ALU.mult, ("T3",) + MGK, MGK)
        tt("dve", Y, Y, TM_, ALU.add, MGK, MGK)
        tt("dve", TB[0:NS, 0:256], Y, T1[0:NS, 256:512], ALU.mult, ("T1",) + MGK, ("TB",))
        dma("sp", conv_b_sample[l][:, 0:256], T3[0:NS, 256:512], ("T3",), (), isout=True)
        dma("sp", conv_b_sample[l][:, 256:512], T2[0:NS, 0:256], ("T2",), (), isout=True)
        tps_to_fm(TB, "TB", 2, 4)
        branch_out(cx, l, 1, "w_b_out", 2, 4, True)

        if KDBG <= 3:
            return
        wc, wck = wload(wview(w_in, 0, 8, 2304, 512), (8, 512))
        ps, pk = tm_linear(wc, wck, 512)
        act(T1[0:NS, 256:512], ps[0:NS, 256:512], AF.Sigmoid, (pk,), ("T1",))
        UC = T2[0:NS, 256:512]
        tt("dve", UC, ps[0:NS, 0:256], T1[0:NS, 256:512], ALU.mult, (pk, "T1"), ("T2",))
        W30 = SCR[0:NS, 1536:1792]
        CB = SCR[0:NS, 1792:2048]
        YC = SCR[0:NS, 2048:2304]
        TC = SCR[0:NS, 2304:2560]
        dma("sp", W30, W["c_conv_w"][l][30].partition_broadcast(NS), (), MGK)
        dma("sp", CB, W["c_conv_b"][l].partition_broadcast(NS), (), MGK)
        tt("dve", YC, UC, W30, ALU.mult, ("T2",) + MGK, MGK)
        tt("dve", YC, YC, CB, ALU.add, MGK, MGK)
        WCK = SCR[0:NS, 4096:4096 + 1536]
        PCK = SCR[0:NS, 6144:6144 + 1536]
        for k5 in range(5):
            dma("sp", WCK, W["c_conv_w"][l][k5 * 6:(k5 + 1) * 6].rearrange("k c -> (k c)").partition_broadcast(NS), (), FK)
            dma("sp", PCK, state_c[l][:, k5 * 1536:(k5 + 1) * 1536], (), FK)
            tt("dve", PCK, PCK, WCK, ALU.mult, FK, FK)
            red(TC, PCK.rearrange("p (k c) -> p c k", k=6), FK, MGK)
            tt("dve", YC, YC, TC, ALU.add, MGK, MGK)
        dma("sp", conv_c_sample[l][:, 0:29 * 256], state_c[l][:, 256:30 * 256], (), (), isout=True)
        dma("sp", conv_c_sample[l][:, 29 * 256:30 * 256], UC, ("T2",), (), isout=True)
        LG = SCR[0:NS, 2560:2816]
        LB = SCR[0:NS, 2816:3072]
        dma("sp", LG, W["c_ln_g"][l].partition_broadcast(NS), (), MGK)
        dma("sp", LB, W["c_ln_b"][l].partition_broadcast(NS), (), MGK)
        T.op("dve", lambda e: e.bn_stats(ST[0:NS, 16:22], YC), r=MGK, w=("ST",))
        T.op("dve", lambda e: e.bn_aggr(ST[0:NS, 22:24], ST[0:NS, 16:22]), r=("ST",), w=("ST",))
        rstd(ST[0:NS, 23:24], ST[0:NS, 23:24], 1.0, ("ST",), ("ST",))
        tsc("dve", YC, YC, ST[0:NS, 22:23], ST[0:NS, 23:24], ALU.subtract, ALU.mult, MGK + ("ST",), MGK)
        tt("dve", YC, YC, LG, ALU.mult, MGK, MGK)
        tt("dve", YC, YC, LB, ALU.add, MGK, MGK)
        act(TB[0:NS, 0:256], YC, AF.Silu, MGK, ("TB",))
        tps_to_fm(TB, "TB", 2, 6)
        branch_out(cx, l, 2, "w_c_out", 2, 6, False)

        if KDBG <= 4:
            return
        wd, wdk = wload(wview(w_in, 0, 8, 2816, 512), (8, 512))
        ps, pk = tm_linear(wd, wdk, 512)
        gelu_from_psum(ps[0:NS, :], T1[0:NS, :], T3[0:NS, :], "T3", pk, "T1")
        T.op("dve", lambda e: e.bn_stats(ST[0:NS, 16:22], T1[0:NS, 256:512]), r=("T1",), w=("ST",))
        T.op("dve", lambda e: e.bn_aggr(ST[0:NS, 22:24], ST[0:NS, 16:22]), r=("ST",), w=("ST",))
        rstd(ST[0:NS, 23:24], ST[0:NS, 23:24], 1.0, ("ST",), ("ST",))
        DV = T3[0:NS, 0:256]
        DG = T3[0:NS, 256:512]
        tsc("dve", DV, T1[0:NS, 256:512], ST[0:NS, 22:23], ST[0:NS, 23:24], ALU.subtract, ALU.mult, ("T1", "ST"), ("T3",))
        tt("dve", DV, DV, dlg_b[0:NS, :], ALU.mult, ("T3", "dlg_b"), ("T3",))
        tt("dve", DV, DV, dlb_b[0:NS, :], ALU.add, ("T3", "dlb_b"), ("T3",))
        dma("sp", d_v_sample[l], DV, ("T3",), (), isout=True)
        DV3 = DV.rearrange("p (g c) -> p g c", g=4)
        DG3 = DG.rearrange("p (g c) -> p g c", g=4)
        tt("dve", DG3, DV3, w00[0:NS, 0:4, None].to_broadcast([NS, 4, 64]), ALU.mult, ("T3", "w00"), ("T3",))
        tt("dve", DG3, DG3, w00[0:NS, 4:8, None].to_broadcast([NS, 4, 64]), ALU.add, ("T3", "w00"), ("T3",))
        tt("dve", TB[0:NS, 0:256], DG, T1[0:NS, 0:256], ALU.mult, ("T3", "T1"), ("TB",))
        tps_to_fm(TB, "TB", 2, 0)
        branch_out(cx, l, 3, "w_d_out", 2, 0, False)

        if KDBG <= 5:
            return
        ARd = qT[:].rearrange("p a b -> p (a b)").bitcast(F32)[0:NS, :].rearrange("p (a d) -> p a d", a=8)
        aro = pa_dram.rearrange("(b a) w -> b a w", a=8)
        for hc in range(8):
            h = hc // 2
            dma("sp", ARd[:, hc, :], aro[:, hc, h * 128:(h + 1) * 128], ("pa_dram",), ("qT",))
        dma("sp", ST[0:NS, 32:40], aro[:, :, 512], ("pa_dram",), ("ST",), slow=True)
        tt("dve", T2[0:NS, :], q_s, k_s, ALU.mult, (kQ,), ("T2",))
        red(ST[0:NS, 40:48], T2[0:NS, :].rearrange("p (a d) -> p a d", a=8), ("T2",), ("ST",))
        act(ST[0:NS, 40:48], ST[0:NS, 40:48], AF.Exp, ("ST",), ("ST",))
        tt("dve", ST[0:NS, 32:40], ST[0:NS, 32:40], ST[0:NS, 40:48], ALU.add, ("ST",), ("ST",))
        for hc in range(8):
            h = hc // 2
            stt("dve", ARd[:, hc, :], v_s[:, h * 128:(h + 1) * 128], ST[0:NS, 40 + hc:41 + hc], ARd[:, hc, :], ALU.mult, ALU.add,
                (kV, "ST", "qT"), ("qT",))
        recip(ST[0:NS, 32:40], ST[0:NS, 32:40], ("ST",), ("ST",))
        for h in range(4):
            tt("dve", ST[0:NS, 33 + 2 * h:34 + 2 * h], ST[0:NS, 33 + 2 * h:34 + 2 * h], lam_s[0:NS, 4:5], ALU.mult, ("ST", "lam_s"), ("ST",))
            oh = T2[0:NS, h * 128:(h + 1) * 128]
            tsc("dve", oh, ARd[:, 2 * h, :], ST[0:NS, 32 + 2 * h:33 + 2 * h], None, ALU.mult, None, ("qT", "ST"), ("T2",))
            stt("dve", oh, ARd[:, 2 * h + 1, :], ST[0:NS, 33 + 2 * h:34 + 2 * h], oh, ALU.mult, ALU.add, ("qT", "ST", "T2"), ("T2",))
        if KDBG <= 5.2:
            return
        v3 = qknorm_tm(T2[0:NS, :], "T2", NS, 4, 128, T3[0:NS, :], "T3", 48)
        tt("dve", TB[0:NS, :].rearrange("p (h d) -> p h d", h=4), v3, gsub_b[0:NS, None, :].to_broadcast([NS, 4, 128]),
           ALU.mult, ("T2", "gsub_b"), ("TB",))
        if KDBG <= 5.4:
            return
        tps_to_fm(TB, "TB", 4, 0)
        if KDBG <= 5.5:
            return
        branch_out(cx, l, 0, "w_a_out", 4, 0, False)
        if KDBG <= 5.7:
            return
        merge_proj(cx, l)

        if KDBG <= 6:
            return
        rmsnorm_fm(cx, g_x, "g_x")
        wxq, wxqk = wload(wview(W["w_xq"][l], 0, 8, 0, 512), (8, 512))
        ps, pk = tm_linear(wxq, wxqk, 512)
        cp("act", T1[0:NS, :], ps[0:NS, :], (pk,), ("T1",))
        v3 = qknorm_tm(T1[0:NS, :], "T1", NS, 4, 128, T3[0:NS, :], "T3", 0)
        GX = SCR[0:NS, 3072:3200]
        dma("sp", GX, W["x_qnorm_g"][l].partition_broadcast(NS), (), MGK)
        tt("dve", v3, v3, GX[:, None, :].to_broadcast([NS, 4, 128]), ALU.mult, ("T1",) + MGK, ("T1",))
        dma("sp", qx_dram, T1[0:NS, :], ("T1",), ("qx_dram",))
        MK = SCR[:, 4096:5120]
        MV = SCR[:, 5120:6144]
        for b in range(NS):
            dma("sp", T2[:], qx_dram[b].partition_broadcast(128), ("qx_dram",), ("T2",))
            dma("sp", MK.rearrange("p (k d) -> p k d", k=2), cmk[l, b].rearrange("(k p) d -> p k d", p=128), (), ("FA",))
            dma("sp", MV.rearrange("p (k d) -> p k d", k=2), cmv[l, b].rearrange("(k p) d -> p k d", p=128), (), ("FB",))
            M3 = MK.rearrange("p (k d) -> p k d", k=2)
            tt("dve", M3, M3, T2[:, None, :].to_broadcast([128, 2, 512]), ALU.mult, ("FA", "T2"), ("FA",))
            red(Pc[:, b, :], MK.rearrange("p (a d) -> p a d", d=128), ("FA",), ("Pc",))
            act(Pc[:, b, :], Pc[:, b, :], AF.Exp, ("Pc",), ("Pc",), scale=128.0 ** -0.5)
            po, pok = bank()
            for kc in range(2):
                mm(po[0:4, :], Pc[:, b, kc * 4:(kc + 1) * 4], MV[:, kc * 512:(kc + 1) * 512], kc == 0, kc == 1, ("Pc", "FB"), (pok,))
            cp("act", T3[0:4, :], po[0:4, :], (pok,), ("T3",))
            dma("sp", om_dram[b], T3[0:4, :], ("T3",), ("om_dram",))
        pl, plk = bank()
        mm(pl[0:1, 0:NS * 8], ones_f[:, 0:1], Pc[:].rearrange("p a b -> p (a b)"), True, True, ("ones_f", "Pc"), (plk,))
        cp("act", T3[0:1, 0:NS * 8], pl[0:1, 0:NS * 8], (plk,), ("T3",))
        dma("sp", ol_dram, T3[0:1, 0:NS * 8], ("T3",), ("ol_dram",))
        for h in range(4):
            dma("sp", T1[0:NS, h * 128:(h + 1) * 128], om_dram[:, h, h * 128:(h + 1) * 128], ("om_dram",), ("T1",))
        dma("sp", ST[0:NS, 48:56], ol_dram.rearrange("o (b a) -> (o b) a", a=8), ("ol_dram",), ("ST",))
        tt("dve", ST[0:NS, 48:52], ST[0:NS, 48:52], ST[0:NS, 52:56], ALU.add, ("ST",), ("ST",))
        recip(ST[0:NS, 48:52], ST[0:NS, 48:52], ("ST",), ("ST",))
        tt("dve", TB[0:NS, :].rearrange("p (h d) -> p h d", h=4), T1[0:NS, :].rearrange("p (h d) -> p h d", h=4),
           ST[0:NS, 48:52, None].to_broadcast([NS, 4, 128]), ALU.mult, ("T1", "ST"), ("TB",))
        tps_to_fm(TB, "TB", 4, 0)
        xo_proj(cx, l)
        ffn(cx, l, S1[:, 0:2, :].rearrange("p a b -> p (a b)"), lambda j: "S1")
        if l == L - 1:
            for half, dst, dk in ((0, T1, "T1"), (1, T2, "T2")):
                ps, pk = bank()
                for j in range(4):
                    tp(ps[0:NS, j * 128:(j + 1) * 128], xsT[:, half * 4 + j, 0:NS], identf[:], ("xsT", "identf"), (pk,))
                cp("act", dst[0:NS, :], ps[0:NS, :], (pk,), (dk,))
                dma("sp", y_sample[:, half * 512:(half + 1) * 512], dst[0:NS, :], (dk,), (), isout=True)

    for l in range(L):
        T.new_epoch()
        load_layer_params(l)
        mem_layer(l)
        if with_sample:
            sample_layer(l)
        for i in range(NT):
            T.new_epoch()
            tile_layer(l, i)
    T.finish()
    with nc.Block() as block:
        T.emit(block)
    stack.close()
    return nc


_CACHE = {}


def _consts():
    ident = np.eye(128, dtype=np.float32)
    kk = np.arange(128)[:, None]
    qq = np.arange(128)[None, :]
    mask = np.where(kk > qq, -1e30, 0.0).astype(np.float32)
    tril = (qq <= kk).astype(np.float32)
    slopes = np.array([2.0 ** (-8.0 * (i + 1) / 4) for i in range(4)], dtype=np.float64)
    p = np.arange(128)[:, None, None]
    dl = np.arange(16)[None, None, :]
    alibi = (slopes[None, :, None] * (p - 128.0 * dl)).astype(np.float32).reshape(128, 64)
    return dict(c_ident=ident, c_mask=mask, c_tril=tril, c_alibi=alibi), slopes


WEIGHT_KEYS = ["norm_mix_g", "w_in", "a_qnorm_g", "a_knorm_g", "a_lam", "a_subln_g", "w_a_out", "b_conv_w",
               "b_conv_b", "w_b_out", "c_conv_w", "c_conv_b", "c_ln_g", "c_ln_b", "w_c_out", "d_ln_g", "d_ln_b",
               "d_ws", "d_bs", "w_d_out", "w_o", "norm_x_g", "mem_norm_g", "w_xq", "w_xk", "w_xv", "x_qnorm_g",
               "x_knorm_g", "w_xo", "norm_ffn_g", "w_up", "w_down"]


def kernel(**inp):
    x_prompt = np.asarray(inp["x_prompt"])
    B, SEQ, _ = x_prompt.shape
    L = np.asarray(inp["w_in"]).shape[0]
    page_table = np.ascontiguousarray(np.asarray(inp["page_table"]).astype(np.int32))
    NSG, NPG = page_table.shape
    NSL = NSG // NCORES
    cache_k = np.asarray(inp["cache_k_a"])
    cache_v = np.asarray(inp["cache_v_a"])
    NPOOL, PAGE_TOK = cache_k.shape[1], cache_k.shape[2]
    NCHK = PAGE_TOK // 4
    past_len = NPG * PAGE_TOK
    key = (SEQ, NPG, NPOOL, L, NSL, PAGE_TOK)
    if key not in _CACHE:
        _CACHE[key] = build(SEQ, NPG, NPOOL, L, NSL, PAGE_TOK)
    nc = _CACHE[key]
    consts, slopes = _consts()
    shared = {k: np.asarray(inp[k]) for k in WEIGHT_KEYS}
    shared["ck"] = np.ascontiguousarray(
        cache_k.reshape(L, NPOOL, NCHK, 2048).transpose(0, 2, 1, 3)).reshape(L * NCHK * NPOOL, 2048)
    shared["cv"] = np.ascontiguousarray(
        cache_v.reshape(L, NPOOL, NCHK, 2048).transpose(0, 2, 1, 3)).reshape(L * NCHK * NPOOL, 2048)
    BPG = max(1, min(NSL, 128 // NPG))
    j = (np.arange(128) % NPG)[:, None, None]
    t = np.arange(PAGE_TOK)[None, :, None]
    dist = past_len - (PAGE_TOK * j + t)
    shared["c_sbias"] = (-slopes[None, None, :] * dist).astype(np.float32).reshape(128, PAGE_TOK * 4)
    x_sample = np.asarray(inp["x_sample"]).reshape(NSG, D)
    state_b = np.asarray(inp["state_conv_b"]).reshape(L, NSG, 512)
    state_c = np.asarray(inp["state_conv_c"]).reshape(L, NSG, 30 * 256)
    cmk = np.asarray(inp["cache_mem_k"]).reshape(L, NSG, 256, 512)
    cmv = np.asarray(inp["cache_mem_v"]).reshape(L, NSG, 256, 512)
    mem_prompt = np.asarray(inp["mem_prompt"])
    in_maps = []
    for c in range(NCORES):
        m = dict(consts)
        m.update(shared)
        sl = slice(c * NSL, (c + 1) * NSL)
        m["x_prompt"] = np.ascontiguousarray(x_prompt[c])
        m["mem_prompt"] = np.ascontiguousarray(mem_prompt[c])
        m["x_sample"] = np.ascontiguousarray(x_sample[sl])
        m["state_conv_b"] = np.ascontiguousarray(state_b[:, sl])
        m["state_conv_c"] = np.ascontiguousarray(state_c[:, sl])
        m["cache_mem_k"] = np.ascontiguousarray(cmk[:, sl])
        m["cache_mem_v"] = np.ascontiguousarray(cmv[:, sl])
        m["page_table"] = np.ascontiguousarray(page_table[sl])
        in_maps.append(m)
    if inp.get("_prepare_only"):
        return nc, in_maps
    res = run_bass_kernel_spmd(nc, in_maps, core_ids=list(range(NCORES))).results
    y_prompt = np.stack([r["y_prompt"] for r in res], 0)
    kp = np.stack([r["k_a_prompt"] for r in res], 1).reshape(L, B, SEQ, 4, 128)
    vp = np.stack([r["v_a_prompt"] for r in res], 1).reshape(L, B, SEQ, 4, 128)
    cbp = np.stack([r["conv_b_prompt"] for r in res], 1)
    ccp = np.stack([r["conv_c_prompt"] for r in res], 1)
    mkp = np.stack([r["mem_k_prompt"] for r in res], 1).reshape(L, B, 256, 4, 128)
    mvp = np.stack([r["mem_v_prompt"] for r in res], 1).reshape(L, B, 256, 4, 128)
    ys = np.concatenate([r["y_sample"] for r in res], 0).reshape(NSG, 1, D)
    ks = np.concatenate([r["k_a_sample"] for r in res], 1).reshape(L, NSG, 1, 4, 128)
    vs = np.concatenate([r["v_a_sample"] for r in res], 1).reshape(L, NSG, 1, 4, 128)
    cbs = np.concatenate([r["conv_b_sample"] for r in res], 1).reshape(L, NSG, 2, 256)
    ccs = np.concatenate([r["conv_c_sample"] for r in res], 1).reshape(L, NSG, 30, 256)
    dvs = np.concatenate([r["d_v_sample"] for r in res], 1).reshape(L, NSG, 1, 256)
    return (y_prompt, ys, kp, vp, cbp, ccp, mkp, mvp, ks, vs, cbs, ccs, dvs)
```

```python
import contextlib
import math
import numpy as np
import concourse.bass as bass
import concourse.mybir as mybir
from concourse.bass_utils import run_bass_kernel_spmd

F32 = mybir.dt.float32
BF16 = mybir.dt.bfloat16
I32 = mybir.dt.int32
AF = mybir.ActivationFunctionType
ALU = mybir.AluOpType
AX = mybir.AxisListType

D = 1024
NCH = 8
INC = 7424
DFF = 4096
EPS = 1e-6
NCORES = 8
TT = 512
NEPOCH = 12
import os
KDBG = float(os.environ.get('KDBG', '99'))
SW_CLEAR = os.environ.get('SW_CLEAR', '0') == '1'


class _Op:
    __slots__ = ("eng", "fn", "deps", "sig", "kind", "sem", "val", "ep", "clr", "pre")


class Tr:
    def __init__(self, nc, stack, ndma=32):
        self.nc = nc
        self.names = ["pe", "act", "dve", "pool", "sp"]
        self.ops = {k: [] for k in self.names}
        self.lastw = {}
        self.readers = {}
        self.esem = {k: [stack.enter_context(nc.semaphore("s_%s%d" % (k, i))) for i in range(NEPOCH)]
                     for k in self.names if k != "sp"}
        self.esem["sp"] = [stack.enter_context(nc.semaphore("s_sp"))] * NEPOCH
        self.dsem = [stack.enter_context(nc.semaphore("d%d" % i)) for i in range(ndma)]
        self.dval = [0] * ndma
        self.dlast = [None] * ndma
        self.dnext = 0
        self.outs = []
        self.epoch = 0
        self.stack = stack
        self.bufsem = {}

    def new_epoch(self):
        self.epoch += 1
        assert self.epoch < NEPOCH, "too many epochs"

    def _mk(self, eng, fn, kind, r, w, extra=()):
        op = _Op()
        op.eng, op.fn, op.kind, op.sig, op.sem, op.val, op.ep = eng, fn, kind, False, None, 0, self.epoch
        op.clr, op.pre = False, None
        deps = list(extra)
        for k in r:
            lw = self.lastw.get(k)
            if lw is not None:
                deps.append(lw)
        for k in w:
            lw = self.lastw.get(k)
            if lw is not None:
                deps.append(lw)
            deps.extend(self.readers.get(k, {}).values())
        keep = []
        for d in deps:
            if d is op or d in keep:
                continue
            if d.kind == "c" and kind == "c" and d.eng == eng and eng == "pe":
                continue
            d.sig = True
            keep.append(d)
        op.deps = keep
        for k in w:
            self.lastw[k] = op
            self.readers[k] = {}
        for k in r:
            rk = (eng, id(op)) if kind == "d" else eng
            self.readers.setdefault(k, {})[rk] = op
        self.ops[eng].append(op)
        return op

    def op(self, eng, fn, r=(), w=()):
        return self._mk(eng, fn, "c", r, w)

    def dma(self, q, fn, r=(), w=(), out=False):
        k = self.dnext
        self.dnext = (self.dnext + 1) % len(self.dsem)
        extra = [self.dlast[k]] if self.dlast[k] is not None else []
        op = self._mk(q, fn, "d", r, w, extra)
        self.dval[k] += 16
        op.sem, op.val = self.dsem[k], self.dval[k]
        self.dlast[k] = op
        if out:
            self.outs.append(op)
        return op

    def dma_sw(self, fn, r, w, bufkey):
        if bufkey not in self.bufsem:
            i = len(self.bufsem)
            self.bufsem[bufkey] = [[self.stack.enter_context(self.nc.semaphore("b%d_%d" % (i, j))) for j in range(2)], 0]
        sems, n = self.bufsem[bufkey]
        self.bufsem[bufkey][1] = n + 1
        op = self._mk("pool", fn, "d", r, w)
        if SW_CLEAR:
            op.sem, op.val, op.clr = sems[n % 2], 16, True
            op.pre = sems[(n - 1) % 2] if n >= 1 else None
        else:
            op.sem, op.val = sems[0], 16 * (n + 1)
        return op

    def finish(self):
        return self._mk("sp", None, "c", (), (), list(self.outs))

    def emit(self, block):
        for k in self.names:
            c = {}
            for op in self.ops[k]:
                if op.kind == "c" and op.sig:
                    c[op.ep] = c.get(op.ep, 0) + 1
                    op.val = c[op.ep]
        tr = self

        def body(name):
            def run(e):
                known = {}
                for op in tr.ops[name]:
                    waits = {}
                    for d in op.deps:
                        sem = d.sem if d.kind == "d" else tr.esem[d.eng][d.ep]
                        key = id(sem)
                        if d.clr:
                            if known.get(key) is d:
                                continue
                            waits[key] = (sem, 16)
                            known[key] = d
                            continue
                        if known.get(key, 0) >= d.val:
                            continue
                        if key not in waits or waits[key][1] < d.val:
                            waits[key] = (sem, d.val)
                    for key, (sem, val) in waits.items():
                        e.wait_ge(sem, val)
                        if not isinstance(known.get(key), _Op):
                            known[key] = val
                    if op.pre is not None:
                        e.sem_clear(op.pre)
                    if op.fn is None:
                        continue
                    ins = op.fn(e)
                    if op.kind == "d":
                        ins.then_inc(op.sem, 16)
                    elif op.sig:
                        ins.then_inc(tr.esem[name][op.ep], 1)
            return run

        block.tensor(body("pe"))
        block.scalar(body("act"))
        block.vector(body("dve"))
        block.gpsimd(body("pool"))
        block.sync(body("sp"))


class Cx:
    pass


def build(SEQ, NPG, NPOOL, L=2, NS=4, PAGE_TOK=128, with_sample=True):
    NT = SEQ // TT
    NKT = SEQ // 128
    NCHK = PAGE_TOK // 4
    BPG = max(1, min(NS, 128 // NPG))
    NG = (NS + BPG - 1) // BPG
    NCOL = BPG * 8
    if with_sample:
        assert NS % BPG == 0 and (BPG == 1 or NPG % 32 == 0) and BPG * NPG <= 128
    nc = bass.Bass("TRN2", target_bir_lowering=False)
    stack = contextlib.ExitStack()

    def din(name, shape, dt=F32):
        return nc.dram_tensor(name, list(shape), dt, kind="ExternalInput").ap()

    def dout(name, shape, dt=F32):
        return nc.dram_tensor(name, list(shape), dt, kind="ExternalOutput").ap()

    x_prompt = din("x_prompt", [SEQ, D])
    mem_prompt = din("mem_prompt", [256, D])
    c_ident = din("c_ident", [128, 128])
    c_mask = din("c_mask", [128, 128])
    c_tril = din("c_tril", [128, 128])
    c_alibi = din("c_alibi", [128, 64])
    W = {}
    wshapes = dict(
        norm_mix_g=[L, D], w_in=[L, D, INC], a_qnorm_g=[L, 64], a_knorm_g=[L, 64], a_lam=[L, 4, 64],
        a_subln_g=[L, 128], w_a_out=[L, 512, D], b_conv_w=[L, 3, 256], b_conv_b=[L, 256],
        w_b_out=[L, 256, D], c_conv_w=[L, 31, 256], c_conv_b=[L, 256], c_ln_g=[L, 256], c_ln_b=[L, 256],
        w_c_out=[L, 256, D], d_ln_g=[L, 256], d_ln_b=[L, 256], d_ws=[L, 4, 128, 128], d_bs=[L, 4, 128],
        w_d_out=[L, 256, D], w_o=[L, D, D], norm_x_g=[L, D], mem_norm_g=[L, D], w_xq=[L, D, 512],
        w_xk=[L, D, 512], w_xv=[L, D, 512], x_qnorm_g=[L, 128], x_knorm_g=[L, 128], w_xo=[L, 512, D],
        norm_ffn_g=[L, D], w_up=[L, D, DFF], w_down=[L, DFF, D])
    for k, s in wshapes.items():
        W[k] = din(k, s)

    y_prompt = dout("y_prompt", [SEQ, D])
    k_a_prompt = dout("k_a_prompt", [L, SEQ, 512])
    v_a_prompt = dout("v_a_prompt", [L, SEQ, 512])
    conv_b_prompt = dout("conv_b_prompt", [L, 2, 256])
    conv_c_prompt = dout("conv_c_prompt", [L, 30, 256])
    mem_k_prompt = dout("mem_k_prompt", [L, 256, 512])
    mem_v_prompt = dout("mem_v_prompt", [L, 256, 512])
    xs_dram = nc.dram_tensor("xs_scratch", [128, NCH, SEQ], F32).ap()
    if with_sample:
        x_sample = din("x_sample", [NS, D])
        ck = din("ck", [L * NCHK * NPOOL, 2048])
        cv = din("cv", [L * NCHK * NPOOL, 2048])
        state_b = din("state_conv_b", [L, NS, 512])
        state_c = din("state_conv_c", [L, NS, 30 * 256])
        cmk = din("cache_mem_k", [L, NS, 256, 512])
        cmv = din("cache_mem_v", [L, NS, 256, 512])
        page_table = din("page_table", [NS, NPG], I32)
        c_sbias = din("c_sbias", [128, NCHK * 16])
        y_sample = dout("y_sample", [NS, D])
        k_a_sample = dout("k_a_sample", [L, NS, 512])
        v_a_sample = dout("v_a_sample", [L, NS, 512])
        conv_b_sample = dout("conv_b_sample", [L, NS, 512])
        conv_c_sample = dout("conv_c_sample", [L, NS, 30 * 256])
        d_v_sample = dout("d_v_sample", [L, NS, 256])
        qs_dram = nc.dram_tensor("qs_scratch", [NS, 512], F32).ap()
        qx_dram = nc.dram_tensor("qx_scratch", [NS, 512], F32).ap()
        om_dram = nc.dram_tensor("om_scratch", [NS, 4, 512], F32).ap()
        ol_dram = nc.dram_tensor("ol_scratch", [1, NS * 8], F32).ap()
        pa_dram = nc.dram_tensor("pa_scratch", [NG * NCOL, 516], F32).ap()

    T = Tr(nc, stack)

    def sb(name, shape, dt):
        return stack.enter_context(nc.sbuf_tensor(name, list(shape), dt))

    identf = sb("identf", [128, 128], F32)
    identb = sb("identb", [128, 128], BF16)
    maskf = sb("maskf", [128, 128], F32)
    maskb = sb("maskb", [128, 128], BF16)
    trilf = sb("trilf", [128, 128], F32)
    alibi = sb("alibi", [128, 64], F32)
    ones_dm = sb("ones_dm", [128, 128], BF16)
    ones_c = sb("ones_c", [128, 128], BF16)
    ones_h = sb("ones_h", [128, 128], BF16)
    ones_1 = sb("ones_1", [128, 128], BF16)
    ones_f = sb("ones_f", [128, 2], F32)
    g_mix = sb("g_mix", [128, NCH], F32)
    g_x = sb("g_x", [128, NCH], F32)
    g_ffn = sb("g_ffn", [128, NCH], F32)
    g_mem = sb("g_mem", [128, NCH], F32)
    gq_b = sb("gq_b", [128, 64], F32)
    gk_b = sb("gk_b", [128, 64], F32)
    gsub_b = sb("gsub_b", [128, 128], F32)
    lam_b = sb("lam_b", [128, 256], F32)
    lam_t = sb("lam_t", [128, 128], F32)
    lam_s = sb("lam_s", [128, 8], F32)
    bcw = sb("bcw", [128, 2, 3], F32)
    bcb = sb("bcb", [128, 2], F32)
    ccw = sb("ccw", [128, 2, 31], F32)
    ccb = sb("ccb", [128, 2], F32)
    clg = sb("clg", [128, 2], F32)
    clb = sb("clb", [128, 2], F32)
    dlg_b = sb("dlg_b", [128, 256], F32)
    dlb_b = sb("dlb_b", [128, 256], F32)
    bsT = sb("bsT", [128, 2, 128], F32)
    wsf = sb("wsf", [128, 4, 128], F32)
    wsb = sb("wsb", [128, 4, 128], BF16)
    WT = sb("WT", [128, 4, 128], BF16)
    gxq = sb("gxq", [128, 1], F32)
    gxk_b = sb("gxk_b", [128, 128], F32)
    mkT = sb("mkT", [128, 4, 256], BF16)
    mvb = sb("mvb", [128, 2, 512], BF16)
    memTg = sb("memTg", [128, NCH, 256], BF16)
    mrstd = sb("mrstd", [128, 2], F32)
    kT = sb("kT", [128, 4, SEQ], BF16)
    Vx = sb("Vx", [128, NKT, 4, 132], BF16)
    xT = sb("xT", [128, NCH, TT], F32)
    H = sb("H", [128, NCH, TT], BF16)
    S1 = sb("S1", [128, NCH, TT], BF16)
    RS = sb("RS", [128, TT], F32)
    SCR = sb("SCR", [128, 8448], F32)
    MG = SCR[:, 0:4096].rearrange("p (a b) -> p a b", a=8)
    FA = SCR[:, 4096:5184].rearrange("p (a b) -> p a b", a=2)
    FB = SCR[:, 5184:6272].rearrange("p (a b) -> p a b", a=2)
    FC = SCR[:, 6272:7296].rearrange("p (a b) -> p a b", a=2)
    FD = SCR[:, 7296:8320].rearrange("p (a b) -> p a b", a=2)
    MGK = tuple(("MG", f) for f in range(8))
    FK = ("FA", "FB", "FC", "FD")
    T1 = sb("T1", [128, 512], F32)
    T2 = sb("T2", [128, 512], F32)
    T3 = sb("T3", [128, 512], F32)
    TB = sb("TB", [128, 512], BF16)
    qT = sb("qT", [128, 4, TT], BF16)
    PT = [sb("PT%d" % i, [128, 16, 128], BF16) for i in range(2)]
    SG = [sb("SG%d" % i, [128, TT], F32) for i in range(2)]
    ST = sb("ST", [128, 64], F32)
    NSLOT = 4
    WS = [sb("WS%d" % i, [128, 4096], BF16) for i in range(NSLOT)]
    PS = [stack.enter_context(nc.psum_tensor("ps%d" % i, [128, 512], F32)) for i in range(8)]
    if with_sample:
        xsT = sb("xsT", [128, NCH, NS], F32)
        Hs = sb("Hs", [128, NCH, NS], BF16)
        S1s = sb("S1s", [128, NCH, NS], BF16)
        MGs = sb("MGs", [128, NCH, NS], F32)
        RSs = sb("RSs", [128, NS], F32)
        SGs = [sb("SGs%d" % i, [128, NS], F32) for i in range(2)]
        Ssc = sb("Ssc", [128, 4, 8], F32)
        Pz = sb("Pz", [128, 4, NCOL], F32)
        Pzs = sb("Pzs", [128, NCOL], F32)
        PTI = sb("PTI", [128, NG], I32)
        sbias = sb("sbias", [128, NCHK * 16], F32)
        Pc = sb("Pc", [128, NS, 8], F32)
        w00 = sb("w00", [128, 8], F32)

    st = dict(bank=0, slot=0, pt=0, sg=0, sgs=0)

    def bank():
        b = st["bank"]
        st["bank"] = (b + 1) % 8
        return PS[b], ("ps", b)

    CP = Cx()
    CP.xT, CP.H, CP.S1, CP.MG, CP.RS, CP.N = xT, H, S1, MG, RS, TT
    CP.kx, CP.kh, CP.ks1, CP.krs, CP.kmg = "xT", "H", "S1", "RS", "MG"
    CP.SG, CP.sgk, CP.sgi = SG, "SG", "sg"
    if with_sample:
        CS = Cx()
        CS.xT, CS.H, CS.S1, CS.MG, CS.RS, CS.N = xsT, Hs, S1s, MGs, RSs, NS
        CS.kx, CS.kh, CS.ks1, CS.krs, CS.kmg = "xsT", "Hs", "S1s", "RSs", "MGs"
        CS.SG, CS.sgk, CS.sgi = SGs, "SGs", "sgs"

    def sgbuf(cx):
        i = st[cx.sgi]
        st[cx.sgi] = 1 - i
        return cx.SG[i], (cx.sgk, i)

    def mm(out, lhsT, rhs, start, stop, r, w):
        T.op("pe", lambda e, o=out, a=lhsT, b=rhs, s0=start, s1=stop: e.matmul(o, a, b, start=s0, stop=s1), r=r, w=w)

    def tp(out, in_, ident, r, w):
        T.op("pe", lambda e, o=out, a=in_, i=ident: e.transpose(o, a, i), r=r, w=w)

    def act(out, in_, func, r, w, bias=None, scale=None):
        def f(e, o=out, a=in_, fn=func, b=bias, s=scale):
            kw = {}
            if b is not None:
                kw["bias"] = b
            if s is not None:
                kw["scale"] = s
            return e.activation(o, a, fn, **kw)
        T.op("act", f, r=r, w=w)

    def tsc(eng, out, in0, s1, s2, op0, op1, r, w):
        if op1 is None:
            T.op(eng, lambda e, o=out, a=in0, x=s1, p=op0: e.tensor_scalar(o, a, x, 0.0, p, ALU.add), r=r, w=w)
        else:
            T.op(eng, lambda e, o=out, a=in0, x=s1, y=s2, p=op0, q=op1: e.tensor_scalar(o, a, x, y, p, q), r=r, w=w)

    def rstd(out, in_, scale, r, w):
        act(out, in_, AF.Sqrt, r, w, bias=EPS, scale=scale)
        T.op("dve", lambda e, o=out: e.reciprocal(o, o), r=w, w=w)

    def recip(out, in_, r, w):
        T.op("dve", lambda e, o=out, a=in_: e.reciprocal(o, a), r=r, w=w)

    def red(out, in_, r, w):
        T.op("dve", lambda e, o=out, a=in_: e.tensor_reduce(o, a, AX.X, ALU.add), r=r, w=w)

    def stt(eng, out, in0, sc, in1, op0, op1, r, w):
        T.op(eng, lambda e, o=out, a=in0, s=sc, b=in1, p=op0, q=op1: e.scalar_tensor_tensor(o, a, s, b, p, q), r=r, w=w)

    def tt(eng, out, in0, in1, op, r, w):
        T.op(eng, lambda e, o=out, a=in0, b=in1, p=op: e.tensor_tensor(o, a, b, p), r=r, w=w)

    def cp(eng, out, in_, r, w):
        if eng == "act":
            T.op("act", lambda e, o=out, a=in_: e.copy(o, a), r=r, w=w)
        else:
            T.op(eng, lambda e, o=out, a=in_: e.tensor_copy(o, a), r=r, w=w)

    def dma(q, out, in_, r, w, isout=False, slow=False):
        if slow:
            T.dma(q, lambda e, o=out, a=in_: e.dma_start(out=o, in_=a, allow_slow_non_contiguous=True), r=r, w=w, out=isout)
        else:
            T.dma(q, lambda e, o=out, a=in_: e.dma_start(out=o, in_=a), r=r, w=w, out=isout)

    def wload(src_ap, shape3):
        i = st["slot"]
        st["slot"] = (i + 1) % NSLOT
        a, b = shape3
        dst = WS[i][:, 0:a * b].rearrange("p (a b) -> p a b", a=a)
        T.dma_sw(lambda e, o=dst, s=src_ap: e.dma_start(out=o, in_=s), (), (("WS", i),), ("WS", i))
        return dst, ("WS", i)

    def wview(w2d, r0, nr, c0, ncol):
        return w2d[r0 * 128:(r0 + nr) * 128, c0:c0 + ncol].rearrange("(k p) n -> p k n", p=128)

    dma("sp", identf[:], c_ident, (), ("identf",))
    dma("sp", maskf[:], c_mask, (), ("maskf",))
    dma("sp", trilf[:], c_tril, (), ("trilf",))
    dma("sp", alibi[:], c_alibi, (), ("alibi",))
    cp("dve", identb[:], identf[:], ("identf",), ("identb",))
    cp("dve", maskb[:], maskf[:], ("maskf",), ("maskb",))
    T.op("pool", lambda e: e.memset(ones_dm[:], 1.0 / D), w=("ones_dm",))
    T.op("pool", lambda e: e.memset(ones_c[:], 1.0 / 256), w=("ones_c",))
    T.op("pool", lambda e: e.memset(ones_h[:], 1.0 / 128), w=("ones_h",))
    T.op("pool", lambda e: e.memset(ones_1[:], 1.0), w=("ones_1",))
    T.op("pool", lambda e: e.memset(ones_f[:], 1.0), w=("ones_f",))
    T.op("pool", lambda e: e.memset(Vx[:], 1.0), w=("Vx",))
    if with_sample:
        T.op("pool", lambda e: e.memset(Pz[:], 0.0), w=("Pz",))
        dma("sp", sbias[:], c_sbias, (), ("sbias",))
        T.op("pool", lambda e: e.memset(PTI[:], 0), w=("PTI",))
        for g in range(NG):
            dma("sp", PTI[0:BPG * NPG, g:g + 1], page_table[g * BPG:(g + 1) * BPG, :].rearrange("b (j o) -> (b j) o", o=1),
                (), ("PTI",), slow=True)

    def rmsnorm_fm(cx, gt, gkey):
        N = cx.N
        act(cx.S1[:, :, 0:N], cx.xT[:, :, 0:N], AF.Square, (cx.kx,), (cx.ks1,))
        ps, pk = bank()
        for c in range(NCH):
            mm(ps[:, 0:N], ones_dm[:], cx.S1[:, c, 0:N], c == 0, c == NCH - 1, ("ones_dm", cx.ks1), (pk,))
        rstd(cx.RS[:, 0:N], ps[:, 0:N], 1.0, (pk,), (cx.krs,))
        for c in range(NCH):
            stt("dve", cx.H[:, c, 0:N], cx.xT[:, c, 0:N], gt[:, c:c + 1], cx.RS[:, 0:N], ALU.mult, ALU.mult,
                (cx.kx, cx.krs, gkey), (cx.kh,))

    def load_layer_params(l):
        def fm(dst, src, key):
            dma("sp", dst[:], src.rearrange("(c p) -> p c", p=128), (), (key,), slow=True)
        fm(g_mix, W["norm_mix_g"][l], "g_mix")
        fm(g_x, W["norm_x_g"][l], "g_x")
        fm(g_ffn, W["norm_ffn_g"][l], "g_ffn")
        fm(g_mem, W["mem_norm_g"][l], "g_mem")
        fm(bcb, W["b_conv_b"][l], "bcb")
        fm(ccb, W["c_conv_b"][l], "ccb")
        fm(clg, W["c_ln_g"][l], "clg")
        fm(clb, W["c_ln_b"][l], "clb")
        for cc in range(2):
            dma("sp", bcw[:, cc, :], W["b_conv_w"][l][:, cc * 128:(cc + 1) * 128].rearrange("k p -> p k"), (), ("bcw",), slow=True)
            dma("sp", ccw[:, cc, :], W["c_conv_w"][l][:, cc * 128:(cc + 1) * 128].rearrange("k p -> p k"), (), ("ccw",), slow=True)
        dma("sp", gxq[:], W["x_qnorm_g"][l].rearrange("(p o) -> p o", o=1), (), ("gxq",), slow=True)

        def bc(dst, src, key):
            dma("sp", dst[:], src.partition_broadcast(128), (), (key,))
        bc(gq_b, W["a_qnorm_g"][l], "gq_b")
        bc(gk_b, W["a_knorm_g"][l], "gk_b")
        bc(gsub_b, W["a_subln_g"][l], "gsub_b")
        bc(lam_b, W["a_lam"][l].rearrange("a b -> (a b)"), "lam_b")
        bc(dlg_b, W["d_ln_g"][l], "dlg_b")
        bc(dlb_b, W["d_ln_b"][l], "dlb_b")
        bc(gxk_b, W["x_knorm_g"][l], "gxk_b")
        for g in range(4):
            cc, gg = g // 2, g % 2
            dma("sp", bsT[gg * 64:(gg + 1) * 64, cc, :], W["d_bs"][l, g].partition_broadcast(64), (), ("bsT",))
        dma("sp", wsf[:], W["d_ws"][l].rearrange("g t s -> t g s"), (), ("wsf",))
        if with_sample:
            dma("sp", w00[:, 0:4], W["d_ws"][l][:, 0, 0].partition_broadcast(128), (), ("w00",), slow=True)
            dma("sp", w00[:, 4:8], W["d_bs"][l][:, 0].partition_broadcast(128), (), ("w00",), slow=True)
        lam_init = 0.8 - 0.6 * math.exp(-0.3 * l)
        tsc("dve", gq_b[:], gq_b[:], 0.125, None, ALU.mult, None, ("gq_b",), ("gq_b",))
        tsc("dve", gsub_b[:], gsub_b[:], 1.0 - lam_init, None, ALU.mult, None, ("gsub_b",), ("gsub_b",))
        tt("dve", lam_t[:, 0:64], lam_b[:, 0:64], lam_b[:, 64:128], ALU.mult, ("lam_b",), ("lam_t",))
        tt("dve", lam_t[:, 64:128], lam_b[:, 128:192], lam_b[:, 192:256], ALU.mult, ("lam_b",), ("lam_t",))
        red(lam_s[:, 0:2], lam_t[:, 0:128].rearrange("p (a b) -> p a b", a=2), ("lam_t",), ("lam_s",))
        act(lam_s[:, 2:4], lam_s[:, 0:2], AF.Exp, ("lam_s",), ("lam_s",))
        tt("dve", lam_s[:, 4:5], lam_s[:, 3:4], lam_s[:, 2:3], ALU.subtract, ("lam_s",), ("lam_s",))
        tsc("dve", lam_s[:, 4:5], lam_s[:, 4:5], -lam_init, None, ALU.add, None, ("lam_s",), ("lam_s",))
        tt("dve", wsb[:], wsf[:], trilf[:, None, :].to_broadcast([128, 4, 128]), ALU.mult, ("wsf", "trilf"), ("wsb",))
        ps, pk = bank()
        psb = ps[:].bitcast(BF16)
        for g in range(4):
            tp(psb[:, g * 128:(g + 1) * 128], wsb[:, g, :], identb[:], ("wsb", "identb"), (pk,))
        cp("dve", WT[:].rearrange("p g t -> p (g t)"), psb[:, 0:512], (pk,), ("WT",))

    def mem_layer(l):
        for tcn in range(2):
            dma("sp", T1[:], mem_prompt[tcn * 128:(tcn + 1) * 128, 0:512], (), ("T1",))
            dma("sp", T2[:], mem_prompt[tcn * 128:(tcn + 1) * 128, 512:1024], (), ("T2",))
            act(T3[:], T1[:], AF.Square, ("T1",), ("T3",))
            red(ST[:, 0:1], T3[:], ("T3",), ("ST",))
            act(T3[:], T2[:], AF.Square, ("T2",), ("T3",))
            red(ST[:, 1:2], T3[:], ("T3",), ("ST",))
            tt("dve", ST[:, 2:3], ST[:, 0:1], ST[:, 1:2], ALU.add, ("ST",), ("ST",))
            rstd(mrstd[:, tcn:tcn + 1], ST[:, 2:3], 1.0 / D, ("ST",), ("mrstd",))
            for half, src, sk in ((0, T1, "T1"), (1, T2, "T2")):
                cp("dve", TB[:], src[:], (sk,), ("TB",))
                ps, pk = bank()
                psb = ps[:].bitcast(BF16)
                for j in range(4):
                    tp(psb[:, j * 128:(j + 1) * 128], TB[:, j * 128:(j + 1) * 128], identb[:], ("TB", "identb"), (pk,))
                for j in range(4):
                    c = half * 4 + j
                    tsc("dve", memTg[:, c, tcn * 128:(tcn + 1) * 128], psb[:, j * 128:(j + 1) * 128],
                        g_mem[:, c:c + 1], None, ALU.mult, None, (pk, "g_mem"), ("memTg",))
        wk, wkk = wload(wview(W["w_xk"][l], 0, 8, 0, 512), (8, 512))
        wv, wvk = wload(wview(W["w_xv"][l], 0, 8, 0, 512), (8, 512))
        for tcn in range(2):
            ps, pk = bank()
            for c in range(NCH):
                mm(ps[:], memTg[:, c, tcn * 128:(tcn + 1) * 128], wk[:, c, :], c == 0, c == NCH - 1, ("memTg", wkk), (pk,))
            act(T1[:], ps[:], AF.Copy, (pk, "mrstd"), ("T1",), scale=mrstd[:, tcn:tcn + 1])
            tt("dve", T3[:], T1[:], T1[:], ALU.mult, ("T1",), ("T3",))
            red(ST[:, 0:4], T3[:].rearrange("p (h d) -> p h d", h=4), ("T3",), ("ST",))
            rstd(ST[:, 0:4], ST[:, 0:4], 1.0 / 128, ("ST",), ("ST",))
            v3 = T1[:].rearrange("p (h d) -> p h d", h=4)
            tt("dve", v3, v3, ST[:, 0:4, None].to_broadcast([128, 4, 128]), ALU.mult, ("T1", "ST"), ("T1",))
            tt("dve", v3, v3, gxk_b[:, None, :].to_broadcast([128, 4, 128]), ALU.mult, ("T1", "gxk_b"), ("T1",))
            dma("sp", mem_k_prompt[l, tcn * 128:(tcn + 1) * 128, :], T1[:], ("T1",), (), isout=True)
            cp("dve", TB[:], T1[:], ("T1",), ("TB",))
            ps2, pk2 = bank()
            psb = ps2[:].bitcast(BF16)
            for h in range(4):
                tp(psb[:, h * 128:(h + 1) * 128], TB[:, h * 128:(h + 1) * 128], identb[:], ("TB", "identb"), (pk2,))
            for h in range(4):
                cp("dve", mkT[:, h, tcn * 128:(tcn + 1) * 128], psb[:, h * 128:(h + 1) * 128], (pk2,), ("mkT",))
            ps, pk = bank()
            for c in range(NCH):
                mm(ps[:], memTg[:, c, tcn * 128:(tcn + 1) * 128], wv[:, c, :], c == 0, c == NCH - 1, ("memTg", wvk), (pk,))
            act(T2[:], ps[:], AF.Copy, (pk, "mrstd"), ("T2",), scale=mrstd[:, tcn:tcn + 1])
            dma("sp", mem_v_prompt[l, tcn * 128:(tcn + 1) * 128, :], T2[:], ("T2",), (), isout=True)
            cp("dve", mvb[:, tcn, :], T2[:], ("T2",), ("mvb",))

    def load_x_tile(l, i):
        if l == 0:
            for s in range(4):
                r0 = i * TT + s * 128
                dma("sp", T1[:], x_prompt[r0:r0 + 128, 0:512], (), ("T1",))
                dma("sp", T2[:], x_prompt[r0:r0 + 128, 512:1024], (), ("T2",))
                for half, src, sk in ((0, T1, "T1"), (1, T2, "T2")):
                    ps, pk = bank()
                    for j in range(4):
                        tp(ps[:, j * 128:(j + 1) * 128], src[:, j * 128:(j + 1) * 128], identf[:], (sk, "identf"), (pk,))
                    for j in range(4):
                        cp("act", xT[:, half * 4 + j, s * 128:(s + 1) * 128], ps[:, j * 128:(j + 1) * 128], (pk,), ("xT",))
        else:
            dma("sp", xT[:], xs_dram[:, :, i * TT:(i + 1) * TT], (("xs", i),), ("xT",))

    def store_x_tile(l, i):
        if l < L - 1:
            dma("sp", xs_dram[:, :, i * TT:(i + 1) * TT], xT[:], ("xT",), (("xs", i),))
        else:
            for s in range(4):
                r0 = i * TT + s * 128
                for half, dst, dk in ((0, T1, "T1"), (1, T2, "T2")):
                    ps, pk = bank()
                    for j in range(4):
                        tp(ps[:, j * 128:(j + 1) * 128], xT[:, half * 4 + j, s * 128:(s + 1) * 128], identf[:],
                           ("xT", "identf"), (pk,))
                    cp("act", dst[:], ps[:], (pk,), (dk,))
                    dma("sp", y_prompt[r0:r0 + 128, half * 512:(half + 1) * 512], dst[:], (dk,), (), isout=True)

    def branch_out(cx, l, bi, wout_name, nk, s1_base, first):
        N = cx.N
        for half in range(2):
            gcol = 3328 + bi * D + half * 512
            wg, wgk = wload(wview(W["w_in"][l], 0, 8, gcol, 512), (8, 512))
            wo, wok = wload(wview(W[wout_name][l], 0, nk, half * 512, 512), (nk, 512))
            for fl in range(4):
                f = half * 4 + fl
                psy, pky = bank()
                for kc in range(nk):
                    mm(psy[:, 0:N], wo[:, kc, fl * 128:(fl + 1) * 128], cx.S1[:, s1_base + kc, 0:N], kc == 0, kc == nk - 1,
                       (wok, cx.ks1), (pky,))
                psg, pkg = bank()
                for c in range(NCH):
                    mm(psg[:, 0:N], wg[:, c, fl * 128:(fl + 1) * 128], cx.H[:, c, 0:N], c == 0, c == NCH - 1, (wgk, cx.kh), (pkg,))
                sg, sgk = sgbuf(cx)
                act(sg[:, 0:N], psg[:, 0:N], AF.Sigmoid, (pkg,), (sgk,))
                mk = (cx.kmg, f)
                if first:
                    tt("dve", cx.MG[:, f, 0:N], psy[:, 0:N], sg[:, 0:N], ALU.mult, (pky, sgk), (mk,))
                else:
                    tt("dve", sg[:, 0:N], psy[:, 0:N], sg[:, 0:N], ALU.mult, (pky, sgk), (sgk,))
                    tt("dve", cx.MG[:, f, 0:N], cx.MG[:, f, 0:N], sg[:, 0:N], ALU.add, (mk, sgk), (mk,))

    def merge_proj(cx, l):
        N = cx.N
        for f in range(NCH):
            cp("act", cx.H[:, f, 0:N], cx.MG[:, f, 0:N], ((cx.kmg, f),), (cx.kh,))
        if KDBG <= 5.8 and cx.N != TT:
            return
        for half in range(2):
            if KDBG <= 5.9 and cx.N != TT and half == 1:
                return
            if KDBG == 5.95 and cx.N != TT and half == 0:
                continue
            wo, wok = wload(wview(W["w_o"][l], 0, 8, half * 512, 512), (8, 512))
            for fl in range(4):
                f = half * 4 + fl
                ps, pk = bank()
                for c in range(NCH):
                    mm(ps[:, 0:N], wo[:, c, fl * 128:(fl + 1) * 128], cx.H[:, c, 0:N], c == 0, c == NCH - 1, (wok, cx.kh), (pk,))
                tt("dve", cx.xT[:, f, 0:N], cx.xT[:, f, 0:N], ps[:, 0:N], ALU.add, (cx.kx, pk), (cx.kx,))

    def xo_proj(cx, l):
        N = cx.N
        for half in range(2):
            wxo, wxok = wload(wview(W["w_xo"][l], 0, 4, half * 512, 512), (4, 512))
            for fl in range(4):
                f = half * 4 + fl
                ps, pk = bank()
                for kc in range(4):
                    mm(ps[:, 0:N], wxo[:, kc, fl * 128:(fl + 1) * 128], cx.S1[:, kc, 0:N], kc == 0, kc == 3, (wxok, cx.ks1), (pk,))
                tt("dve", cx.xT[:, f, 0:N], cx.xT[:, f, 0:N], ps[:, 0:N], ALU.add, (cx.kx, pk), (cx.kx,))

    def ffn(cx, l, Aflat, akeys):
        N = cx.N
        rmsnorm_fm(cx, g_ffn, "g_ffn")
        for half in range(2):
            for j4 in range(4):
                wu, wuk = wload(wview(W["w_up"][l], 0, 8, (half * 16 + j4 * 4) * 128, 512), (8, 512))
                for jj in range(4):
                    j = j4 * 4 + jj
                    ps, pk = bank()
                    for c in range(NCH):
                        mm(ps[:, 0:N], wu[:, c, jj * 128:(jj + 1) * 128], cx.H[:, c, 0:N], c == 0, c == NCH - 1, (wuk, cx.kh), (pk,))
                    sg, sgk = sgbuf(cx)
                    act(sg[:, 0:N], ps[:, 0:N], AF.Relu, (pk,), (sgk,))
                    act(Aflat[:, j * N:(j + 1) * N], sg[:, 0:N], AF.Square, (sgk,), (akeys(j),))
            for fp in range(4):
                wdn, wdnk = wload(wview(W["w_down"][l], half * 16, 16, fp * 256, 256), (16, 256))
                for fl in range(2):
                    f = fp * 2 + fl
                    ps, pk = bank()
                    for j in range(16):
                        mm(ps[:, 0:N], wdn[:, j, fl * 128:(fl + 1) * 128], Aflat[:, j * N:(j + 1) * N],
                           j == 0, j == 15, (wdnk, akeys(j)), (pk,))
                    tt("dve", cx.xT[:, f, 0:N], cx.xT[:, f, 0:N], ps[:, 0:N], ALU.add, (cx.kx, pk), (cx.kx,))

    def gelu_from_psum(ps_ap, out_ap, tmp, tmpk, pk, outk):
        act(tmp, ps_ap, AF.Square, (pk,), (tmpk,))
        tsc("dve", tmp, tmp, 0.044715, 1.0, ALU.mult, ALU.add, (tmpk,), (tmpk,))
        tt("dve", tmp, tmp, ps_ap, ALU.mult, (tmpk, pk), (tmpk,))
        act(tmp, tmp, AF.Sigmoid, (tmpk,), (tmpk,), scale=1.5957691216)
        tt("dve", out_ap, tmp, ps_ap, ALU.mult, (tmpk, pk), (outk,))

    def qknorm_tm(stage, sk, P, ngrp, gdim, sqbuf, sqk, stcol):
        act(sqbuf, stage, AF.Square, (sk,), (sqk,))
        red(ST[0:P, stcol:stcol + ngrp], sqbuf.rearrange("p (g d) -> p g d", g=ngrp), (sqk,), ("ST",))
        rstd(ST[0:P, stcol:stcol + ngrp], ST[0:P, stcol:stcol + ngrp], 1.0 / gdim, ("ST",), ("ST",))
        v3 = stage.rearrange("p (g d) -> p g d", g=ngrp)
        tt("dve", v3, v3, ST[0:P, stcol:stcol + ngrp, None].to_broadcast([P, ngrp, gdim]), ALU.mult, (sk, "ST"), (sk,))
        return v3

    def tile_layer(l, i):
        cx = CP
        lastt = (i == NT - 1)
        w_in = W["w_in"][l]
        load_x_tile(l, i)
        rmsnorm_fm(cx, g_mix, "g_mix")

        wq, wqk = wload(wview(w_in, 0, 8, 0, 512), (8, 512))
        wkk_, wkkk = wload(wview(w_in, 0, 8, 512, 512), (8, 512))
        wv, wvk = wload(wview(w_in, 0, 8, 1024, 512), (8, 512))
        for s in range(4):
            tok = slice(s * 128, (s + 1) * 128)
            jg = i * 4 + s
            r0 = i * TT + s * 128
            def qk_chain(which, wt, wtk, stage, sk, sq, sqk, stc, tb, tbk):
                ps, pk = bank()
                for c in range(NCH):
                    mm(ps[:], H[:, c, tok], wt[:, c, :], c == 0, c == NCH - 1, ("H", wtk), (pk,))
                yield
                cp("act", stage[:], ps[:], (pk,), (sk,))
                yield
                act(sq, stage[:], AF.Square, (sk,), (sqk,))
                yield
                red(ST[:, stc:stc + 8], sq.rearrange("p (g d) -> p g d", g=8), (sqk,), (("ST", stc),))
                yield
                act(ST[:, stc:stc + 8], ST[:, stc:stc + 8], AF.Sqrt, (("ST", stc),), (("ST", stc),), bias=EPS, scale=1.0 / 64)
                yield
                recip(ST[:, stc:stc + 8], ST[:, stc:stc + 8], (("ST", stc),), (("ST", stc),))
                yield
                v3 = stage[:].rearrange("p (g d) -> p g d", g=8)
                tt("dve", v3, v3, ST[:, stc:stc + 8, None].to_broadcast([128, 8, 64]), ALU.mult, (sk, ("ST", stc)), (sk,))
                yield
                if which == 0:
                    tt("dve", tb[:].rearrange("p (g d) -> p g d", g=8), v3, gq_b[:, None, :].to_broadcast([128, 8, 64]),
                       ALU.mult, (sk, "gq_b"), (tbk,))
                    yield
                else:
                    tt("dve", v3, v3, gk_b[:, None, :].to_broadcast([128, 8, 64]), ALU.mult, (sk, "gk_b"), (sk,))
                    yield
                    dma("sp", k_a_prompt[l, r0:r0 + 128, :], stage[:], (sk,), (), isout=True)
                    cp("dve", tb[:], stage[:], (sk,), (tbk,))
                    yield
                ps2, pk2 = bank()
                psb = ps2[:].bitcast(BF16)
                for h in range(4):
                    tp(psb[:, h * 128:(h + 1) * 128], tb[:, h * 128:(h + 1) * 128], identb[:], (tbk, "identb"), (pk2,))
                yield
                if which == 0:
                    cp("act", qT[:, :, tok], psb[:, 0:512].rearrange("p (h t) -> p h t", h=4), (pk2,), ("qT",))
                else:
                    cp("act", kT[:, :, r0:r0 + 128], psb[:, 0:512].rearrange("p (h t) -> p h t", h=4), (pk2,), ("kT",))
                yield

            SGb = SG[0][:].bitcast(BF16)
            gq_ = qk_chain(0, wq, wqk, T1, "T1", T3[:], "T3", 0, TB, "TB")
            gk_ = qk_chain(1, wkk_, wkkk, T2, "T2", SG[1][:], ("SG", 1), 56, SGb[:, 0:512], ("SG", 0))
            live = [gq_, gk_]
            while live:
                for g_ in list(live):
                    try:
                        next(g_)
                    except StopIteration:
                        live.remove(g_)
            ps, pk = bank()
            for c in range(NCH):
                mm(ps[:], H[:, c, tok], wv[:, c, :], c == 0, c == NCH - 1, ("H", wvk), (pk,))
            cp("act", T3[:], ps[:], (pk,), ("T3",))
            dma("sp", v_a_prompt[l, r0:r0 + 128, :], T3[:], ("T3",), (), isout=True)
            cp("dve", Vx[:, jg, :, 0:128], T3[:].rearrange("p (h d) -> p h d", h=4), ("T3",), ("Vx",))

        for s in range(4):
            tok = slice(s * 128, (s + 1) * 128)
            jg = i * 4 + s
            nk_t = jg + 1
            for h in range(4):
                pso = []
                pipelined = nk_t <= 8
                stage = []
                for c in range(2):
                    prow = slice(c * 64, (c + 1) * 64)
                    pi = st["pt"]
                    st["pt"] = 1 - pi
                    ptb, ptk = PT[pi], ("PT", pi)
                    sbanks = []
                    for kt in range(nk_t):
                        if kt % 4 == 0:
                            sbanks.append(bank())
                        pss, pks = sbanks[-1]
                        blk = slice((kt % 4) * 128, (kt % 4 + 1) * 128)
                        diag = (kt == jg)
                        mm(pss[:, blk], kT[prow, h, kt * 128:(kt + 1) * 128], qT[prow, h, tok], True, not diag,
                           ("kT", "qT"), (pks,))
                        if diag:
                            mm(pss[:, blk], identb[:], maskb[:], False, True, ("identb", "maskb"), (pks,))

                    def finish_c(ptb=ptb, ptk=ptk, sbanks=sbanks):
                        for kt in range(nk_t):
                            pss, pks = sbanks[kt // 4]
                            blk = slice((kt % 4) * 128, (kt % 4 + 1) * 128)
                            dl = jg - kt
                            act(ptb[:, kt, :], pss[:, blk], AF.Exp, (pks, "alibi"), (ptk,),
                                bias=alibi[:, h * 16 + dl:h * 16 + dl + 1], scale=1.0)
                        po, pok = bank()
                        for kt in range(nk_t):
                            mm(po[:, 0:129], ptb[:, kt, :], Vx[:, kt, h, 0:129], kt == 0, kt == nk_t - 1, (ptk, "Vx"), (pok,))
                        pso.append((po, pok))
                    if pipelined:
                        stage.append(finish_c)
                    else:
                        finish_c()
                for fn_ in stage:
                    fn_()
                (p0, k0), (p1, k1) = pso
                recip(ST[:, 8:9], p0[:, 128:129], (k0,), ("ST",))
                recip(ST[:, 9:10], p1[:, 128:129], (k1,), ("ST",))
                tt("dve", ST[:, 9:10], ST[:, 9:10], lam_s[:, 4:5], ALU.mult, ("ST", "lam_s"), ("ST",))
                tsc("dve", T1[:, 0:128], p0[:, 0:128], ST[:, 8:9], None, ALU.mult, None, (k0, "ST"), ("T1",))
                stt("dve", T1[:, 0:128], p1[:, 0:128], ST[:, 9:10], T1[:, 0:128], ALU.mult, ALU.add, (k1, "ST", "T1"), ("T1",))
                act(T1[:, 128:256], T1[:, 0:128], AF.Square, ("T1",), ("T1",))
                red(ST[:, 10:11], T1[:, 128:256], ("T1",), ("ST",))
                rstd(ST[:, 10:11], ST[:, 10:11], 1.0 / 128, ("ST",), ("ST",))
                stt("dve", TB[:, 0:128], T1[:, 0:128], ST[:, 10:11], gsub_b[:], ALU.mult, ALU.mult,
                    ("T1", "ST", "gsub_b"), ("TB",))
                pt_, ptk_ = bank()
                ptb_ = pt_[:].bitcast(BF16)
                tp(ptb_[:, 0:128], TB[:, 0:128], identb[:], ("TB", "identb"), (ptk_,))
                cp("act", S1[:, h, tok], ptb_[:, 0:128], (ptk_,), ("S1",))
        branch_out(cx, l, 0, "w_a_out", 4, 0, True)

        wb1, wb1k = wload(wview(w_in, 0, 8, 1536, 512), (8, 512))
        wb2, wb2k = wload(wview(w_in, 0, 8, 2048, 256), (8, 256))
        for cc in range(2):
            if i == 0:
                T.op("dve", lambda e, cc=cc: e.memset(FA[:, cc, 0:2], 0.0), w=("FA",))
            else:
                cp("dve", FA[:, cc, 0:2], FA[:, cc, 512:514], ("FA",), ("FA",))
            psx, pkx = bank()
            for c in range(NCH):
                mm(psx[:], wb1[:, c, cc * 128:(cc + 1) * 128], H[:, c, :], c == 0, c == NCH - 1, (wb1k, "H"), (pkx,))
            cp("act", FC[:, cc, :], psx[:], (pkx,), ("FC",))
            psb_, pkb = bank()
            for c in range(NCH):
                mm(psb_[:], wb1[:, c, 256 + cc * 128:256 + (cc + 1) * 128], H[:, c, :], c == 0, c == NCH - 1, (wb1k, "H"), (pkb,))
            cp("act", FD[:, cc, :], psb_[:], (pkb,), ("FD",))
            psc, pkc = bank()
            for c in range(NCH):
                mm(psc[:], wb2[:, c, cc * 128:(cc + 1) * 128], H[:, c, :], c == 0, c == NCH - 1, (wb2k, "H"), (pkc,))
            tt("dve", FA[:, cc, 2:514], psc[:], FC[:, cc, :], ALU.mult, (pkc, "FC"), ("FA",))
            tsc("dve", FC[:, cc, :], FA[:, cc, 2:514], bcw[:, cc, 2:3], bcb[:, cc:cc + 1], ALU.mult, ALU.add,
                ("FA", "bcw", "bcb"), ("FC",))
            stt("dve", FC[:, cc, :], FA[:, cc, 1:513], bcw[:, cc, 1:2], FC[:, cc, :], ALU.mult, ALU.add, ("FA", "FC", "bcw"), ("FC",))
            stt("dve", FC[:, cc, :], FA[:, cc, 0:512], bcw[:, cc, 0:1], FC[:, cc, :], ALU.mult, ALU.add, ("FA", "FC", "bcw"), ("FC",))
            tt("dve", S1[:, 4 + cc, :], FC[:, cc, :], FD[:, cc, :], ALU.mult, ("FC", "FD"), ("S1",))
            if lastt:
                dma("sp", conv_b_prompt[l, :, cc * 128:(cc + 1) * 128].rearrange("t c -> c t"), FA[:, cc, 512:514],
                    ("FA",), (), isout=True, slow=True)
        branch_out(cx, l, 1, "w_b_out", 2, 4, False)

        wc, wck = wload(wview(w_in, 0, 8, 2304, 512), (8, 512))
        for cc in range(2):
            if i == 0:
                T.op("dve", lambda e, cc=cc: e.memset(FB[:, cc, 0:30], 0.0), w=("FB",))
            else:
                cp("dve", FB[:, cc, 0:30], FB[:, cc, 512:542], ("FB",), ("FB",))
            psg, pkg = bank()
            for c in range(NCH):
                mm(psg[:], wc[:, c, 256 + cc * 128:256 + (cc + 1) * 128], H[:, c, :], c == 0, c == NCH - 1, (wck, "H"), (pkg,))
            act(FC[:, cc, :], psg[:], AF.Sigmoid, (pkg,), ("FC",))
            psa, pka = bank()
            for c in range(NCH):
                mm(psa[:], wc[:, c, cc * 128:(cc + 1) * 128], H[:, c, :], c == 0, c == NCH - 1, (wck, "H"), (pka,))
            tt("dve", FB[:, cc, 30:542], psa[:], FC[:, cc, :], ALU.mult, (pka, "FC"), ("FB",))
            tsc("dve", FC[:, cc, :], FB[:, cc, 30:542], ccw[:, cc, 30:31], ccb[:, cc:cc + 1], ALU.mult, ALU.add,
                ("FB", "ccw", "ccb"), ("FC",))
            for k in range(30):
                stt("dve", FC[:, cc, :], FB[:, cc, k:k + 512], ccw[:, cc, k:k + 1], FC[:, cc, :], ALU.mult, ALU.add,
                    ("FB", "FC", "ccw"), ("FC",))
            if lastt:
                pst, pkt = bank()
                tp(pst[0:30, 0:128], FB[:, cc, 512:542], identf[:], ("FB", "identf"), (pkt,))
                cp("act", T1[0:30, cc * 128:(cc + 1) * 128], pst[0:30, 0:128], (pkt,), ("T1",))
        if lastt:
            dma("sp", conv_c_prompt[l], T1[0:30, 0:256], ("T1",), (), isout=True)
        for cc in range(2):
            cp("act", S1[:, cc, :], FC[:, cc, :], ("FC",), ("S1",))
            act(S1[:, 2 + cc, :], FC[:, cc, :], AF.Square, ("FC",), ("S1",))
        psm, pkm = bank()
        for cc in range(2):
            mm(psm[:], ones_c[:], S1[:, cc, :], cc == 0, cc == 1, ("ones_c", "S1"), (pkm,))
        psq, pkq = bank()
        for cc in range(2):
            mm(psq[:], ones_c[:], S1[:, 2 + cc, :], cc == 0, cc == 1, ("ones_c", "S1"), (pkq,))
        cp("act", FD[:, 0, :], psm[:], (pkm,), ("FD",))
        tt("dve", FD[:, 1, :], FD[:, 0, :], FD[:, 0, :], ALU.mult, ("FD",), ("FD",))
        tt("dve", FD[:, 1, :], psq[:], FD[:, 1, :], ALU.subtract, (pkq, "FD"), ("FD",))
        rstd(FD[:, 1, :], FD[:, 1, :], 1.0, ("FD",), ("FD",))
        for cc in range(2):
            tt("dve", FC[:, cc, :], FC[:, cc, :], FD[:, 0, :], ALU.subtract, ("FC", "FD"), ("FC",))
            tt("dve", FC[:, cc, :], FC[:, cc, :], FD[:, 1, :], ALU.mult, ("FC", "FD"), ("FC",))
            act(S1[:, 6 + cc, :], FC[:, cc, :], AF.Silu, ("FC", "clg", "clb"), ("S1",), bias=clb[:, cc:cc + 1],
                scale=clg[:, cc:cc + 1])
        branch_out(cx, l, 2, "w_c_out", 2, 6, False)

        wd, wdk = wload(wview(w_in, 0, 8, 2816, 512), (8, 512))
        for cc in range(2):
            ps, pk = bank()
            for c in range(NCH):
                mm(ps[:], wd[:, c, cc * 128:(cc + 1) * 128], H[:, c, :], c == 0, c == NCH - 1, (wdk, "H"), (pk,))
            gelu_from_psum(ps[:], FD[:, cc, :], FC[:, cc, :], "FC", pk, "FD")
        psd = [bank(), bank()]
        for s in range(4):
            tok = slice(s * 128, (s + 1) * 128)
            ps, pk = bank()
            for c in range(NCH):
                mm(ps[:, 0:256], H[:, c, tok], wd[:, c, 256:512], c == 0, c == NCH - 1, ("H", wdk), (pk,))
            gelu_from_psum(ps[:, 0:256], T1[:, 0:256], T1[:, 256:512], "T1", pk, "T1")
            T.op("dve", lambda e: e.bn_stats(ST[:, 16:22], T1[:, 0:256]), r=("T1",), w=("ST",))
            T.op("dve", lambda e: e.bn_aggr(ST[:, 22:24], ST[:, 16:22]), r=("ST",), w=("ST",))
            rstd(ST[:, 23:24], ST[:, 23:24], 1.0, ("ST",), ("ST",))
            tsc("dve", T1[:, 0:256], T1[:, 0:256], ST[:, 22:23], ST[:, 23:24], ALU.subtract, ALU.mult, ("T1", "ST"), ("T1",))
            tt("dve", T1[:, 0:256], T1[:, 0:256], dlg_b[:], ALU.mult, ("T1", "dlg_b"), ("T1",))
            tt("dve", TB[:, 0:256], T1[:, 0:256], dlb_b[:], ALU.add, ("T1", "dlb_b"), ("TB",))
            for g in range(4):
                cc, gg = g // 2, g % 2
                pd, pdk = psd[cc]
                mm(pd[gg * 64:(gg + 1) * 64, tok], TB[:, g * 64:(g + 1) * 64], WT[:, g, :], True, True, ("TB", "WT"), (pdk,))
        for cc in range(2):
            pd, pdk = psd[cc]
            tt("dve", FC[:, cc, :].rearrange("p (s t) -> p s t", s=4), pd[:].rearrange("p (s t) -> p s t", s=4),
               bsT[:, cc, None, :].to_broadcast([128, 4, 128]), ALU.add, (pdk, "bsT"), ("FC",))
            tt("dve", S1[:, cc, :], FC[:, cc, :], FD[:, cc, :], ALU.mult, ("FC", "FD"), ("S1",))
        branch_out(cx, l, 3, "w_d_out", 2, 0, False)
        merge_proj(cx, l)

        rmsnorm_fm(cx, g_x, "g_x")
        wxq, wxqk = wload(wview(W["w_xq"][l], 0, 8, 0, 512), (8, 512))
        for h in range(4):
            psq_, pkq_ = bank()
            for c in range(NCH):
                mm(psq_[:], wxq[:, c, h * 128:(h + 1) * 128], H[:, c, :], c == 0, c == NCH - 1, (wxqk, "H"), (pkq_,))
            act(S1[:, 4, :], psq_[:], AF.Square, (pkq_,), ("S1",))
            psm_, pkm_ = bank()
            mm(psm_[:], ones_h[:], S1[:, 4, :], True, True, ("ones_h", "S1"), (pkm_,))
            rstd(RS[:], psm_[:], 1.0, (pkm_,), ("RS",))
            stt("dve", qT[:, h, :], psq_[:], gxq[:, 0:1], RS[:], ALU.mult, ALU.mult, (pkq_, "gxq", "RS"), ("qT",))
            ptb, ptk = PT[0], ("PT", 0)
            p2 = ptb[:].rearrange("p a b -> p (a b)")
            for kc in range(2):
                pss, pks = bank()
                mm(pss[:], mkT[:, h, kc * 128:(kc + 1) * 128], qT[:, h, :], True, True, ("mkT", "qT"), (pks,))
                act(p2[:, kc * 512:(kc + 1) * 512], pss[:], AF.Exp, (pks,), (ptk,), scale=128.0 ** -0.5)
            po, pok = bank()
            pl, plk = bank()
            for kc in range(2):
                mm(po[:], mvb[:, kc, h * 128:(h + 1) * 128], p2[:, kc * 512:(kc + 1) * 512], kc == 0, kc == 1, ("mvb", ptk), (pok,))
            for kc in range(2):
                mm(pl[:], ones_1[:], p2[:, kc * 512:(kc + 1) * 512], kc == 0, kc == 1, ("ones_1", ptk), (plk,))
            recip(RS[:], pl[:], (plk,), ("RS",))
            tt("dve", S1[:, h, :], po[:], RS[:], ALU.mult, (pok, "RS"), ("S1",))
        xo_proj(cx, l)
        ffn(cx, l, MG.rearrange("p a b -> p (a b)").bitcast(BF16), lambda j: ("MG", j // 2))
        store_x_tile(l, i)

    def tps_to_fm(src_bf, srck, nchunk, dst_base):
        ps, pk = bank()
        psb = ps[:].bitcast(BF16)
        for j in range(nchunk):
            tp(psb[:, j * NS:(j + 1) * NS], src_bf[0:NS, j * 128:(j + 1) * 128], identb[0:NS, 0:NS], (srck, "identb"), (pk,))
        for j in range(nchunk):
            cp("act", S1s[:, dst_base + j, 0:NS], psb[:, j * NS:(j + 1) * NS], (pk,), ("S1s",))

    def tm_linear(wt, wtk, ncol):
        ps, pk = bank()
        for c in range(NCH):
            mm(ps[0:NS, 0:ncol], Hs[:, c, 0:NS], wt[:, c, 0:ncol], c == 0, c == NCH - 1, ("Hs", wtk), (pk,))
        return ps, pk

    def sample_layer(l):
        cx = CS
        w_in = W["w_in"][l]
        QK = PT[0][:].rearrange("p a b -> p (a b)").bitcast(F32)
        VX = PT[1][:].rearrange("p a b -> p (a b)").bitcast(F32)
        q_s, k_s, v_s = QK[0:NS, 0:512], QK[0:NS, 512:1024], VX[0:NS, 0:512]
        kQ, kV = ("PT", 0), ("PT", 1)
        if l == 0:
            dma("sp", T1[0:NS, :], x_sample[:, 0:512], (), ("T1",))
            dma("sp", T2[0:NS, :], x_sample[:, 512:1024], (), ("T2",))
            ps, pk = bank()
            for half, src, sk in ((0, T1, "T1"), (1, T2, "T2")):
                for j in range(4):
                    c = half * 4 + j
                    tp(ps[:, c * NS:(c + 1) * NS], src[0:NS, j * 128:(j + 1) * 128], identf[0:NS, 0:NS], (sk, "identf"), (pk,))
            for c in range(NCH):
                cp("act", xsT[:, c, :], ps[:, c * NS:(c + 1) * NS], (pk,), ("xsT",))
        rmsnorm_fm(cx, g_mix, "g_mix")

        wq, wqk = wload(wview(w_in, 0, 8, 0, 512), (8, 512))
        wk_, wkk = wload(wview(w_in, 0, 8, 512, 512), (8, 512))
        wv, wvk = wload(wview(w_in, 0, 8, 1024, 512), (8, 512))
        ps, pk = tm_linear(wq, wqk, 512)
        cp("act", q_s, ps[0:NS, :], (pk,), (kQ,))
        ps, pk = tm_linear(wk_, wkk, 512)
        cp("act", k_s, ps[0:NS, :], (pk,), (kQ,))
        ps, pk = tm_linear(wv, wvk, 512)
        cp("act", v_s, ps[0:NS, :], (pk,), (kV,))
        v3 = qknorm_tm(q_s, kQ, NS, 8, 64, T3[0:NS, :], "T3", 0)
        tt("dve", v3, v3, gq_b[0:NS, None, :].to_broadcast([NS, 8, 64]), ALU.mult, (kQ, "gq_b"), (kQ,))
        v3 = qknorm_tm(k_s, kQ, NS, 8, 64, T3[0:NS, :], "T3", 0)
        tt("dve", v3, v3, gk_b[0:NS, None, :].to_broadcast([NS, 8, 64]), ALU.mult, (kQ, "gk_b"), (kQ,))
        dma("sp", k_a_sample[l], k_s, (kQ,), (), isout=True)
        dma("sp", v_a_sample[l], v_s, (kV,), (), isout=True)
        dma("sp", qs_dram, q_s, (kQ,), ("qs_dram",))

        if KDBG <= 1:
            return
        PU = BPG * NPG
        KCs = [SCR[0:PU, 0:2048], SCR[0:PU, 2048:4096]]
        VCs = [SCR[0:PU, 4096:6144], SCR[0:PU, 6272:8320]]
        KKs = [MGK[0:4], MGK[4:8]]
        VKs = [("FA", "FB"), ("FC", "FD")]
        for g in range(NG):
            for bl in range(BPG):
                dma("sp", T2[bl * NPG:(bl + 1) * NPG, :], qs_dram[g * BPG + bl].partition_broadcast(NPG), ("qs_dram",), ("T2",))
            po, pok = bank()
            pl, plk = bank()
            for ch in range(NCHK):
                bi_ = ch % 2
                KC, VC, KK, VK = KCs[bi_], VCs[bi_], KKs[bi_], VKs[bi_]
                eoff = (l * NCHK + ch) * NPOOL * 2048
                T.dma_sw(lambda e, eoff=eoff, g=g, KC=KC: e.indirect_dma_start(
                    out=KC, out_offset=None, in_=ck,
                    in_offset=bass.IndirectOffsetOnAxis(ap=PTI[0:PU, g:g + 1], axis=0), element_offset=eoff), ("PTI",), KK, ("KC", bi_))
                T.dma_sw(lambda e, eoff=eoff, g=g, VC=VC: e.indirect_dma_start(
                    out=VC, out_offset=None, in_=cv,
                    in_offset=bass.IndirectOffsetOnAxis(ap=PTI[0:PU, g:g + 1], axis=0), element_offset=eoff), ("PTI",), VK, ("VC", bi_))
                K3 = KC.rearrange("p (r d) -> p r d", r=4)
                tt("dve", K3, K3, T2[0:PU, None, :].to_broadcast([PU, 4, 512]), ALU.mult, KK + ("T2",), KK)
                red(Ssc[0:PU].rearrange("p r c -> p (r c)"), KC.rearrange("p (a d) -> p a d", d=64), KK, ("Ssc",))
                sv4 = Ssc[0:PU].rearrange("p r (h c) -> p r h c", c=2)
                tt("dve", sv4, sv4,
                   sbias[0:PU, ch * 16:(ch + 1) * 16].rearrange("p (r h) -> p r h", h=4)[:, :, :, None].to_broadcast([PU, 4, 4, 2]),
                   ALU.add, ("Ssc", "sbias"), ("Ssc",))
                for bl in range(BPG):
                    rows = slice(bl * NPG, (bl + 1) * NPG)
                    act(Pz[rows, :, bl * 8:(bl + 1) * 8], Ssc[rows, :, :], AF.Exp, ("Ssc",), ("Pz",))
                V3 = VC.rearrange("p (r d) -> p r d", r=4)
                for r4 in range(4):
                    mm(po[0:NCOL, :], Pz[0:PU, r4, :], V3[:, r4, :], ch == 0 and r4 == 0, ch == NCHK - 1 and r4 == 3,
                       ("Pz",) + VK, (pok,))
                red(Pzs[0:PU, :], Pz[0:PU].rearrange("p r c -> p c r"), ("Pz",), ("Pzs",))
                mm(pl[0:NCOL, 0:2], Pzs[0:PU, :], ones_f[0:PU, 0:2], ch == 0, ch == NCHK - 1, ("Pzs", "ones_f"), (plk,))
            cp("act", T3[0:NCOL, :], po[0:NCOL, :], (pok,), ("T3",))
            cp("act", ST[0:NCOL, 30:31], pl[0:NCOL, 0:1], (plk,), ("ST",))
            dma("sp", pa_dram[g * NCOL:(g + 1) * NCOL, 0:512], T3[0:NCOL, :], ("T3",), ("pa_dram",))
            dma("sp", pa_dram[g * NCOL:(g + 1) * NCOL, 512:513], ST[0:NCOL, 30:31], ("ST",), ("pa_dram",), slow=True)

        if KDBG <= 2:
            return
        wb1, wb1k = wload(wview(w_in, 0, 8, 1536, 512), (8, 512))
        wb2, wb2k = wload(wview(w_in, 0, 8, 2048, 256), (8, 256))
        ps, pk = tm_linear(wb1, wb1k, 512)
        cp("act", T1[0:NS, :], ps[0:NS, :], (pk,), ("T1",))
        ps, pk = tm_linear(wb2, wb2k, 256)
        tt("dve", T2[0:NS, 0:256], ps[0:NS, 0:256], T1[0:NS, 0:256], ALU.mult, (pk, "T1"), ("T2",))
        dma("sp", T3[0:NS, :], state_b[l], (), ("T3",))
        WB = SCR[0:NS, 0:768]
        BB = SCR[0:NS, 768:1024]
        Y = SCR[0:NS, 1024:1280]
        TM_ = SCR[0:NS, 1280:1536]
        dma("sp", WB, W["b_conv_w"][l].rearrange("k c -> (k c)").partition_broadcast(NS), (), MGK)
        dma("sp", BB, W["b_conv_b"][l].partition_broadcast(NS), (), MGK)
        tt("dve", Y, T2[0:NS, 0:256], WB[:, 512:768], ALU.mult, ("T2",) + MGK, MGK)
        tt("dve", Y, Y, BB, ALU.add, MGK, MGK)
        tt("dve", TM_, T3[0:NS, 256:512], WB[:, 256:512], ALU.mult, ("T3",) + MGK, MGK)
        tt("dve", Y, Y, TM_, ALU.add, MGK, MGK)
        tt("dve", TM_, T3[0:NS, 0:256], WB[:, 0:256], ALU.mult, ("T3",) + MGK, MGK)
        tt("dve", Y, Y, TM_, ALU.add, MGK, MGK)
        tt("dve", TB[0:NS, 0:256], Y, T1[0:NS, 256:512], ALU.mult, ("T1",) + MGK, ("TB",))
        dma("sp", conv_b_sample[l][:, 0:256], T3[0:NS, 256:512], ("T3",), (), isout=True)
        dma("sp", conv_b_sample[l][:, 256:512], T2[0:NS, 0:256], ("T2",), (), isout=True)
        tps_to_fm(TB, "TB", 2, 4)
        branch_out(cx, l, 1, "w_b_out", 2, 4, True)

        if KDBG <= 3:
            return
        wc, wck = wload(wview(w_in, 0, 8, 2304, 512), (8, 512))
        ps, pk = tm_linear(wc, wck, 512)
        act(T1[0:NS, 256:512], ps[0:NS, 256:512], AF.Sigmoid, (pk,), ("T1",))
        UC = T2[0:NS, 256:512]
        tt("dve", UC, ps[0:NS, 0:256], T1[0:NS, 256:512], ALU.mult, (pk, "T1"), ("T2",))
        W30 = SCR[0:NS, 1536:1792]
        CB = SCR[0:NS, 1792:2048]
        YC = SCR[0:NS, 2048:2304]
        TC = SCR[0:NS, 2304:2560]
        dma("sp", W30, W["c_conv_w"][l][30].partition_broadcast(NS), (), MGK)
        dma("sp", CB, W["c_conv_b"][l].partition_broadcast(NS), (), MGK)
        tt("dve", YC, UC, W30, ALU.mult, ("T2",) + MGK, MGK)
        tt("dve", YC, YC, CB, ALU.add, MGK, MGK)
        WCK = SCR[0:NS, 4096:4096 + 1536]
        PCK = SCR[0:NS, 6144:6144 + 1536]
        for k5 in range(5):
            dma("sp", WCK, W["c_conv_w"][l][k5 * 6:(k5 + 1) * 6].rearrange("k c -> (k c)").partition_broadcast(NS), (), FK)
            dma("sp", PCK, state_c[l][:, k5 * 1536:(k5 + 1) * 1536], (), FK)
            tt("dve", PCK, PCK, WCK, ALU.mult, FK, FK)
            red(TC, PCK.rearrange("p (k c) -> p c k", k=6), FK, MGK)
            tt("dve", YC, YC, TC, ALU.add, MGK, MGK)
        dma("sp", conv_c_sample[l][:, 0:29 * 256], state_c[l][:, 256:30 * 256], (), (), isout=True)
        dma("sp", conv_c_sample[l][:, 29 * 256:30 * 256], UC, ("T2",), (), isout=True)
        LG = SCR[0:NS, 2560:2816]
        LB = SCR[0:NS, 2816:3072]
        dma("sp", LG, W["c_ln_g"][l].partition_broadcast(NS), (), MGK)
        dma("sp", LB, W["c_ln_b"][l].partition_broadcast(NS), (), MGK)
        T.op("dve", lambda e: e.bn_stats(ST[0:NS, 16:22], YC), r=MGK, w=("ST",))
        T.op("dve", lambda e: e.bn_aggr(ST[0:NS, 22:24], ST[0:NS, 16:22]), r=("ST",), w=("ST",))
        rstd(ST[0:NS, 23:24], ST[0:NS, 23:24], 1.0, ("ST",), ("ST",))
        tsc("dve", YC, YC, ST[0:NS, 22:23], ST[0:NS, 23:24], ALU.subtract, ALU.mult, MGK + ("ST",), MGK)
        tt("dve", YC, YC, LG, ALU.mult, MGK, MGK)
        tt("dve", YC, YC, LB, ALU.add, MGK, MGK)
        act(TB[0:NS, 0:256], YC, AF.Silu, MGK, ("TB",))
        tps_to_fm(TB, "TB", 2, 6)
        branch_out(cx, l, 2, "w_c_out", 2, 6, False)

        if KDBG <= 4:
            return
        wd, wdk = wload(wview(w_in, 0, 8, 2816, 512), (8, 512))
        ps, pk = tm_linear(wd, wdk, 512)
        gelu_from_psum(ps[0:NS, :], T1[0:NS, :], T3[0:NS, :], "T3", pk, "T1")
        T.op("dve", lambda e: e.bn_stats(ST[0:NS, 16:22], T1[0:NS, 256:512]), r=("T1",), w=("ST",))
        T.op("dve", lambda e: e.bn_aggr(ST[0:NS, 22:24], ST[0:NS, 16:22]), r=("ST",), w=("ST",))
        rstd(ST[0:NS, 23:24], ST[0:NS, 23:24], 1.0, ("ST",), ("ST",))
        DV = T3[0:NS, 0:256]
        DG = T3[0:NS, 256:512]
        tsc("dve", DV, T1[0:NS, 256:512], ST[0:NS, 22:23], ST[0:NS, 23:24], ALU.subtract, ALU.mult, ("T1", "ST"), ("T3",))
        tt("dve", DV, DV, dlg_b[0:NS, :], ALU.mult, ("T3", "dlg_b"), ("T3",))
        tt("dve", DV, DV, dlb_b[0:NS, :], ALU.add, ("T3", "dlb_b"), ("T3",))
        dma("sp", d_v_sample[l], DV, ("T3",), (), isout=True)
        DV3 = DV.rearrange("p (g c) -> p g c", g=4)
        DG3 = DG.rearrange("p (g c) -> p g c", g=4)
        tt("dve", DG3, DV3, w00[0:NS, 0:4, None].to_broadcast([NS, 4, 64]), ALU.mult, ("T3", "w00"), ("T3",))
        tt("dve", DG3, DG3, w00[0:NS, 4:8, None].to_broadcast([NS, 4, 64]), ALU.add, ("T3", "w00"), ("T3",))
        tt("dve", TB[0:NS, 0:256], DG, T1[0:NS, 0:256], ALU.mult, ("T3", "T1"), ("TB",))
        tps_to_fm(TB, "TB", 2, 0)
        branch_out(cx, l, 3, "w_d_out", 2, 0, False)

        if KDBG <= 5:
            return
        ARd = qT[:].rearrange("p a b -> p (a b)").bitcast(F32)[0:NS, :].rearrange("p (a d) -> p a d", a=8)
        aro = pa_dram.rearrange("(b a) w -> b a w", a=8)
        for hc in range(8):
            h = hc // 2
            dma("sp", ARd[:, hc, :], aro[:, hc, h * 128:(h + 1) * 128], ("pa_dram",), ("qT",))
        dma("sp", ST[0:NS, 32:40], aro[:, :, 512], ("pa_dram",), ("ST",), slow=True)
        tt("dve", T2[0:NS, :], q_s, k_s, ALU.mult, (kQ,), ("T2",))
        red(ST[0:NS, 40:48], T2[0:NS, :].rearrange("p (a d) -> p a d", a=8), ("T2",), ("ST",))
        act(ST[0:NS, 40:48], ST[0:NS, 40:48], AF.Exp, ("ST",), ("ST",))
        tt("dve", ST[0:NS, 32:40], ST[0:NS, 32:40], ST[0:NS, 40:48], ALU.add, ("ST",), ("ST",))
        for hc in range(8):
            h = hc // 2
            stt("dve", ARd[:, hc, :], v_s[:, h * 128:(h + 1) * 128], ST[0:NS, 40 + hc:41 + hc], ARd[:, hc, :], ALU.mult, ALU.add,
                (kV, "ST", "qT"), ("qT",))
        recip(ST[0:NS, 32:40], ST[0:NS, 32:40], ("ST",), ("ST",))
        for h in range(4):
            tt("dve", ST[0:NS, 33 + 2 * h:34 + 2 * h], ST[0:NS, 33 + 2 * h:34 + 2 * h], lam_s[0:NS, 4:5], ALU.mult, ("ST", "lam_s"), ("ST",))
            oh = T2[0:NS, h * 128:(h + 1) * 128]
            tsc("dve", oh, ARd[:, 2 * h, :], ST[0:NS, 32 + 2 * h:33 + 2 * h], None, ALU.mult, None, ("qT", "ST"), ("T2",))
            stt("dve", oh, ARd[:, 2 * h + 1, :], ST[0:NS, 33 + 2 * h:34 + 2 * h], oh, ALU.mult, ALU.add, ("qT", "ST", "T2"), ("T2",))
        if KDBG <= 5.2:
            return
        v3 = qknorm_tm(T2[0:NS, :], "T2", NS, 4, 128, T3[0:NS, :], "T3", 48)
        tt("dve", TB[0:NS, :].rearrange("p (h d) -> p h d", h=4), v3, gsub_b[0:NS, None, :].to_broadcast([NS, 4, 128]),
           ALU.mult, ("T2", "gsub_b"), ("TB",))
        if KDBG <= 5.4:
            return
        tps_to_fm(TB, "TB", 4, 0)
        if KDBG <= 5.5:
            return
        branch_out(cx, l, 0, "w_a_out", 4, 0, False)
        if KDBG <= 5.7:
            return
        merge_proj(cx, l)

        if KDBG <= 6:
            return
        rmsnorm_fm(cx, g_x, "g_x")
        wxq, wxqk = wload(wview(W["w_xq"][l], 0, 8, 0, 512), (8, 512))
        ps, pk = tm_linear(wxq, wxqk, 512)
        cp("act", T1[0:NS, :], ps[0:NS, :], (pk,), ("T1",))
        v3 = qknorm_tm(T1[0:NS, :], "T1", NS, 4, 128, T3[0:NS, :], "T3", 0)
        GX = SCR[0:NS, 3072:3200]
        dma("sp", GX, W["x_qnorm_g"][l].partition_broadcast(NS), (), MGK)
        tt("dve", v3, v3, GX[:, None, :].to_broadcast([NS, 4, 128]), ALU.mult, ("T1",) + MGK, ("T1",))
        dma("sp", qx_dram, T1[0:NS, :], ("T1",), ("qx_dram",))
        MK = SCR[:, 4096:5120]
        MV = SCR[:, 5120:6144]
        for b in range(NS):
            dma("sp", T2[:], qx_dram[b].partition_broadcast(128), ("qx_dram",), ("T2",))
            dma("sp", MK.rearrange("p (k d) -> p k d", k=2), cmk[l, b].rearrange("(k p) d -> p k d", p=128), (), ("FA",))
            dma("sp", MV.rearrange("p (k d) -> p k d", k=2), cmv[l, b].rearrange("(k p) d -> p k d", p=128), (), ("FB",))
            M3 = MK.rearrange("p (k d) -> p k d", k=2)
            tt("dve", M3, M3, T2[:, None, :].to_broadcast([128, 2, 512]), ALU.mult, ("FA", "T2"), ("FA",))
            red(Pc[:, b, :], MK.rearrange("p (a d) -> p a d", d=128), ("FA",), ("Pc",))
            act(Pc[:, b, :], Pc[:, b, :], AF.Exp, ("Pc",), ("Pc",), scale=128.0 ** -0.5)
            po, pok = bank()
            for kc in range(2):
                mm(po[0:4, :], Pc[:, b, kc * 4:(kc + 1) * 4], MV[:, kc * 512:(kc + 1) * 512], kc == 0, kc == 1, ("Pc", "FB"), (pok,))
            cp("act", T3[0:4, :], po[0:4, :], (pok,), ("T3",))
            dma("sp", om_dram[b], T3[0:4, :], ("T3",), ("om_dram",))
        pl, plk = bank()
        mm(pl[0:1, 0:NS * 8], ones_f[:, 0:1], Pc[:].rearrange("p a b -> p (a b)"), True, True, ("ones_f", "Pc"), (plk,))
        cp("act", T3[0:1, 0:NS * 8], pl[0:1, 0:NS * 8], (plk,), ("T3",))
        dma("sp", ol_dram, T3[0:1, 0:NS * 8], ("T3",), ("ol_dram",))
        for h in range(4):
            dma("sp", T1[0:NS, h * 128:(h + 1) * 128], om_dram[:, h, h * 128:(h + 1) * 128], ("om_dram",), ("T1",))
        dma("sp", ST[0:NS, 48:56], ol_dram.rearrange("o (b a) -> (o b) a", a=8), ("ol_dram",), ("ST",))
        tt("dve", ST[0:NS, 48:52], ST[0:NS, 48:52], ST[0:NS, 52:56], ALU.add, ("ST",), ("ST",))
        recip(ST[0:NS, 48:52], ST[0:NS, 48:52], ("ST",), ("ST",))
        tt("dve", TB[0:NS, :].rearrange("p (h d) -> p h d", h=4), T1[0:NS, :].rearrange("p (h d) -> p h d", h=4),
           ST[0:NS, 48:52, None].to_broadcast([NS, 4, 128]), ALU.mult, ("T1", "ST"), ("TB",))
        tps_to_fm(TB, "TB", 4, 0)
        xo_proj(cx, l)
        ffn(cx, l, S1[:, 0:2, :].rearrange("p a b -> p (a b)"), lambda j: "S1")
        if l == L - 1:
            for half, dst, dk in ((0, T1, "T1"), (1, T2, "T2")):
                ps, pk = bank()
                for j in range(4):
                    tp(ps[0:NS, j * 128:(j + 1) * 128], xsT[:, half * 4 + j, 0:NS], identf[:], ("xsT", "identf"), (pk,))
                cp("act", dst[0:NS, :], ps[0:NS, :], (pk,), (dk,))
                dma("sp", y_sample[:, half * 512:(half + 1) * 512], dst[0:NS, :], (dk,), (), isout=True)

    for l in range(L):
        T.new_epoch()
        load_layer_params(l)
        mem_layer(l)
        if with_sample:
            sample_layer(l)
        for i in range(NT):
            T.new_epoch()
            tile_layer(l, i)
    T.finish()
    with nc.Block() as block:
        T.emit(block)
    stack.close()
    return nc


_CACHE = {}


def _consts():
    ident = np.eye(128, dtype=np.float32)
    kk = np.arange(128)[:, None]
    qq = np.arange(128)[None, :]
    mask = np.where(kk > qq, -1e30, 0.0).astype(np.float32)
    tril = (qq <= kk).astype(np.float32)
    slopes = np.array([2.0 ** (-8.0 * (i + 1) / 4) for i in range(4)], dtype=np.float64)
    p = np.arange(128)[:, None, None]
    dl = np.arange(16)[None, None, :]
    alibi = (slopes[None, :, None] * (p - 128.0 * dl)).astype(np.float32).reshape(128, 64)
    return dict(c_ident=ident, c_mask=mask, c_tril=tril, c_alibi=alibi), slopes


WEIGHT_KEYS = ["norm_mix_g", "w_in", "a_qnorm_g", "a_knorm_g", "a_lam", "a_subln_g", "w_a_out", "b_conv_w",
               "b_conv_b", "w_b_out", "c_conv_w", "c_conv_b", "c_ln_g", "c_ln_b", "w_c_out", "d_ln_g", "d_ln_b",
               "d_ws", "d_bs", "w_d_out", "w_o", "norm_x_g", "mem_norm_g", "w_xq", "w_xk", "w_xv", "x_qnorm_g",
               "x_knorm_g", "w_xo", "norm_ffn_g", "w_up", "w_down"]


def kernel(**inp):
    x_prompt = np.asarray(inp["x_prompt"])
    B, SEQ, _ = x_prompt.shape
    L = np.asarray(inp["w_in"]).shape[0]
    page_table = np.ascontiguousarray(np.asarray(inp["page_table"]).astype(np.int32))
    NSG, NPG = page_table.shape
    NSL = NSG // NCORES
    cache_k = np.asarray(inp["cache_k_a"])
    cache_v = np.asarray(inp["cache_v_a"])
    NPOOL, PAGE_TOK = cache_k.shape[1], cache_k.shape[2]
    NCHK = PAGE_TOK // 4
    past_len = NPG * PAGE_TOK
    key = (SEQ, NPG, NPOOL, L, NSL, PAGE_TOK)
    if key not in _CACHE:
        _CACHE[key] = build(SEQ, NPG, NPOOL, L, NSL, PAGE_TOK)
    nc = _CACHE[key]
    consts, slopes = _consts()
    shared = {k: np.asarray(inp[k]) for k in WEIGHT_KEYS}
    shared["ck"] = np.ascontiguousarray(
        cache_k.reshape(L, NPOOL, NCHK, 2048).transpose(0, 2, 1, 3)).reshape(L * NCHK * NPOOL, 2048)
    shared["cv"] = np.ascontiguousarray(
        cache_v.reshape(L, NPOOL, NCHK, 2048).transpose(0, 2, 1, 3)).reshape(L * NCHK * NPOOL, 2048)
    BPG = max(1, min(NSL, 128 // NPG))
    j = (np.arange(128) % NPG)[:, None, None]
    t = np.arange(PAGE_TOK)[None, :, None]
    dist = past_len - (PAGE_TOK * j + t)
    shared["c_sbias"] = (-slopes[None, None, :] * dist).astype(np.float32).reshape(128, PAGE_TOK * 4)
    x_sample = np.asarray(inp["x_sample"]).reshape(NSG, D)
    state_b = np.asarray(inp["state_conv_b"]).reshape(L, NSG, 512)
    state_c = np.asarray(inp["state_conv_c"]).reshape(L, NSG, 30 * 256)
    cmk = np.asarray(inp["cache_mem_k"]).reshape(L, NSG, 256, 512)
    cmv = np.asarray(inp["cache_mem_v"]).reshape(L, NSG, 256, 512)
    mem_prompt = np.asarray(inp["mem_prompt"])
    in_maps = []
    for c in range(NCORES):
        m = dict(consts)
        m.update(shared)
        sl = slice(c * NSL, (c + 1) * NSL)
        m["x_prompt"] = np.ascontiguousarray(x_prompt[c])
        m["mem_prompt"] = np.ascontiguousarray(mem_prompt[c])
        m["x_sample"] = np.ascontiguousarray(x_sample[sl])
        m["state_conv_b"] = np.ascontiguousarray(state_b[:, sl])
        m["state_conv_c"] = np.ascontiguousarray(state_c[:, sl])
        m["cache_mem_k"] = np.ascontiguousarray(cmk[:, sl])
        m["cache_mem_v"] = np.ascontiguousarray(cmv[:, sl])
        m["page_table"] = np.ascontiguousarray(page_table[sl])
        in_maps.append(m)
    if inp.get("_prepare_only"):
        return nc, in_maps
    res = run_bass_kernel_spmd(nc, in_maps, core_ids=list(range(NCORES))).results
    y_prompt = np.stack([r["y_prompt"] for r in res], 0)
    kp = np.stack([r["k_a_prompt"] for r in res], 1).reshape(L, B, SEQ, 4, 128)
    vp = np.stack([r["v_a_prompt"] for r in res], 1).reshape(L, B, SEQ, 4, 128)
    cbp = np.stack([r["conv_b_prompt"] for r in res], 1)
    ccp = np.stack([r["conv_c_prompt"] for r in res], 1)
    mkp = np.stack([r["mem_k_prompt"] for r in res], 1).reshape(L, B, 256, 4, 128)
    mvp = np.stack([r["mem_v_prompt"] for r in res], 1).reshape(L, B, 256, 4, 128)
    ys = np.concatenate([r["y_sample"] for r in res], 0).reshape(NSG, 1, D)
    ks = np.concatenate([r["k_a_sample"] for r in res], 1).reshape(L, NSG, 1, 4, 128)
    vs = np.concatenate([r["v_a_sample"] for r in res], 1).reshape(L, NSG, 1, 4, 128)
    cbs = np.concatenate([r["conv_b_sample"] for r in res], 1).reshape(L, NSG, 2, 256)
    ccs = np.concatenate([r["conv_c_sample"] for r in res], 1).reshape(L, NSG, 30, 256)
    dvs = np.concatenate([r["d_v_sample"] for r in res], 1).reshape(L, NSG, 1, 256)
    return (y_prompt, ys, kp, vp, cbp, ccp, mkp, mvp, ks, vs, cbs, ccs, dvs)
```

```python
import contextlib
import math
import numpy as np
import concourse.bass as bass
import concourse.mybir as mybir
from concourse.bass_utils import run_bass_kernel_spmd

F32 = mybir.dt.float32
BF16 = mybir.dt.bfloat16
I32 = mybir.dt.int32
AF = mybir.ActivationFunctionType
ALU = mybir.AluOpType
AX = mybir.AxisListType

D = 1024
NCH = 8
INC = 7424
DFF = 4096
EPS = 1e-6
NCORES = 8
TT = 512
NEPOCH = 12
import os
KDBG = float(os.environ.get('KDBG', '99'))
SW_CLEAR = os.environ.get('SW_CLEAR', '0') == '1'


class _Op:
    __slots__ = ("eng", "fn", "deps", "sig", "kind", "sem", "val", "ep", "clr", "pre")


class Tr:
    def __init__(self, nc, stack, ndma=32):
        self.nc = nc
        self.names = ["pe", "act", "dve", "pool", "sp"]
        self.ops = {k: [] for k in self.names}
        self.lastw = {}
        self.readers = {}
        self.esem = {k: [stack.enter_context(nc.semaphore("s_%s%d" % (k, i))) for i in range(NEPOCH)]
                     for k in self.names if k != "sp"}
        self.esem["sp"] = [stack.enter_context(nc.semaphore("s_sp"))] * NEPOCH
        self.dsem = [stack.enter_context(nc.semaphore("d%d" % i)) for i in range(ndma)]
        self.dval = [0] * ndma
        self.dlast = [None] * ndma
        self.dnext = 0
        self.outs = []
        self.epoch = 0
        self.stack = stack
        self.bufsem = {}

    def new_epoch(self):
        self.epoch += 1
        assert self.epoch < NEPOCH, "too many epochs"

    def _mk(self, eng, fn, kind, r, w, extra=()):
        op = _Op()
        op.eng, op.fn, op.kind, op.sig, op.sem, op.val, op.ep = eng, fn, kind, False, None, 0, self.epoch
        op.clr, op.pre = False, None
        deps = list(extra)
        for k in r:
            lw = self.lastw.get(k)
            if lw is not None:
                deps.append(lw)
        for k in w:
            lw = self.lastw.get(k)
            if lw is not None:
                deps.append(lw)
            deps.extend(self.readers.get(k, {}).values())
        keep = []
        for d in deps:
            if d is op or d in keep:
                continue
            if d.kind == "c" and kind == "c" and d.eng == eng and eng == "pe":
                continue
            d.sig = True
            keep.append(d)
        op.deps = keep
        for k in w:
            self.lastw[k] = op
            self.readers[k] = {}
        for k in r:
            rk = (eng, id(op)) if kind == "d" else eng
            self.readers.setdefault(k, {})[rk] = op
        self.ops[eng].append(op)
        return op

    def op(self, eng, fn, r=(), w=()):
        return self._mk(eng, fn, "c", r, w)

    def dma(self, q, fn, r=(), w=(), out=False):
        k = self.dnext
        self.dnext = (self.dnext + 1) % len(self.dsem)
        extra = [self.dlast[k]] if self.dlast[k] is not None else []
        op = self._mk(q, fn, "d", r, w, extra)
        self.dval[k] += 16
        op.sem, op.val = self.dsem[k], self.dval[k]
        self.dlast[k] = op
        if out:
            self.outs.append(op)
        return op

    def dma_sw(self, fn, r, w, bufkey):
        if bufkey not in self.bufsem:
            i = len(self.bufsem)
            self.bufsem[bufkey] = [[self.stack.enter_context(self.nc.semaphore("b%d_%d" % (i, j))) for j in range(2)], 0]
        sems, n = self.bufsem[bufkey]
        self.bufsem[bufkey][1] = n + 1
        op = self._mk("pool", fn, "d", r, w)
        if SW_CLEAR:
            op.sem, op.val, op.clr = sems[n % 2], 16, True
            op.pre = sems[(n - 1) % 2] if n >= 1 else None
        else:
            op.sem, op.val = sems[0], 16 * (n + 1)
        return op

    def finish(self):
        return self._mk("sp", None, "c", (), (), list(self.outs))

    def emit(self, block):
        for k in self.names:
            c = {}
            for op in self.ops[k]:
                if op.kind == "c" and op.sig:
                    c[op.ep] = c.get(op.ep, 0) + 1
                    op.val = c[op.ep]
        tr = self

        def body(name):
            def run(e):
                known = {}
                for op in tr.ops[name]:
                    waits = {}
                    for d in op.deps:
                        sem = d.sem if d.kind == "d" else tr.esem[d.eng][d.ep]
                        key = id(sem)
                        if d.clr:
                            if known.get(key) is d:
                                continue
                            waits[key] = (sem, 16)
                            known[key] = d
                            continue
                        if known.get(key, 0) >= d.val:
                            continue
                        if key not in waits or waits[key][1] < d.val:
                            waits[key] = (sem, d.val)
                    for key, (sem, val) in waits.items():
                        e.wait_ge(sem, val)
                        if not isinstance(known.get(key), _Op):
                            known[key] = val
                    if op.pre is not None:
                        e.sem_clear(op.pre)
                    if op.fn is None:
                        continue
                    ins = op.fn(e)
                    if op.kind == "d":
                        ins.then_inc(op.sem, 16)
                    elif op.sig:
                        ins.then_inc(tr.esem[name][op.ep], 1)
            return run

        block.tensor(body("pe"))
        block.scalar(body("act"))
        block.vector(body("dve"))
        block.gpsimd(body("pool"))
        block.sync(body("sp"))


class Cx:
    pass


def build(SEQ, NPG, NPOOL, L=2, NS=4, PAGE_TOK=128, with_sample=True):
    NT = SEQ // TT
    NKT = SEQ // 128
    NCHK = PAGE_TOK // 4
    BPG = max(1, min(NS, 128 // NPG))
    NG = (NS + BPG - 1) // BPG
    NCOL = BPG * 8
    if with_sample:
        assert NS % BPG == 0 and (BPG == 1 or NPG % 32 == 0) and BPG * NPG <= 128
    nc = bass.Bass("TRN2", target_bir_lowering=False)
    stack = contextlib.ExitStack()

    def din(name, shape, dt=F32):
        return nc.dram_tensor(name, list(shape), dt, kind="ExternalInput").ap()

    def dout(name, shape, dt=F32):
        return nc.dram_tensor(name, list(shape), dt, kind="ExternalOutput").ap()

    x_prompt = din("x_prompt", [SEQ, D])
    mem_prompt = din("mem_prompt", [256, D])
    c_ident = din("c_ident", [128, 128])
    c_mask = din("c_mask", [128, 128])
    c_tril = din("c_tril", [128, 128])
    c_alibi = din("c_alibi", [128, 64])
    W = {}
    wshapes = dict(
        norm_mix_g=[L, D], w_in=[L, D, INC], a_qnorm_g=[L, 64], a_knorm_g=[L, 64], a_lam=[L, 4, 64],
        a_subln_g=[L, 128], w_a_out=[L, 512, D], b_conv_w=[L, 3, 256], b_conv_b=[L, 256],
        w_b_out=[L, 256, D], c_conv_w=[L, 31, 256], c_conv_b=[L, 256], c_ln_g=[L, 256], c_ln_b=[L, 256],
        w_c_out=[L, 256, D], d_ln_g=[L, 256], d_ln_b=[L, 256], d_ws=[L, 4, 128, 128], d_bs=[L, 4, 128],
        w_d_out=[L, 256, D], w_o=[L, D, D], norm_x_g=[L, D], mem_norm_g=[L, D], w_xq=[L, D, 512],
        w_xk=[L, D, 512], w_xv=[L, D, 512], x_qnorm_g=[L, 128], x_knorm_g=[L, 128], w_xo=[L, 512, D],
        norm_ffn_g=[L, D], w_up=[L, D, DFF], w_down=[L, DFF, D])
    for k, s in wshapes.items():
        W[k] = din(k, s)

    y_prompt = dout("y_prompt", [SEQ, D])
    k_a_prompt = dout("k_a_prompt", [L, SEQ, 512])
    v_a_prompt = dout("v_a_prompt", [L, SEQ, 512])
    conv_b_prompt = dout("conv_b_prompt", [L, 2, 256])
    conv_c_prompt = dout("conv_c_prompt", [L, 30, 256])
    mem_k_prompt = dout("mem_k_prompt", [L, 256, 512])
    mem_v_prompt = dout("mem_v_prompt", [L, 256, 512])
    xs_dram = nc.dram_tensor("xs_scratch", [128, NCH, SEQ], F32).ap()
    if with_sample:
        x_sample = din("x_sample", [NS, D])
        ck = din("ck", [L * NCHK * NPOOL, 2048])
        cv = din("cv", [L * NCHK * NPOOL, 2048])
        state_b = din("state_conv_b", [L, NS, 512])
        state_c = din("state_conv_c", [L, NS, 30 * 256])
        cmk = din("cache_mem_k", [L, NS, 256, 512])
        cmv = din("cache_mem_v", [L, NS, 256, 512])
        page_table = din("page_table", [NS, NPG], I32)
        c_sbias = din("c_sbias", [128, NCHK * 16])
        y_sample = dout("y_sample", [NS, D])
        k_a_sample = dout("k_a_sample", [L, NS, 512])
        v_a_sample = dout("v_a_sample", [L, NS, 512])
        conv_b_sample = dout("conv_b_sample", [L, NS, 512])
        conv_c_sample = dout("conv_c_sample", [L, NS, 30 * 256])
        d_v_sample = dout("d_v_sample", [L, NS, 256])
        qs_dram = nc.dram_tensor("qs_scratch", [NS, 512], F32).ap()
        qx_dram = nc.dram_tensor("qx_scratch", [NS, 512], F32).ap()
        om_dram = nc.dram_tensor("om_scratch", [NS, 4, 512], F32).ap()
        ol_dram = nc.dram_tensor("ol_scratch", [1, NS * 8], F32).ap()
        pa_dram = nc.dram_tensor("pa_scratch", [NG * NCOL, 516], F32).ap()

    T = Tr(nc, stack)

    def sb(name, shape, dt):
        return stack.enter_context(nc.sbuf_tensor(name, list(shape), dt))

    identf = sb("identf", [128, 128], F32)
    identb = sb("identb", [128, 128], BF16)
    maskf = sb("maskf", [128, 128], F32)
    maskb = sb("maskb", [128, 128], BF16)
    trilf = sb("trilf", [128, 128], F32)
    alibi = sb("alibi", [128, 64], F32)
    ones_dm = sb("ones_dm", [128, 128], BF16)
    ones_c = sb("ones_c", [128, 128], BF16)
    ones_h = sb("ones_h", [128, 128], BF16)
    ones_1 = sb("ones_1", [128, 128], BF16)
    ones_f = sb("ones_f", [128, 2], F32)
    g_mix = sb("g_mix", [128, NCH], F32)
    g_x = sb("g_x", [128, NCH], F32)
    g_ffn = sb("g_ffn", [128, NCH], F32)
    g_mem = sb("g_mem", [128, NCH], F32)
    gq_b = sb("gq_b", [128, 64], F32)
    gk_b = sb("gk_b", [128, 64], F32)
    gsub_b = sb("gsub_b", [128, 128], F32)
    lam_b = sb("lam_b", [128, 256], F32)
    lam_t = sb("lam_t", [128, 128], F32)
    lam_s = sb("lam_s", [128, 8], F32)
    bcw = sb("bcw", [128, 2, 3], F32)
    bcb = sb("bcb", [128, 2], F32)
    ccw = sb("ccw", [128, 2, 31], F32)
    ccb = sb("ccb", [128, 2], F32)
    clg = sb("clg", [128, 2], F32)
    clb = sb("clb", [128, 2], F32)
    dlg_b = sb("dlg_b", [128, 256], F32)
    dlb_b = sb("dlb_b", [128, 256], F32)
    bsT = sb("bsT", [128, 2, 128], F32)
    wsf = sb("wsf", [128, 4, 128], F32)
    wsb = sb("wsb", [128, 4, 128], BF16)
    WT = sb("WT", [128, 4, 128], BF16)
    gxq = sb("gxq", [128, 1], F32)
    gxk_b = sb("gxk_b", [128, 128], F32)
    mkT = sb("mkT", [128, 4, 256], BF16)
    mvb = sb("mvb", [128, 2, 512], BF16)
    memTg = sb("memTg", [128, NCH, 256], BF16)
    mrstd = sb("mrstd", [128, 2], F32)
    kT = sb("kT", [128, 4, SEQ], BF16)
    Vx = sb("Vx", [128, NKT, 4, 132], BF16)
    xT = sb("xT", [128, NCH, TT], F32)
    H = sb("H", [128, NCH, TT], BF16)
    S1 = sb("S1", [128, NCH, TT], BF16)
    RS = sb("RS", [128, TT], F32)
    SCR = sb("SCR", [128, 8448], F32)
    MG = SCR[:, 0:4096].rearrange("p (a b) -> p a b", a=8)
    FA = SCR[:, 4096:5184].rearrange("p (a b) -> p a b", a=2)
    FB = SCR[:, 5184:6272].rearrange("p (a b) -> p a b", a=2)
    FC = SCR[:, 6272:7296].rearrange("p (a b) -> p a b", a=2)
    FD = SCR[:, 7296:8320].rearrange("p (a b) -> p a b", a=2)
    MGK = tuple(("MG", f) for f in range(8))
    FK = ("FA", "FB", "FC", "FD")
    T1 = sb("T1", [128, 512], F32)
    T2 = sb("T2", [128, 512], F32)
    T3 = sb("T3", [128, 512], F32)
    TB = sb("TB", [128, 512], BF16)
    qT = sb("qT", [128, 4, TT], BF16)
    PT = [sb("PT%d" % i, [128, 16, 128], BF16) for i in range(2)]
    SG = [sb("SG%d" % i, [128, TT], F32) for i in range(2)]
    ST = sb("ST", [128, 64], F32)
    NSLOT = 4
    WS = [sb("WS%d" % i, [128, 4096], BF16) for i in range(NSLOT)]
    PS = [stack.enter_context(nc.psum_tensor("ps%d" % i, [128, 512], F32)) for i in range(8)]
    if with_sample:
        xsT = sb("xsT", [128, NCH, NS], F32)
        Hs = sb("Hs", [128, NCH, NS], BF16)
        S1s = sb("S1s", [128, NCH, NS], BF16)
        MGs = sb("MGs", [128, NCH, NS], F32)
        RSs = sb("RSs", [128, NS], F32)
        SGs = [sb("SGs%d" % i, [128, NS], F32) for i in range(2)]
        Ssc = sb("Ssc", [128, 4, 8], F32)
        Pz = sb("Pz", [128, 4, NCOL], F32)
        Pzs = sb("Pzs", [128, NCOL], F32)
        PTI = sb("PTI", [128, NG], I32)
        sbias = sb("sbias", [128, NCHK * 16], F32)
        Pc = sb("Pc", [128, NS, 8], F32)
        w00 = sb("w00", [128, 8], F32)

    st = dict(bank=0, slot=0, pt=0, sg=0, sgs=0)

    def bank():
        b = st["bank"]
        st["bank"] = (b + 1) % 8
        return PS[b], ("ps", b)

    CP = Cx()
    CP.xT, CP.H, CP.S1, CP.MG, CP.RS, CP.N = xT, H, S1, MG, RS, TT
    CP.kx, CP.kh, CP.ks1, CP.krs, CP.kmg = "xT", "H", "S1", "RS", "MG"
    CP.SG, CP.sgk, CP.sgi = SG, "SG", "sg"
    if with_sample:
        CS = Cx()
        CS.xT, CS.H, CS.S1, CS.MG, CS.RS, CS.N = xsT, Hs, S1s, MGs, RSs, NS
        CS.kx, CS.kh, CS.ks1, CS.krs, CS.kmg = "xsT", "Hs", "S1s", "RSs", "MGs"
        CS.SG, CS.sgk, CS.sgi = SGs, "SGs", "sgs"

    def sgbuf(cx):
        i = st[cx.sgi]
        st[cx.sgi] = 1 - i
        return cx.SG[i], (cx.sgk, i)

    def mm(out, lhsT, rhs, start, stop, r, w):
        T.op("pe", lambda e, o=out, a=lhsT, b=rhs, s0=start, s1=stop: e.matmul(o, a, b, start=s0, stop=s1), r=r, w=w)

    def tp(out, in_, ident, r, w):
        T.op("pe", lambda e, o=out, a=in_, i=ident: e.transpose(o, a, i), r=r, w=w)

    def act(out, in_, func, r, w, bias=None, scale=None):
        def f(e, o=out, a=in_, fn=func, b=bias, s=scale):
            kw = {}
            if b is not None:
                kw["bias"] = b
            if s is not None:
                kw["scale"] = s
            return e.activation(o, a, fn, **kw)
        T.op("act", f, r=r, w=w)

    def tsc(eng, out, in0, s1, s2, op0, op1, r, w):
        if op1 is None:
            T.op(eng, lambda e, o=out, a=in0, x=s1, p=op0: e.tensor_scalar(o, a, x, 0.0, p, ALU.add), r=r, w=w)
        else:
            T.op(eng, lambda e, o=out, a=in0, x=s1, y=s2, p=op0, q=op1: e.tensor_scalar(o, a, x, y, p, q), r=r, w=w)

    def rstd(out, in_, scale, r, w):
        act(out, in_, AF.Sqrt, r, w, bias=EPS, scale=scale)
        T.op("dve", lambda e, o=out: e.reciprocal(o, o), r=w, w=w)

    def recip(out, in_, r, w):
        T.op("dve", lambda e, o=out, a=in_: e.reciprocal(o, a), r=r, w=w)

    def red(out, in_, r, w):
        T.op("dve", lambda e, o=out, a=in_: e.tensor_reduce(o, a, AX.X, ALU.add), r=r, w=w)

    def stt(eng, out, in0, sc, in1, op0, op1, r, w):
        T.op(eng, lambda e, o=out, a=in0, s=sc, b=in1, p=op0, q=op1: e.scalar_tensor_tensor(o, a, s, b, p, q), r=r, w=w)

    def tt(eng, out, in0, in1, op, r, w):
        T.op(eng, lambda e, o=out, a=in0, b=in1, p=op: e.tensor_tensor(o, a, b, p), r=r, w=w)

    def cp(eng, out, in_, r, w):
        if eng == "act":
            T.op("act", lambda e, o=out, a=in_: e.copy(o, a), r=r, w=w)
        else:
            T.op(eng, lambda e, o=out, a=in_: e.tensor_copy(o, a), r=r, w=w)

    def dma(q, out, in_, r, w, isout=False, slow=False):
        if slow:
            T.dma(q, lambda e, o=out, a=in_: e.dma_start(out=o, in_=a, allow_slow_non_contiguous=True), r=r, w=w, out=isout)
        else:
            T.dma(q, lambda e, o=out, a=in_: e.dma_start(out=o, in_=a), r=r, w=w, out=isout)

    def wload(src_ap, shape3):
        i = st["slot"]
        st["slot"] = (i + 1) % NSLOT
        a, b = shape3
        dst = WS[i][:, 0:a * b].rearrange("p (a b) -> p a b", a=a)
        T.dma_sw(lambda e, o=dst, s=src_ap: e.dma_start(out=o, in_=s), (), (("WS", i),), ("WS", i))
        return dst, ("WS", i)

    def wview(w2d, r0, nr, c0, ncol):
        return w2d[r0 * 128:(r0 + nr) * 128, c0:c0 + ncol].rearrange("(k p) n -> p k n", p=128)

    dma("sp", identf[:], c_ident, (), ("identf",))
    dma("sp", maskf[:], c_mask, (), ("maskf",))
    dma("sp", trilf[:], c_tril, (), ("trilf",))
    dma("sp", alibi[:], c_alibi, (), ("alibi",))
    cp("dve", identb[:], identf[:], ("identf",), ("identb",))
    cp("dve", maskb[:], maskf[:], ("maskf",), ("maskb",))
    T.op("pool", lambda e: e.memset(ones_dm[:], 1.0 / D), w=("ones_dm",))
    T.op("pool", lambda e: e.memset(ones_c[:], 1.0 / 256), w=("ones_c",))
    T.op("pool", lambda e: e.memset(ones_h[:], 1.0 / 128), w=("ones_h",))
    T.op("pool", lambda e: e.memset(ones_1[:], 1.0), w=("ones_1",))
    T.op("pool", lambda e: e.memset(ones_f[:], 1.0), w=("ones_f",))
    T.op("pool", lambda e: e.memset(Vx[:], 1.0), w=("Vx",))
    if with_sample:
        T.op("pool", lambda e: e.memset(Pz[:], 0.0), w=("Pz",))
        dma("sp", sbias[:], c_sbias, (), ("sbias",))
        T.op("pool", lambda e: e.memset(PTI[:], 0), w=("PTI",))
        for g in range(NG):
            dma("sp", PTI[0:BPG * NPG, g:g + 1], page_table[g * BPG:(g + 1) * BPG, :].rearrange("b (j o) -> (b j) o", o=1),
                (), ("PTI",), slow=True)

    def rmsnorm_fm(cx, gt, gkey):
        N = cx.N
        act(cx.S1[:, :, 0:N], cx.xT[:, :, 0:N], AF.Square, (cx.kx,), (cx.ks1,))
        ps, pk = bank()
        for c in range(NCH):
            mm(ps[:, 0:N], ones_dm[:], cx.S1[:, c, 0:N], c == 0, c == NCH - 1, ("ones_dm", cx.ks1), (pk,))
        rstd(cx.RS[:, 0:N], ps[:, 0:N], 1.0, (pk,), (cx.krs,))
        for c in range(NCH):
            stt("dve", cx.H[:, c, 0:N], cx.xT[:, c, 0:N], gt[:, c:c + 1], cx.RS[:, 0:N], ALU.mult, ALU.mult,
                (cx.kx, cx.krs, gkey), (cx.kh,))

    def load_layer_params(l):
        def fm(dst, src, key):
            dma("sp", dst[:], src.rearrange("(c p) -> p c", p=128), (), (key,), slow=True)
        fm(g_mix, W["norm_mix_g"][l], "g_mix")
        fm(g_x, W["norm_x_g"][l], "g_x")
        fm(g_ffn, W["norm_ffn_g"][l], "g_ffn")
        fm(g_mem, W["mem_norm_g"][l], "g_mem")
        fm(bcb, W["b_conv_b"][l], "bcb")
        fm(ccb, W["c_conv_b"][l], "ccb")
        fm(clg, W["c_ln_g"][l], "clg")
        fm(clb, W["c_ln_b"][l], "clb")
        for cc in range(2):
            dma("sp", bcw[:, cc, :], W["b_conv_w"][l][:, cc * 128:(cc + 1) * 128].rearrange("k p -> p k"), (), ("bcw",), slow=True)
            dma("sp", ccw[:, cc, :], W["c_conv_w"][l][:, cc * 128:(cc + 1) * 128].rearrange("k p -> p k"), (), ("ccw",), slow=True)
        dma("sp", gxq[:], W["x_qnorm_g"][l].rearrange("(p o) -> p o", o=1), (), ("gxq",), slow=True)

        def bc(dst, src, key):
            dma("sp", dst[:], src.partition_broadcast(128), (), (key,))
        bc(gq_b, W["a_qnorm_g"][l], "gq_b")
        bc(gk_b, W["a_knorm_g"][l], "gk_b")
        bc(gsub_b, W["a_subln_g"][l], "gsub_b")
        bc(lam_b, W["a_lam"][l].rearrange("a b -> (a b)"), "lam_b")
        bc(dlg_b, W["d_ln_g"][l], "dlg_b")
        bc(dlb_b, W["d_ln_b"][l], "dlb_b")
        bc(gxk_b, W["x_knorm_g"][l], "gxk_b")
        for g in range(4):
            cc, gg = g // 2, g % 2
            dma("sp", bsT[gg * 64:(gg + 1) * 64, cc, :], W["d_bs"][l, g].partition_broadcast(64), (), ("bsT",))
        dma("sp", wsf[:], W["d_ws"][l].rearrange("g t s -> t g s"), (), ("wsf",))
        if with_sample:
            dma("sp", w00[:, 0:4], W["d_ws"][l][:, 0, 0].partition_broadcast(128), (), ("w00",), slow=True)
            dma("sp", w00[:, 4:8], W["d_bs"][l][:, 0].partition_broadcast(128), (), ("w00",), slow=True)
        lam_init = 0.8 - 0.6 * math.exp(-0.3 * l)
        tsc("dve", gq_b[:], gq_b[:], 0.125, None, ALU.mult, None, ("gq_b",), ("gq_b",))
        tsc("dve", gsub_b[:], gsub_b[:], 1.0 - lam_init, None, ALU.mult, None, ("gsub_b",), ("gsub_b",))
        tt("dve", lam_t[:, 0:64], lam_b[:, 0:64], lam_b[:, 64:128], ALU.mult, ("lam_b",), ("lam_t",))
        tt("dve", lam_t[:, 64:128], lam_b[:, 128:192], lam_b[:, 192:256], ALU.mult, ("lam_b",), ("lam_t",))
        red(lam_s[:, 0:2], lam_t[:, 0:128].rearrange("p (a b) -> p a b", a=2), ("lam_t",), ("lam_s",))
        act(lam_s[:, 2:4], lam_s[:, 0:2], AF.Exp, ("lam_s",), ("lam_s",))
        tt("dve", lam_s[:, 4:5], lam_s[:, 3:4], lam_s[:, 2:3], ALU.subtract, ("lam_s",), ("lam_s",))
        tsc("dve", lam_s[:, 4:5], lam_s[:, 4:5], -lam_init, None, ALU.add, None, ("lam_s",), ("lam_s",))
        tt("dve", wsb[:], wsf[:], trilf[:, None, :].to_broadcast([128, 4, 128]), ALU.mult, ("wsf", "trilf"), ("wsb",))
        ps, pk = bank()
        psb = ps[:].bitcast(BF16)
        for g in range(4):
            tp(psb[:, g * 128:(g + 1) * 128], wsb[:, g, :], identb[:], ("wsb", "identb"), (pk,))
        cp("dve", WT[:].rearrange("p g t -> p (g t)"), psb[:, 0:512], (pk,), ("WT",))

    def mem_layer(l):
        for tcn in range(2):
            dma("sp", T1[:], mem_prompt[tcn * 128:(tcn + 1) * 128, 0:512], (), ("T1",))
            dma("sp", T2[:], mem_prompt[tcn * 128:(tcn + 1) * 128, 512:1024], (), ("T2",))
            act(T3[:], T1[:], AF.Square, ("T1",), ("T3",))
            red(ST[:, 0:1], T3[:], ("T3",), ("ST",))
            act(T3[:], T2[:], AF.Square, ("T2",), ("T3",))
            red(ST[:, 1:2], T3[:], ("T3",), ("ST",))
            tt("dve", ST[:, 2:3], ST[:, 0:1], ST[:, 1:2], ALU.add, ("ST",), ("ST",))
            rstd(mrstd[:, tcn:tcn + 1], ST[:, 2:3], 1.0 / D, ("ST",), ("mrstd",))
            for half, src, sk in ((0, T1, "T1"), (1, T2, "T2")):
                cp("dve", TB[:], src[:], (sk,), ("TB",))
                ps, pk = bank()
                psb = ps[:].bitcast(BF16)
                for j in range(4):
                    tp(psb[:, j * 128:(j + 1) * 128], TB[:, j * 128:(j + 1) * 128], identb[:], ("TB", "identb"), (pk,))
                for j in range(4):
                    c = half * 4 + j
                    tsc("dve", memTg[:, c, tcn * 128:(tcn + 1) * 128], psb[:, j * 128:(j + 1) * 128],
                        g_mem[:, c:c + 1], None, ALU.mult, None, (pk, "g_mem"), ("memTg",))
        wk, wkk = wload(wview(W["w_xk"][l], 0, 8, 0, 512), (8, 512))
        wv, wvk = wload(wview(W["w_xv"][l], 0, 8, 0, 512), (8, 512))
        for tcn in range(2):
            ps, pk = bank()
            for c in range(NCH):
                mm(ps[:], memTg[:, c, tcn * 128:(tcn + 1) * 128], wk[:, c, :], c == 0, c == NCH - 1, ("memTg", wkk), (pk,))
            act(T1[:], ps[:], AF.Copy, (pk, "mrstd"), ("T1",), scale=mrstd[:, tcn:tcn + 1])
            tt("dve", T3[:], T1[:], T1[:], ALU.mult, ("T1",), ("T3",))
            red(ST[:, 0:4], T3[:].rearrange("p (h d) -> p h d", h=4), ("T3",), ("ST",))
            rstd(ST[:, 0:4], ST[:, 0:4], 1.0 / 128, ("ST",), ("ST",))
            v3 = T1[:].rearrange("p (h d) -> p h d", h=4)
            tt("dve", v3, v3, ST[:, 0:4, None].to_broadcast([128, 4, 128]), ALU.mult, ("T1", "ST"), ("T1",))
            tt("dve", v3, v3, gxk_b[:, None, :].to_broadcast([128, 4, 128]), ALU.mult, ("T1", "gxk_b"), ("T1",))
            dma("sp", mem_k_prompt[l, tcn * 128:(tcn + 1) * 128, :], T1[:], ("T1",), (), isout=True)
            cp("dve", TB[:], T1[:], ("T1",), ("TB",))
            ps2, pk2 = bank()
            psb = ps2[:].bitcast(BF16)
            for h in range(4):
                tp(psb[:, h * 128:(h + 1) * 128], TB[:, h * 128:(h + 1) * 128], identb[:], ("TB", "identb"), (pk2,))
            for h in range(4):
                cp("dve", mkT[:, h, tcn * 128:(tcn + 1) * 128], psb[:, h * 128:(h + 1) * 128], (pk2,), ("mkT",))
            ps, pk = bank()
            for c in range(NCH):
                mm(ps[:], memTg[:, c, tcn * 128:(tcn + 1) * 128], wv[:, c, :], c == 0, c == NCH - 1, ("memTg", wvk), (pk,))
            act(T2[:], ps[:], AF.Copy, (pk, "mrstd"), ("T2",), scale=mrstd[:, tcn:tcn + 1])
            dma("sp", mem_v_prompt[l, tcn * 128:(tcn + 1) * 128, :], T2[:], ("T2",), (), isout=True)
            cp("dve", mvb[:, tcn, :], T2[:], ("T2",), ("mvb",))

    def load_x_tile(l, i):
        if l == 0:
            for s in range(4):
                r0 = i * TT + s * 128
                dma("sp", T1[:], x_prompt[r0:r0 + 128, 0:512], (), ("T1",))
                dma("sp", T2[:], x_prompt[r0:r0 + 128, 512:1024], (), ("T2",))
                for half, src, sk in ((0, T1, "T1"), (1, T2, "T2")):
                    ps, pk = bank()
                    for j in range(4):
                        tp(ps[:, j * 128:(j + 1) * 128], src[:, j * 128:(j + 1) * 128], identf[:], (sk, "identf"), (pk,))
                    for j in range(4):
                        cp("act", xT[:, half * 4 + j, s * 128:(s + 1) * 128], ps[:, j * 128:(j + 1) * 128], (pk,), ("xT",))
        else:
            dma("sp", xT[:], xs_dram[:, :, i * TT:(i + 1) * TT], (("xs", i),), ("xT",))

    def store_x_tile(l, i):
        if l < L - 1:
            dma("sp", xs_dram[:, :, i * TT:(i + 1) * TT], xT[:], ("xT",), (("xs", i),))
        else:
            for s in range(4):
                r0 = i * TT + s * 128
                for half, dst, dk in ((0, T1, "T1"), (1, T2, "T2")):
                    ps, pk = bank()
                    for j in range(4):
                        tp(ps[:, j * 128:(j + 1) * 128], xT[:, half * 4 + j, s * 128:(s + 1) * 128], identf[:],
                           ("xT", "identf"), (pk,))
                    cp("act", dst[:], ps[:], (pk,), (dk,))
                    dma("sp", y_prompt[r0:r0 + 128, half * 512:(half + 1) * 512], dst[:], (dk,), (), isout=True)

    def branch_out(cx, l, bi, wout_name, nk, s1_base, first):
        N = cx.N
        for half in range(2):
            gcol = 3328 + bi * D + half * 512
            wg, wgk = wload(wview(W["w_in"][l], 0, 8, gcol, 512), (8, 512))
            wo, wok = wload(wview(W[wout_name][l], 0, nk, half * 512, 512), (nk, 512))
            for fl in range(4):
                f = half * 4 + fl
                psy, pky = bank()
                for kc in range(nk):
                    mm(psy[:, 0:N], wo[:, kc, fl * 128:(fl + 1) * 128], cx.S1[:, s1_base + kc, 0:N], kc == 0, kc == nk - 1,
                       (wok, cx.ks1), (pky,))
                psg, pkg = bank()
                for c in range(NCH):
                    mm(psg[:, 0:N], wg[:, c, fl * 128:(fl + 1) * 128], cx.H[:, c, 0:N], c == 0, c == NCH - 1, (wgk, cx.kh), (pkg,))
                sg, sgk = sgbuf(cx)
                act(sg[:, 0:N], psg[:, 0:N], AF.Sigmoid, (pkg,), (sgk,))
                mk = (cx.kmg, f)
                if first:
                    tt("dve", cx.MG[:, f, 0:N], psy[:, 0:N], sg[:, 0:N], ALU.mult, (pky, sgk), (mk,))
                else:
                    tt("dve", sg[:, 0:N], psy[:, 0:N], sg[:, 0:N], ALU.mult, (pky, sgk), (sgk,))
                    tt("dve", cx.MG[:, f, 0:N], cx.MG[:, f, 0:N], sg[:, 0:N], ALU.add, (mk, sgk), (mk,))

    def merge_proj(cx, l):
        N = cx.N
        for f in range(NCH):
            cp("act", cx.H[:, f, 0:N], cx.MG[:, f, 0:N], ((cx.kmg, f),), (cx.kh,))
        if KDBG <= 5.8 and cx.N != TT:
            return
        for half in range(2):
            if KDBG <= 5.9 and cx.N != TT and half == 1:
                return
            if KDBG == 5.95 and cx.N != TT and half == 0:
                continue
            wo, wok = wload(wview(W["w_o"][l], 0, 8, half * 512, 512), (8, 512))
            for fl in range(4):
                f = half * 4 + fl
                ps, pk = bank()
                for c in range(NCH):
                    mm(ps[:, 0:N], wo[:, c, fl * 128:(fl + 1) * 128], cx.H[:, c, 0:N], c == 0, c == NCH - 1, (wok, cx.kh), (pk,))
                tt("dve", cx.xT[:, f, 0:N], cx.xT[:, f, 0:N], ps[:, 0:N], ALU.add, (cx.kx, pk), (cx.kx,))

    def xo_proj(cx, l):
        N = cx.N
        for half in range(2):
            wxo, wxok = wload(wview(W["w_xo"][l], 0, 4, half * 512, 512), (4, 512))
            for fl in range(4):
                f = half * 4 + fl
                ps, pk = bank()
                for kc in range(4):
                    mm(ps[:, 0:N], wxo[:, kc, fl * 128:(fl + 1) * 128], cx.S1[:, kc, 0:N], kc == 0, kc == 3, (wxok, cx.ks1), (pk,))
                tt("dve", cx.xT[:, f, 0:N], cx.xT[:, f, 0:N], ps[:, 0:N], ALU.add, (cx.kx, pk), (cx.kx,))

    def ffn(cx, l, Aflat, akeys):
        N = cx.N
        rmsnorm_fm(cx, g_ffn, "g_ffn")
        for half in range(2):
            for j4 in range(4):
                wu, wuk = wload(wview(W["w_up"][l], 0, 8, (half * 16 + j4 * 4) * 128, 512), (8, 512))
                for jj in range(4):
                    j = j4 * 4 + jj
                    ps, pk = bank()
                    for c in range(NCH):
                        mm(ps[:, 0:N], wu[:, c, jj * 128:(jj + 1) * 128], cx.H[:, c, 0:N], c == 0, c == NCH - 1, (wuk, cx.kh), (pk,))
                    sg, sgk = sgbuf(cx)
                    act(sg[:, 0:N], ps[:, 0:N], AF.Relu, (pk,), (sgk,))
                    act(Aflat[:, j * N:(j + 1) * N], sg[:, 0:N], AF.Square, (sgk,), (akeys(j),))
            for fp in range(4):
                wdn, wdnk = wload(wview(W["w_down"][l], half * 16, 16, fp * 256, 256), (16, 256))
                for fl in range(2):
                    f = fp * 2 + fl
                    ps, pk = bank()
                    for j in range(16):
                        mm(ps[:, 0:N], wdn[:, j, fl * 128:(fl + 1) * 128], Aflat[:, j * N:(j + 1) * N],
                           j == 0, j == 15, (wdnk, akeys(j)), (pk,))
                    tt("dve", cx.xT[:, f, 0:N], cx.xT[:, f, 0:N], ps[:, 0:N], ALU.add, (cx.kx, pk), (cx.kx,))

    def gelu_from_psum(ps_ap, out_ap, tmp, tmpk, pk, outk):
        act(tmp, ps_ap, AF.Square, (pk,), (tmpk,))
        tsc("dve", tmp, tmp, 0.044715, 1.0, ALU.mult, ALU.add, (tmpk,), (tmpk,))
        tt("dve", tmp, tmp, ps_ap, ALU.mult, (tmpk, pk), (tmpk,))
        act(tmp, tmp, AF.Sigmoid, (tmpk,), (tmpk,), scale=1.5957691216)
        tt("dve", out_ap, tmp, ps_ap, ALU.mult, (tmpk, pk), (outk,))

    def qknorm_tm(stage, sk, P, ngrp, gdim, sqbuf, sqk, stcol):
        act(sqbuf, stage, AF.Square, (sk,), (sqk,))
        red(ST[0:P, stcol:stcol + ngrp], sqbuf.rearrange("p (g d) -> p g d", g=ngrp), (sqk,), ("ST",))
        rstd(ST[0:P, stcol:stcol + ngrp], ST[0:P, stcol:stcol + ngrp], 1.0 / gdim, ("ST",), ("ST",))
        v3 = stage.rearrange("p (g d) -> p g d", g=ngrp)
        tt("dve", v3, v3, ST[0:P, stcol:stcol + ngrp, None].to_broadcast([P, ngrp, gdim]), ALU.mult, (sk, "ST"), (sk,))
        return v3

    def tile_layer(l, i):
        cx = CP
        lastt = (i == NT - 1)
        w_in = W["w_in"][l]
        load_x_tile(l, i)
        rmsnorm_fm(cx, g_mix, "g_mix")

        wq, wqk = wload(wview(w_in, 0, 8, 0, 512), (8, 512))
        wkk_, wkkk = wload(wview(w_in, 0, 8, 512, 512), (8, 512))
        wv, wvk = wload(wview(w_in, 0, 8, 1024, 512), (8, 512))
        for s in range(4):
            tok = slice(s * 128, (s + 1) * 128)
            jg = i * 4 + s
            r0 = i * TT + s * 128
            def qk_chain(which, wt, wtk, stage, sk, sq, sqk, stc, tb, tbk):
                ps, pk = bank()
                for c in range(NCH):
                    mm(ps[:], H[:, c, tok], wt[:, c, :], c == 0, c == NCH - 1, ("H", wtk), (pk,))
                yield
                cp("act", stage[:], ps[:], (pk,), (sk,))
                yield
                act(sq, stage[:], AF.Square, (sk,), (sqk,))
                yield
                red(ST[:, stc:stc + 8], sq.rearrange("p (g d) -> p g d", g=8), (sqk,), (("ST", stc),))
                yield
                act(ST[:, stc:stc + 8], ST[:, stc:stc + 8], AF.Sqrt, (("ST", stc),), (("ST", stc),), bias=EPS, scale=1.0 / 64)
                yield
                recip(ST[:, stc:stc + 8], ST[:, stc:stc + 8], (("ST", stc),), (("ST", stc),))
                yield
                v3 = stage[:].rearrange("p (g d) -> p g d", g=8)
                tt("dve", v3, v3, ST[:, stc:stc + 8, None].to_broadcast([128, 8, 64]), ALU.mult, (sk, ("ST", stc)), (sk,))
                yield
                if which == 0:
                    tt("dve", tb[:].rearrange("p (g d) -> p g d", g=8), v3, gq_b[:, None, :].to_broadcast([128, 8, 64]),
                       ALU.mult, (sk, "gq_b"), (tbk,))
                    yield
                else:
                    tt("dve", v3, v3, gk_b[:, None, :].to_broadcast([128, 8, 64]), ALU.mult, (sk, "gk_b"), (sk,))
                    yield
                    dma("sp", k_a_prompt[l, r0:r0 + 128, :], stage[:], (sk,), (), isout=True)
                    cp("dve", tb[:], stage[:], (sk,), (tbk,))
                    yield
                ps2, pk2 = bank()
                psb = ps2[:].bitcast(BF16)
                for h in range(4):
                    tp(psb[:, h * 128:(h + 1) * 128], tb[:, h * 128:(h + 1) * 128], identb[:], (tbk, "identb"), (pk2,))
                yield
                if which == 0:
                    cp("act", qT[:, :, tok], psb[:, 0:512].rearrange("p (h t) -> p h t", h=4), (pk2,), ("qT",))
                else:
                    cp("act", kT[:, :, r0:r0 + 128], psb[:, 0:512].rearrange("p (h t) -> p h t", h=4), (pk2,), ("kT",))
                yield

            SGb = SG[0][:].bitcast(BF16)
            gq_ = qk_chain(0, wq, wqk, T1, "T1", T3[:], "T3", 0, TB, "TB")
            gk_ = qk_chain(1, wkk_, wkkk, T2, "T2", SG[1][:], ("SG", 1), 56, SGb[:, 0:512], ("SG", 0))
            def v_chain():
                vst, vk = MG[:, 0, :], ("MG", 0)
                yield
                ps, pk = bank()
                for c in range(NCH):
                    mm(ps[:], H[:, c, tok], wv[:, c, :], c == 0, c == NCH - 1, ("H", wvk), (pk,))
                yield
                cp("act", vst, ps[:], (pk,), (vk,))
                yield
                dma("sp", v_a_prompt[l, r0:r0 + 128, :], vst, (vk,), (), isout=True)
                cp("dve", Vx[:, jg, :, 0:128], vst.rearrange("p (h d) -> p h d", h=4), (vk,), ("Vx",))
                yield
            live = [gq_, gk_, v_chain()]
            while live:
                for g_ in list(live):
                    try:
                        next(g_)
                    except StopIteration:
                        live.remove(g_)

        for s in range(4):
            tok = slice(s * 128, (s + 1) * 128)
            jg = i * 4 + s
            nk_t = jg + 1
            for h in range(4):
                pso = []
                pipelined = nk_t <= 8
                stage = []
                for c in range(2):
                    prow = slice(c * 64, (c + 1) * 64)
                    pi = st["pt"]
                    st["pt"] = 1 - pi
                    ptb, ptk = PT[pi], ("PT", pi)
                    sbanks = []
                    for kt in range(nk_t):
                        if kt % 4 == 0:
                            sbanks.append(bank())
                        pss, pks = sbanks[-1]
                        blk = slice((kt % 4) * 128, (kt % 4 + 1) * 128)
                        diag = (kt == jg)
                        mm(pss[:, blk], kT[prow, h, kt * 128:(kt + 1) * 128], qT[prow, h, tok], True, not diag,
                           ("kT", "qT"), (pks,))
                        if diag:
                            mm(pss[:, blk], identb[:], maskb[:], False, True, ("identb", "maskb"), (pks,))

                    def finish_c(ptb=ptb, ptk=ptk, sbanks=sbanks):
                        for kt in range(nk_t):
                            pss, pks = sbanks[kt // 4]
                            blk = slice((kt % 4) * 128, (kt % 4 + 1) * 128)
                            dl = jg - kt
                            act(ptb[:, kt, :], pss[:, blk], AF.Exp, (pks, "alibi"), (ptk,),
                                bias=alibi[:, h * 16 + dl:h * 16 + dl + 1], scale=1.0)
                        po, pok = bank()
                        for kt in range(nk_t):
                            mm(po[:, 0:129], ptb[:, kt, :], Vx[:, kt, h, 0:129], kt == 0, kt == nk_t - 1, (ptk, "Vx"), (pok,))
                        pso.append((po, pok))
                    if pipelined:
                        stage.append(finish_c)
                    else:
                        finish_c()
                for fn_ in stage:
                    fn_()
                (p0, k0), (p1, k1) = pso
                recip(ST[:, 8:9], p0[:, 128:129], (k0,), ("ST",))
                recip(ST[:, 9:10], p1[:, 128:129], (k1,), ("ST",))
                tt("dve", ST[:, 9:10], ST[:, 9:10], lam_s[:, 4:5], ALU.mult, ("ST", "lam_s"), ("ST",))
                tsc("dve", T1[:, 0:128], p0[:, 0:128], ST[:, 8:9], None, ALU.mult, None, (k0, "ST"), ("T1",))
                stt("dve", T1[:, 0:128], p1[:, 0:128], ST[:, 9:10], T1[:, 0:128], ALU.mult, ALU.add, (k1, "ST", "T1"), ("T1",))
                act(T1[:, 128:256], T1[:, 0:128], AF.Square, ("T1",), ("T1",))
                red(ST[:, 10:11], T1[:, 128:256], ("T1",), ("ST",))
                rstd(ST[:, 10:11], ST[:, 10:11], 1.0 / 128, ("ST",), ("ST",))
                stt("dve", TB[:, 0:128], T1[:, 0:128], ST[:, 10:11], gsub_b[:], ALU.mult, ALU.mult,
                    ("T1", "ST", "gsub_b"), ("TB",))
                pt_, ptk_ = bank()
                ptb_ = pt_[:].bitcast(BF16)
                tp(ptb_[:, 0:128], TB[:, 0:128], identb[:], ("TB", "identb"), (ptk_,))
                cp("act", S1[:, h, tok], ptb_[:, 0:128], (ptk_,), ("S1",))
        branch_out(cx, l, 0, "w_a_out", 4, 0, True)

        wb1, wb1k = wload(wview(w_in, 0, 8, 1536, 512), (8, 512))
        wb2, wb2k = wload(wview(w_in, 0, 8, 2048, 256), (8, 256))
        for cc in range(2):
            if i == 0:
                T.op("dve", lambda e, cc=cc: e.memset(FA[:, cc, 0:2], 0.0), w=("FA",))
            else:
                cp("dve", FA[:, cc, 0:2], FA[:, cc, 512:514], ("FA",), ("FA",))
            psx, pkx = bank()
            for c in range(NCH):
                mm(psx[:], wb1[:, c, cc * 128:(cc + 1) * 128], H[:, c, :], c == 0, c == NCH - 1, (wb1k, "H"), (pkx,))
            cp("act", FC[:, cc, :], psx[:], (pkx,), ("FC",))
            psb_, pkb = bank()
            for c in range(NCH):
                mm(psb_[:], wb1[:, c, 256 + cc * 128:256 + (cc + 1) * 128], H[:, c, :], c == 0, c == NCH - 1, (wb1k, "H"), (pkb,))
            cp("act", FD[:, cc, :], psb_[:], (pkb,), ("FD",))
            psc, pkc = bank()
            for c in range(NCH):
                mm(psc[:], wb2[:, c, cc * 128:(cc + 1) * 128], H[:, c, :], c == 0, c == NCH - 1, (wb2k, "H"), (pkc,))
            tt("dve", FA[:, cc, 2:514], psc[:], FC[:, cc, :], ALU.mult, (pkc, "FC"), ("FA",))
            tsc("dve", FC[:, cc, :], FA[:, cc, 2:514], bcw[:, cc, 2:3], bcb[:, cc:cc + 1], ALU.mult, ALU.add,
                ("FA", "bcw", "bcb"), ("FC",))
            stt("dve", FC[:, cc, :], FA[:, cc, 1:513], bcw[:, cc, 1:2], FC[:, cc, :], ALU.mult, ALU.add, ("FA", "FC", "bcw"), ("FC",))
            stt("dve", FC[:, cc, :], FA[:, cc, 0:512], bcw[:, cc, 0:1], FC[:, cc, :], ALU.mult, ALU.add, ("FA", "FC", "bcw"), ("FC",))
            tt("dve", S1[:, 4 + cc, :], FC[:, cc, :], FD[:, cc, :], ALU.mult, ("FC", "FD"), ("S1",))
            if lastt:
                dma("sp", conv_b_prompt[l, :, cc * 128:(cc + 1) * 128].rearrange("t c -> c t"), FA[:, cc, 512:514],
                    ("FA",), (), isout=True, slow=True)
        branch_out(cx, l, 1, "w_b_out", 2, 4, False)

        wc, wck = wload(wview(w_in, 0, 8, 2304, 512), (8, 512))
        for cc in range(2):
            if i == 0:
                T.op("dve", lambda e, cc=cc: e.memset(FB[:, cc, 0:30], 0.0), w=("FB",))
            else:
                cp("dve", FB[:, cc, 0:30], FB[:, cc, 512:542], ("FB",), ("FB",))
            psg, pkg = bank()
            for c in range(NCH):
                mm(psg[:], wc[:, c, 256 + cc * 128:256 + (cc + 1) * 128], H[:, c, :], c == 0, c == NCH - 1, (wck, "H"), (pkg,))
            act(FC[:, cc, :], psg[:], AF.Sigmoid, (pkg,), ("FC",))
            psa, pka = bank()
            for c in range(NCH):
                mm(psa[:], wc[:, c, cc * 128:(cc + 1) * 128], H[:, c, :], c == 0, c == NCH - 1, (wck, "H"), (pka,))
            tt("dve", FB[:, cc, 30:542], psa[:], FC[:, cc, :], ALU.mult, (pka, "FC"), ("FB",))
            tsc("dve", FC[:, cc, :], FB[:, cc, 30:542], ccw[:, cc, 30:31], ccb[:, cc:cc + 1], ALU.mult, ALU.add,
                ("FB", "ccw", "ccb"), ("FC",))
            for k in range(30):
                stt("dve", FC[:, cc, :], FB[:, cc, k:k + 512], ccw[:, cc, k:k + 1], FC[:, cc, :], ALU.mult, ALU.add,
                    ("FB", "FC", "ccw"), ("FC",))
            if lastt:
                pst, pkt = bank()
                tp(pst[0:30, 0:128], FB[:, cc, 512:542], identf[:], ("FB", "identf"), (pkt,))
                cp("act", T1[0:30, cc * 128:(cc + 1) * 128], pst[0:30, 0:128], (pkt,), ("T1",))
        if lastt:
            dma("sp", conv_c_prompt[l], T1[0:30, 0:256], ("T1",), (), isout=True)
        for cc in range(2):
            cp("act", S1[:, cc, :], FC[:, cc, :], ("FC",), ("S1",))
            act(S1[:, 2 + cc, :], FC[:, cc, :], AF.Square, ("FC",), ("S1",))
        psm, pkm = bank()
        for cc in range(2):
            mm(psm[:], ones_c[:], S1[:, cc, :], cc == 0, cc == 1, ("ones_c", "S1"), (pkm,))
        psq, pkq = bank()
        for cc in range(2):
            mm(psq[:], ones_c[:], S1[:, 2 + cc, :], cc == 0, cc == 1, ("ones_c", "S1"), (pkq,))
        cp("act", FD[:, 0, :], psm[:], (pkm,), ("FD",))
        tt("dve", FD[:, 1, :], FD[:, 0, :], FD[:, 0, :], ALU.mult, ("FD",), ("FD",))
        tt("dve", FD[:, 1, :], psq[:], FD[:, 1, :], ALU.subtract, (pkq, "FD"), ("FD",))
        rstd(FD[:, 1, :], FD[:, 1, :], 1.0, ("FD",), ("FD",))
        for cc in range(2):
            tt("dve", FC[:, cc, :], FC[:, cc, :], FD[:, 0, :], ALU.subtract, ("FC", "FD"), ("FC",))
            tt("dve", FC[:, cc, :], FC[:, cc, :], FD[:, 1, :], ALU.mult, ("FC", "FD"), ("FC",))
            act(S1[:, 6 + cc, :], FC[:, cc, :], AF.Silu, ("FC", "clg", "clb"), ("S1",), bias=clb[:, cc:cc + 1],
                scale=clg[:, cc:cc + 1])
        branch_out(cx, l, 2, "w_c_out", 2, 6, False)

        wd, wdk = wload(wview(w_in, 0, 8, 2816, 512), (8, 512))
        for cc in range(2):
            ps, pk = bank()
            for c in range(NCH):
                mm(ps[:], wd[:, c, cc * 128:(cc + 1) * 128], H[:, c, :], c == 0, c == NCH - 1, (wdk, "H"), (pk,))
            gelu_from_psum(ps[:], FD[:, cc, :], FC[:, cc, :], "FC", pk, "FD")
        psd = [bank(), bank()]
        for s in range(4):
            tok = slice(s * 128, (s + 1) * 128)
            ps, pk = bank()
            for c in range(NCH):
                mm(ps[:, 0:256], H[:, c, tok], wd[:, c, 256:512], c == 0, c == NCH - 1, ("H", wdk), (pk,))
            gelu_from_psum(ps[:, 0:256], T1[:, 0:256], T1[:, 256:512], "T1", pk, "T1")
            T.op("dve", lambda e: e.bn_stats(ST[:, 16:22], T1[:, 0:256]), r=("T1",), w=("ST",))
            T.op("dve", lambda e: e.bn_aggr(ST[:, 22:24], ST[:, 16:22]), r=("ST",), w=("ST",))
            rstd(ST[:, 23:24], ST[:, 23:24], 1.0, ("ST",), ("ST",))
            tsc("dve", T1[:, 0:256], T1[:, 0:256], ST[:, 22:23], ST[:, 23:24], ALU.subtract, ALU.mult, ("T1", "ST"), ("T1",))
            tt("dve", T1[:, 0:256], T1[:, 0:256], dlg_b[:], ALU.mult, ("T1", "dlg_b"), ("T1",))
            tt("dve", TB[:, 0:256], T1[:, 0:256], dlb_b[:], ALU.add, ("T1", "dlb_b"), ("TB",))
            for g in range(4):
                cc, gg = g // 2, g % 2
                pd, pdk = psd[cc]
                mm(pd[gg * 64:(gg + 1) * 64, tok], TB[:, g * 64:(g + 1) * 64], WT[:, g, :], True, True, ("TB", "WT"), (pdk,))
        for cc in range(2):
            pd, pdk = psd[cc]
            tt("dve", FC[:, cc, :].rearrange("p (s t) -> p s t", s=4), pd[:].rearrange("p (s t) -> p s t", s=4),
               bsT[:, cc, None, :].to_broadcast([128, 4, 128]), ALU.add, (pdk, "bsT"), ("FC",))
            tt("dve", S1[:, cc, :], FC[:, cc, :], FD[:, cc, :], ALU.mult, ("FC", "FD"), ("S1",))
        branch_out(cx, l, 3, "w_d_out", 2, 0, False)
        merge_proj(cx, l)

        rmsnorm_fm(cx, g_x, "g_x")
        wxq, wxqk = wload(wview(W["w_xq"][l], 0, 8, 0, 512), (8, 512))
        for h in range(4):
            psq_, pkq_ = bank()
            for c in range(NCH):
                mm(psq_[:], wxq[:, c, h * 128:(h + 1) * 128], H[:, c, :], c == 0, c == NCH - 1, (wxqk, "H"), (pkq_,))
            act(S1[:, 4, :], psq_[:], AF.Square, (pkq_,), ("S1",))
            psm_, pkm_ = bank()
            mm(psm_[:], ones_h[:], S1[:, 4, :], True, True, ("ones_h", "S1"), (pkm_,))
            rstd(RS[:], psm_[:], 1.0, (pkm_,), ("RS",))
            stt("dve", qT[:, h, :], psq_[:], gxq[:, 0:1], RS[:], ALU.mult, ALU.mult, (pkq_, "gxq", "RS"), ("qT",))
            ptb, ptk = PT[0], ("PT", 0)
            p2 = ptb[:].rearrange("p a b -> p (a b)")
            for kc in range(2):
                pss, pks = bank()
                mm(pss[:], mkT[:, h, kc * 128:(kc + 1) * 128], qT[:, h, :], True, True, ("mkT", "qT"), (pks,))
                act(p2[:, kc * 512:(kc + 1) * 512], pss[:], AF.Exp, (pks,), (ptk,), scale=128.0 ** -0.5)
            po, pok = bank()
            pl, plk = bank()
            for kc in range(2):
                mm(po[:], mvb[:, kc, h * 128:(h + 1) * 128], p2[:, kc * 512:(kc + 1) * 512], kc == 0, kc == 1, ("mvb", ptk), (pok,))
            for kc in range(2):
                mm(pl[:], ones_1[:], p2[:, kc * 512:(kc + 1) * 512], kc == 0, kc == 1, ("ones_1", ptk), (plk,))
            recip(RS[:], pl[:], (plk,), ("RS",))
            tt("dve", S1[:, h, :], po[:], RS[:], ALU.mult, (pok, "RS"), ("S1",))
        xo_proj(cx, l)
        ffn(cx, l, MG.rearrange("p a b -> p (a b)").bitcast(BF16), lambda j: ("MG", j // 2))
        store_x_tile(l, i)

    def tps_to_fm(src_bf, srck, nchunk, dst_base):
        ps, pk = bank()
        psb = ps[:].bitcast(BF16)
        for j in range(nchunk):
            tp(psb[:, j * NS:(j + 1) * NS], src_bf[0:NS, j * 128:(j + 1) * 128], identb[0:NS, 0:NS], (srck, "identb"), (pk,))
        for j in range(nchunk):
            cp("act", S1s[:, dst_base + j, 0:NS], psb[:, j * NS:(j + 1) * NS], (pk,), ("S1s",))

    def tm_linear(wt, wtk, ncol):
        ps, pk = bank()
        for c in range(NCH):
            mm(ps[0:NS, 0:ncol], Hs[:, c, 0:NS], wt[:, c, 0:ncol], c == 0, c == NCH - 1, ("Hs", wtk), (pk,))
        return ps, pk

    def sample_layer(l):
        cx = CS
        w_in = W["w_in"][l]
        QK = PT[0][:].rearrange("p a b -> p (a b)").bitcast(F32)
        VX = PT[1][:].rearrange("p a b -> p (a b)").bitcast(F32)
        q_s, k_s, v_s = QK[0:NS, 0:512], QK[0:NS, 512:1024], VX[0:NS, 0:512]
        kQ, kV = ("PT", 0), ("PT", 1)
        if l == 0:
            dma("sp", T1[0:NS, :], x_sample[:, 0:512], (), ("T1",))
            dma("sp", T2[0:NS, :], x_sample[:, 512:1024], (), ("T2",))
            ps, pk = bank()
            for half, src, sk in ((0, T1, "T1"), (1, T2, "T2")):
                for j in range(4):
                    c = half * 4 + j
                    tp(ps[:, c * NS:(c + 1) * NS], src[0:NS, j * 128:(j + 1) * 128], identf[0:NS, 0:NS], (sk, "identf"), (pk,))
            for c in range(NCH):
                cp("act", xsT[:, c, :], ps[:, c * NS:(c + 1) * NS], (pk,), ("xsT",))
        rmsnorm_fm(cx, g_mix, "g_mix")

        wq, wqk = wload(wview(w_in, 0, 8, 0, 512), (8, 512))
        wk_, wkk = wload(wview(w_in, 0, 8, 512, 512), (8, 512))
        wv, wvk = wload(wview(w_in, 0, 8, 1024, 512), (8, 512))
        ps, pk = tm_linear(wq, wqk, 512)
        cp("act", q_s, ps[0:NS, :], (pk,), (kQ,))
        ps, pk = tm_linear(wk_, wkk, 512)
        cp("act", k_s, ps[0:NS, :], (pk,), (kQ,))
        ps, pk = tm_linear(wv, wvk, 512)
        cp("act", v_s, ps[0:NS, :], (pk,), (kV,))
        v3 = qknorm_tm(q_s, kQ, NS, 8, 64, T3[0:NS, :], "T3", 0)
        tt("dve", v3, v3, gq_b[0:NS, None, :].to_broadcast([NS, 8, 64]), ALU.mult, (kQ, "gq_b"), (kQ,))
        v3 = qknorm_tm(k_s, kQ, NS, 8, 64, T3[0:NS, :], "T3", 0)
        tt("dve", v3, v3, gk_b[0:NS, None, :].to_broadcast([NS, 8, 64]), ALU.mult, (kQ, "gk_b"), (kQ,))
        dma("sp", k_a_sample[l], k_s, (kQ,), (), isout=True)
        dma("sp", v_a_sample[l], v_s, (kV,), (), isout=True)
        dma("sp", qs_dram, q_s, (kQ,), ("qs_dram",))

        if KDBG <= 1:
            return
        PU = BPG * NPG
        KCs = [SCR[0:PU, 0:2048], SCR[0:PU, 2048:4096]]
        VCs = [SCR[0:PU, 4096:6144], SCR[0:PU, 6272:8320]]
        KKs = [MGK[0:4], MGK[4:8]]
        VKs = [("FA", "FB"), ("FC", "FD")]
        for g in range(NG):
            for bl in range(BPG):
                dma("sp", T2[bl * NPG:(bl + 1) * NPG, :], qs_dram[g * BPG + bl].partition_broadcast(NPG), ("qs_dram",), ("T2",))
            po, pok = bank()
            pl, plk = bank()
            for ch in range(NCHK):
                bi_ = ch % 2
                KC, VC, KK, VK = KCs[bi_], VCs[bi_], KKs[bi_], VKs[bi_]
                eoff = (l * NCHK + ch) * NPOOL * 2048
                T.dma_sw(lambda e, eoff=eoff, g=g, KC=KC: e.indirect_dma_start(
                    out=KC, out_offset=None, in_=ck,
                    in_offset=bass.IndirectOffsetOnAxis(ap=PTI[0:PU, g:g + 1], axis=0), element_offset=eoff), ("PTI",), KK, ("KC", bi_))
                T.dma_sw(lambda e, eoff=eoff, g=g, VC=VC: e.indirect_dma_start(
                    out=VC, out_offset=None, in_=cv,
                    in_offset=bass.IndirectOffsetOnAxis(ap=PTI[0:PU, g:g + 1], axis=0), element_offset=eoff), ("PTI",), VK, ("VC", bi_))
                K3 = KC.rearrange("p (r d) -> p r d", r=4)
                tt("dve", K3, K3, T2[0:PU, None, :].to_broadcast([PU, 4, 512]), ALU.mult, KK + ("T2",), KK)
                red(Ssc[0:PU].rearrange("p r c -> p (r c)"), KC.rearrange("p (a d) -> p a d", d=64), KK, ("Ssc",))
                sv4 = Ssc[0:PU].rearrange("p r (h c) -> p r h c", c=2)
                tt("dve", sv4, sv4,
                   sbias[0:PU, ch * 16:(ch + 1) * 16].rearrange("p (r h) -> p r h", h=4)[:, :, :, None].to_broadcast([PU, 4, 4, 2]),
                   ALU.add, ("Ssc", "sbias"), ("Ssc",))
                for bl in range(BPG):
                    rows = slice(bl * NPG, (bl + 1) * NPG)
                    act(Pz[rows, :, bl * 8:(bl + 1) * 8], Ssc[rows, :, :], AF.Exp, ("Ssc",), ("Pz",))
                V3 = VC.rearrange("p (r d) -> p r d", r=4)
                for r4 in range(4):
                    mm(po[0:NCOL, :], Pz[0:PU, r4, :], V3[:, r4, :], ch == 0 and r4 == 0, ch == NCHK - 1 and r4 == 3,
                       ("Pz",) + VK, (pok,))
                red(Pzs[0:PU, :], Pz[0:PU].rearrange("p r c -> p c r"), ("Pz",), ("Pzs",))
                mm(pl[0:NCOL, 0:2], Pzs[0:PU, :], ones_f[0:PU, 0:2], ch == 0, ch == NCHK - 1, ("Pzs", "ones_f"), (plk,))
            cp("act", T3[0:NCOL, :], po[0:NCOL, :], (pok,), ("T3",))
            cp("act", ST[0:NCOL, 30:31], pl[0:NCOL, 0:1], (plk,), ("ST",))
            dma("sp", pa_dram[g * NCOL:(g + 1) * NCOL, 0:512], T3[0:NCOL, :], ("T3",), ("pa_dram",))
            dma("sp", pa_dram[g * NCOL:(g + 1) * NCOL, 512:513], ST[0:NCOL, 30:31], ("ST",), ("pa_dram",), slow=True)

        if KDBG <= 2:
            return
        wb1, wb1k = wload(wview(w_in, 0, 8, 1536, 512), (8, 512))
        wb2, wb2k = wload(wview(w_in, 0, 8, 2048, 256), (8, 256))
        ps, pk = tm_linear(wb1, wb1k, 512)
        cp("act", T1[0:NS, :], ps[0:NS, :], (pk,), ("T1",))
        ps, pk = tm_linear(wb2, wb2k, 256)
        tt("dve", T2[0:NS, 0:256], ps[0:NS, 0:256], T1[0:NS, 0:256], ALU.mult, (pk, "T1"), ("T2",))
        dma("sp", T3[0:NS, :], state_b[l], (), ("T3",))
        WB = SCR[0:NS, 0:768]
        BB = SCR[0:NS, 768:1024]
        Y = SCR[0:NS, 1024:1280]
        TM_ = SCR[0:NS, 1280:1536]
        dma("sp", WB, W["b_conv_w"][l].rearrange("k c -> (k c)").partition_broadcast(NS), (), MGK)
        dma("sp", BB, W["b_conv_b"][l].partition_broadcast(NS), (), MGK)
        tt("dve", Y, T2[0:NS, 0:256], WB[:, 512:768], ALU.mult, ("T2",) + MGK, MGK)
        tt("dve", Y, Y, BB, ALU.add, MGK, MGK)
        tt("dve", TM_, T3[0:NS, 256:512], WB[:, 256:512], ALU.mult, ("T3",) + MGK, MGK)
        tt("dve", Y, Y, TM_, ALU.add, MGK, MGK)
        tt("dve", TM_, T3[0:NS, 0:256], WB[:, 0:256], ALU.mult, ("T3",) + MGK, MGK)
        tt("dve", Y, Y, TM_, ALU.add, MGK, MGK)
        tt("dve", TB[0:NS, 0:256], Y, T1[0:NS, 256:512], ALU.mult, ("T1",) + MGK, ("TB",))
        dma("sp", conv_b_sample[l][:, 0:256], T3[0:NS, 256:512], ("T3",), (), isout=True)
        dma("sp", conv_b_sample[l][:, 256:512], T2[0:NS, 0:256], ("T2",), (), isout=True)
        tps_to_fm(TB, "TB", 2, 4)
        branch_out(cx, l, 1, "w_b_out", 2, 4, True)

        if KDBG <= 3:
            return
        wc, wck = wload(wview(w_in, 0, 8, 2304, 512), (8, 512))
        ps, pk = tm_linear(wc, wck, 512)
        act(T1[0:NS, 256:512], ps[0:NS, 256:512], AF.Sigmoid, (pk,), ("T1",))
        UC = T2[0:NS, 256:512]
        tt("dve", UC, ps[0:NS, 0:256], T1[0:NS, 256:512], ALU.mult, (pk, "T1"), ("T2",))
        W30 = SCR[0:NS, 1536:1792]
        CB = SCR[0:NS, 1792:2048]
        YC = SCR[0:NS, 2048:2304]
        TC = SCR[0:NS, 2304:2560]
        dma("sp", W30, W["c_conv_w"][l][30].partition_broadcast(NS), (), MGK)
        dma("sp", CB, W["c_conv_b"][l].partition_broadcast(NS), (), MGK)
        tt("dve", YC, UC, W30, ALU.mult, ("T2",) + MGK, MGK)
        tt("dve", YC, YC, CB, ALU.add, MGK, MGK)
        WCK = SCR[0:NS, 4096:4096 + 1536]
        PCK = SCR[0:NS, 6144:6144 + 1536]
        for k5 in range(5):
            dma("sp", WCK, W["c_conv_w"][l][k5 * 6:(k5 + 1) * 6].rearrange("k c -> (k c)").partition_broadcast(NS), (), FK)
            dma("sp", PCK, state_c[l][:, k5 * 1536:(k5 + 1) * 1536], (), FK)
            tt("dve", PCK, PCK, WCK, ALU.mult, FK, FK)
            red(TC, PCK.rearrange("p (k c) -> p c k", k=6), FK, MGK)
            tt("dve", YC, YC, TC, ALU.add, MGK, MGK)
        dma("sp", conv_c_sample[l][:, 0:29 * 256], state_c[l][:, 256:30 * 256], (), (), isout=True)
        dma("sp", conv_c_sample[l][:, 29 * 256:30 * 256], UC, ("T2",), (), isout=True)
        LG = SCR[0:NS, 2560:2816]
        LB = SCR[0:NS, 2816:3072]
        dma("sp", LG, W["c_ln_g"][l].partition_broadcast(NS), (), MGK)
        dma("sp", LB, W["c_ln_b"][l].partition_broadcast(NS), (), MGK)
        T.op("dve", lambda e: e.bn_stats(ST[0:NS, 16:22], YC), r=MGK, w=("ST",))
        T.op("dve", lambda e: e.bn_aggr(ST[0:NS, 22:24], ST[0:NS, 16:22]), r=("ST",), w=("ST",))
        rstd(ST[0:NS, 23:24], ST[0:NS, 23:24], 1.0, ("ST",), ("ST",))
        tsc("dve", YC, YC, ST[0:NS, 22:23], ST[0:NS, 23:24], ALU.subtract, ALU.mult, MGK + ("ST",), MGK)
        tt("dve", YC, YC, LG, ALU.mult, MGK, MGK)
        tt("dve", YC, YC, LB, ALU.add, MGK, MGK)
        act(TB[0:NS, 0:256], YC, AF.Silu, MGK, ("TB",))
        tps_to_fm(TB, "TB", 2, 6)
        branch_out(cx, l, 2, "w_c_out", 2, 6, False)

        if KDBG <= 4:
            return
        wd, wdk = wload(wview(w_in, 0, 8, 2816, 512), (8, 512))
        ps, pk = tm_linear(wd, wdk, 512)
        gelu_from_psum(ps[0:NS, :], T1[0:NS, :], T3[0:NS, :], "T3", pk, "T1")
        T.op("dve", lambda e: e.bn_stats(ST[0:NS, 16:22], T1[0:NS, 256:512]), r=("T1",), w=("ST",))
        T.op("dve", lambda e: e.bn_aggr(ST[0:NS, 22:24], ST[0:NS, 16:22]), r=("ST",), w=("ST",))
        rstd(ST[0:NS, 23:24], ST[0:NS, 23:24], 1.0, ("ST",), ("ST",))
        DV = T3[0:NS, 0:256]
        DG = T3[0:NS, 256:512]
        tsc("dve", DV, T1[0:NS, 256:512], ST[0:NS, 22:23], ST[0:NS, 23:24], ALU.subtract, ALU.mult, ("T1", "ST"), ("T3",))
        tt("dve", DV, DV, dlg_b[0:NS, :], ALU.mult, ("T3", "dlg_b"), ("T3",))
        tt("dve", DV, DV, dlb_b[0:NS, :], ALU.add, ("T3", "dlb_b"), ("T3",))
        dma("sp", d_v_sample[l], DV, ("T3",), (), isout=True)
        DV3 = DV.rearrange("p (g c) -> p g c", g=4)
        DG3 = DG.rearrange("p (g c) -> p g c", g=4)
        tt("dve", DG3, DV3, w00[0:NS, 0:4, None].to_broadcast([NS, 4, 64]), ALU.mult, ("T3", "w00"), ("T3",))
        tt("dve", DG3, DG3, w00[0:NS, 4:8, None].to_broadcast([NS, 4, 64]), ALU.add, ("T3", "w00"), ("T3",))
        tt("dve", TB[0:NS, 0:256], DG, T1[0:NS, 0:256], ALU.mult, ("T3", "T1"), ("TB",))
        tps_to_fm(TB, "TB", 2, 0)
        branch_out(cx, l, 3, "w_d_out", 2, 0, False)

        if KDBG <= 5:
            return
        ARd = qT[:].rearrange("p a b -> p (a b)").bitcast(F32)[0:NS, :].rearrange("p (a d) -> p a d", a=8)
        aro = pa_dram.rearrange("(b a) w -> b a w", a=8)
        for hc in range(8):
            h = hc // 2
            dma("sp", ARd[:, hc, :], aro[:, hc, h * 128:(h + 1) * 128], ("pa_dram",), ("qT",))
        dma("sp", ST[0:NS, 32:40], aro[:, :, 512], ("pa_dram",), ("ST",), slow=True)
        tt("dve", T2[0:NS, :], q_s, k_s, ALU.mult, (kQ,), ("T2",))
        red(ST[0:NS, 40:48], T2[0:NS, :].rearrange("p (a d) -> p a d", a=8), ("T2",), ("ST",))
        act(ST[0:NS, 40:48], ST[0:NS, 40:48], AF.Exp, ("ST",), ("ST",))
        tt("dve", ST[0:NS, 32:40], ST[0:NS, 32:40], ST[0:NS, 40:48], ALU.add, ("ST",), ("ST",))
        for hc in range(8):
            h = hc // 2
            stt("dve", ARd[:, hc, :], v_s[:, h * 128:(h + 1) * 128], ST[0:NS, 40 + hc:41 + hc], ARd[:, hc, :], ALU.mult, ALU.add,
                (kV, "ST", "qT"), ("qT",))
        recip(ST[0:NS, 32:40], ST[0:NS, 32:40], ("ST",), ("ST",))
        for h in range(4):
            tt("dve", ST[0:NS, 33 + 2 * h:34 + 2 * h], ST[0:NS, 33 + 2 * h:34 + 2 * h], lam_s[0:NS, 4:5], ALU.mult, ("ST", "lam_s"), ("ST",))
            oh = T2[0:NS, h * 128:(h + 1) * 128]
            tsc("dve", oh, ARd[:, 2 * h, :], ST[0:NS, 32 + 2 * h:33 + 2 * h], None, ALU.mult, None, ("qT", "ST"), ("T2",))
            stt("dve", oh, ARd[:, 2 * h + 1, :], ST[0:NS, 33 + 2 * h:34 + 2 * h], oh, ALU.mult, ALU.add, ("qT", "ST", "T2"), ("T2",))
        if KDBG <= 5.2:
            return
        v3 = qknorm_tm(T2[0:NS, :], "T2", NS, 4, 128, T3[0:NS, :], "T3", 48)
        tt("dve", TB[0:NS, :].rearrange("p (h d) -> p h d", h=4), v3, gsub_b[0:NS, None, :].to_broadcast([NS, 4, 128]),
           ALU.mult, ("T2", "gsub_b"), ("TB",))
        if KDBG <= 5.4:
            return
        tps_to_fm(TB, "TB", 4, 0)
        if KDBG <= 5.5:
            return
        branch_out(cx, l, 0, "w_a_out", 4, 0, False)
        if KDBG <= 5.7:
            return
        merge_proj(cx, l)

        if KDBG <= 6:
            return
        rmsnorm_fm(cx, g_x, "g_x")
        wxq, wxqk = wload(wview(W["w_xq"][l], 0, 8, 0, 512), (8, 512))
        ps, pk = tm_linear(wxq, wxqk, 512)
        cp("act", T1[0:NS, :], ps[0:NS, :], (pk,), ("T1",))
        v3 = qknorm_tm(T1[0:NS, :], "T1", NS, 4, 128, T3[0:NS, :], "T3", 0)
        GX = SCR[0:NS, 3072:3200]
        dma("sp", GX, W["x_qnorm_g"][l].partition_broadcast(NS), (), MGK)
        tt("dve", v3, v3, GX[:, None, :].to_broadcast([NS, 4, 128]), ALU.mult, ("T1",) + MGK, ("T1",))
        dma("sp", qx_dram, T1[0:NS, :], ("T1",), ("qx_dram",))
        MK = SCR[:, 4096:5120]
        MV = SCR[:, 5120:6144]
        for b in range(NS):
            dma("sp", T2[:], qx_dram[b].partition_broadcast(128), ("qx_dram",), ("T2",))
            dma("sp", MK.rearrange("p (k d) -> p k d", k=2), cmk[l, b].rearrange("(k p) d -> p k d", p=128), (), ("FA",))
            dma("sp", MV.rearrange("p (k d) -> p k d", k=2), cmv[l, b].rearrange("(k p) d -> p k d", p=128), (), ("FB",))
            M3 = MK.rearrange("p (k d) -> p k d", k=2)
            tt("dve", M3, M3, T2[:, None, :].to_broadcast([128, 2, 512]), ALU.mult, ("FA", "T2"), ("FA",))
            red(Pc[:, b, :], MK.rearrange("p (a d) -> p a d", d=128), ("FA",), ("Pc",))
            act(Pc[:, b, :], Pc[:, b, :], AF.Exp, ("Pc",), ("Pc",), scale=128.0 ** -0.5)
            po, pok = bank()
            for kc in range(2):
                mm(po[0:4, :], Pc[:, b, kc * 4:(kc + 1) * 4], MV[:, kc * 512:(kc + 1) * 512], kc == 0, kc == 1, ("Pc", "FB"), (pok,))
            cp("act", T3[0:4, :], po[0:4, :], (pok,), ("T3",))
            dma("sp", om_dram[b], T3[0:4, :], ("T3",), ("om_dram",))
        pl, plk = bank()
        mm(pl[0:1, 0:NS * 8], ones_f[:, 0:1], Pc[:].rearrange("p a b -> p (a b)"), True, True, ("ones_f", "Pc"), (plk,))
        cp("act", T3[0:1, 0:NS * 8], pl[0:1, 0:NS * 8], (plk,), ("T3",))
        dma("sp", ol_dram, T3[0:1, 0:NS * 8], ("T3",), ("ol_dram",))
        for h in range(4):
            dma("sp", T1[0:NS, h * 128:(h + 1) * 128], om_dram[:, h, h * 128:(h + 1) * 128], ("om_dram",), ("T1",))
        dma("sp", ST[0:NS, 48:56], ol_dram.rearrange("o (b a) -> (o b) a", a=8), ("ol_dram",), ("ST",))
        tt("dve", ST[0:NS, 48:52], ST[0:NS, 48:52], ST[0:NS, 52:56], ALU.add, ("ST",), ("ST",))
        recip(ST[0:NS, 48:52], ST[0:NS, 48:52], ("ST",), ("ST",))
        tt("dve", TB[0:NS, :].rearrange("p (h d) -> p h d", h=4), T1[0:NS, :].rearrange("p (h d) -> p h d", h=4),
           ST[0:NS, 48:52, None].to_broadcast([NS, 4, 128]), ALU.mult, ("T1", "ST"), ("TB",))
        tps_to_fm(TB, "TB", 4, 0)
        xo_proj(cx, l)
        ffn(cx, l, S1[:, 0:2, :].rearrange("p a b -> p (a b)"), lambda j: "S1")
        if l == L - 1:
            for half, dst, dk in ((0, T1, "T1"), (1, T2, "T2")):
                ps, pk = bank()
                for j in range(4):
                    tp(ps[0:NS, j * 128:(j + 1) * 128], xsT[:, half * 4 + j, 0:NS], identf[:], ("xsT", "identf"), (pk,))
                cp("act", dst[0:NS, :], ps[0:NS, :], (pk,), (dk,))
                dma("sp", y_sample[:, half * 512:(half + 1) * 512], dst[0:NS, :], (dk,), (), isout=True)

    for l in range(L):
        T.new_epoch()
        load_layer_params(l)
        mem_layer(l)
        if with_sample:
            sample_layer(l)
        for i in range(NT):
            T.new_epoch()
            tile_layer(l, i)
    T.finish()
    with nc.Block() as block:
        T.emit(block)
    stack.close()
    return nc


_CACHE = {}


def _consts():
    ident = np.eye(128, dtype=np.float32)
    kk = np.arange(128)[:, None]
    qq = np.arange(128)[None, :]
    mask = np.where(kk > qq, -1e30, 0.0).astype(np.float32)
    tril = (qq <= kk).astype(np.float32)
    slopes = np.array([2.0 ** (-8.0 * (i + 1) / 4) for i in range(4)], dtype=np.float64)
    p = np.arange(128)[:, None, None]
    dl = np.arange(16)[None, None, :]
    alibi = (slopes[None, :, None] * (p - 128.0 * dl)).astype(np.float32).reshape(128, 64)
    return dict(c_ident=ident, c_mask=mask, c_tril=tril, c_alibi=alibi), slopes


WEIGHT_KEYS = ["norm_mix_g", "w_in", "a_qnorm_g", "a_knorm_g", "a_lam", "a_subln_g", "w_a_out", "b_conv_w",
               "b_conv_b", "w_b_out", "c_conv_w", "c_conv_b", "c_ln_g", "c_ln_b", "w_c_out", "d_ln_g", "d_ln_b",
               "d_ws", "d_bs", "w_d_out", "w_o", "norm_x_g", "mem_norm_g", "w_xq", "w_xk", "w_xv", "x_qnorm_g",
               "x_knorm_g", "w_xo", "norm_ffn_g", "w_up", "w_down"]


def kernel(**inp):
    x_prompt = np.asarray(inp["x_prompt"])
    B, SEQ, _ = x_prompt.shape
    L = np.asarray(inp["w_in"]).shape[0]
    page_table = np.ascontiguousarray(np.asarray(inp["page_table"]).astype(np.int32))
    NSG, NPG = page_table.shape
    NSL = NSG // NCORES
    cache_k = np.asarray(inp["cache_k_a"])
    cache_v = np.asarray(inp["cache_v_a"])
    NPOOL, PAGE_TOK = cache_k.shape[1], cache_k.shape[2]
    NCHK = PAGE_TOK // 4
    past_len = NPG * PAGE_TOK
    key = (SEQ, NPG, NPOOL, L, NSL, PAGE_TOK)
    if key not in _CACHE:
        _CACHE[key] = build(SEQ, NPG, NPOOL, L, NSL, PAGE_TOK)
    nc = _CACHE[key]
    consts, slopes = _consts()
    shared = {k: np.asarray(inp[k]) for k in WEIGHT_KEYS}
    shared["ck"] = np.ascontiguousarray(
        cache_k.reshape(L, NPOOL, NCHK, 2048).transpose(0, 2, 1, 3)).reshape(L * NCHK * NPOOL, 2048)
    shared["cv"] = np.ascontiguousarray(
        cache_v.reshape(L, NPOOL, NCHK, 2048).transpose(0, 2, 1, 3)).reshape(L * NCHK * NPOOL, 2048)
    BPG = max(1, min(NSL, 128 // NPG))
    j = (np.arange(128) % NPG)[:, None, None]
    t = np.arange(PAGE_TOK)[None, :, None]
    dist = past_len - (PAGE_TOK * j + t)
    shared["c_sbias"] = (-slopes[None, None, :] * dist).astype(np.float32).reshape(128, PAGE_TOK * 4)
    x_sample = np.asarray(inp["x_sample"]).reshape(NSG, D)
    state_b = np.asarray(inp["state_conv_b"]).reshape(L, NSG, 512)
    state_c = np.asarray(inp["state_conv_c"]).reshape(L, NSG, 30 * 256)
    cmk = np.asarray(inp["cache_mem_k"]).reshape(L, NSG, 256, 512)
    cmv = np.asarray(inp["cache_mem_v"]).reshape(L, NSG, 256, 512)
    mem_prompt = np.asarray(inp["mem_prompt"])
    in_maps = []
    for c in range(NCORES):
        m = dict(consts)
        m.update(shared)
        sl = slice(c * NSL, (c + 1) * NSL)
        m["x_prompt"] = np.ascontiguousarray(x_prompt[c])
        m["mem_prompt"] = np.ascontiguousarray(mem_prompt[c])
        m["x_sample"] = np.ascontiguousarray(x_sample[sl])
        m["state_conv_b"] = np.ascontiguousarray(state_b[:, sl])
        m["state_conv_c"] = np.ascontiguousarray(state_c[:, sl])
        m["cache_mem_k"] = np.ascontiguousarray(cmk[:, sl])
        m["cache_mem_v"] = np.ascontiguousarray(cmv[:, sl])
        m["page_table"] = np.ascontiguousarray(page_table[sl])
        in_maps.append(m)
    if inp.get("_prepare_only"):
        return nc, in_maps
    res = run_bass_kernel_spmd(nc, in_maps, core_ids=list(range(NCORES))).results
    y_prompt = np.stack([r["y_prompt"] for r in res], 0)
    kp = np.stack([r["k_a_prompt"] for r in res], 1).reshape(L, B, SEQ, 4, 128)
    vp = np.stack([r["v_a_prompt"] for r in res], 1).reshape(L, B, SEQ, 4, 128)
    cbp = np.stack([r["conv_b_prompt"] for r in res], 1)
    ccp = np.stack([r["conv_c_prompt"] for r in res], 1)
    mkp = np.stack([r["mem_k_prompt"] for r in res], 1).reshape(L, B, 256, 4, 128)
    mvp = np.stack([r["mem_v_prompt"] for r in res], 1).reshape(L, B, 256, 4, 128)
    ys = np.concatenate([r["y_sample"] for r in res], 0).reshape(NSG, 1, D)
    ks = np.concatenate([r["k_a_sample"] for r in res], 1).reshape(L, NSG, 1, 4, 128)
    vs = np.concatenate([r["v_a_sample"] for r in res], 1).reshape(L, NSG, 1, 4, 128)
    cbs = np.concatenate([r["conv_b_sample"] for r in res], 1).reshape(L, NSG, 2, 256)
    ccs = np.concatenate([r["conv_c_sample"] for r in res], 1).reshape(L, NSG, 30, 256)
    dvs = np.concatenate([r["d_v_sample"] for r in res], 1).reshape(L, NSG, 1, 256)
    return (y_prompt, ys, kp, vp, cbp, ccp, mkp, mvp, ks, vs, cbs, ccs, dvs)
```
